# Optimizing a Trainium2 kernel written in Bass

```python
import math
import jax, jax.numpy as jnp
from jax import lax
import numpy as np

D_MODEL = 1024
BATCH = 8
SEQ = 4096
DEPTH = 2

ATTN_HEADS = 8
HEAD_DIM = 64
ATTN_W = ATTN_HEADS * 2 * HEAD_DIM
Q_BLOCK = 128
LRU_W = D_MODEL
LRU_BLOCKS = 8
LRU_BW = LRU_W // LRU_BLOCKS
LRU_C = 8.0
CONV_W = 4
N_BRANCH = 2
IN_COLS = 3 * ATTN_W + 2 * LRU_W + N_BRANCH * D_MODEL
PEER_HEADS = 8
N_KEYS = 128
N_EXPERTS = N_KEYS * N_KEYS
D_KEY = 256
D_KEY_HALF = D_KEY // 2
PEER_TOPK = 16
PEER_CHUNK = 128
DEEPNORM_ALPHA = (2.0 * DEPTH) ** 0.25
DEEPNORM_BETA = (8.0 * DEPTH) ** -0.25
LN_EPS = 1e-5
RMS_EPS = 1e-6

kernel_name = "hybrid_diffattn_rglru_peer_deepnorm"


def layer_norm(x, g, b):
    xf = x.astype(jnp.float32)
    mu = jnp.mean(xf, axis=-1, keepdims=True)
    var = jnp.mean(jnp.square(xf - mu), axis=-1, keepdims=True)
    y = (xf - mu) * lax.rsqrt(var + LN_EPS) * g.astype(jnp.float32) + b.astype(jnp.float32)
    return y.astype(x.dtype)


def alibi_slopes(n_heads):
    return jnp.exp2(-8.0 * jnp.arange(1, n_heads + 1, dtype=jnp.float32) / n_heads)


def diff_attention(q, k, v, lambda_qk, subln_g, lam_init):
    B, S = q.shape[0], q.shape[1]
    n_blk = S // Q_BLOCK
    scale = HEAD_DIM ** -0.5
    lq = lambda_qk.astype(jnp.float32)
    lam = jnp.exp(jnp.sum(lq[0] * lq[1])) - jnp.exp(jnp.sum(lq[2] * lq[3])) + lam_init
    slopes = alibi_slopes(ATTN_HEADS)
    k1, k2 = k[:, :, :, 0], k[:, :, :, 1]
    vf = v.astype(jnp.float32)
    pos = jnp.arange(S, dtype=jnp.int32)

    def to_blocks(t):
        return jnp.moveaxis(t.reshape(B, n_blk, Q_BLOCK, ATTN_HEADS, HEAD_DIM), 1, 0)

    def block(args):
        q1b, q2b, qpos = args
        rel = (qpos[:, None] - pos[None, :]).astype(jnp.float32)
        causal = pos[None, :] <= qpos[:, None]
        bias = -slopes[:, None, None] * rel

        def probs(qb, kk):
            s = jnp.einsum('bqhd,bkhd->bhqk', qb, kk).astype(jnp.float32) * scale + bias
            s = jnp.where(causal, s, -jnp.inf)
            return jax.nn.softmax(s, axis=-1)

        attn = probs(q1b, k1) - lam * probs(q2b, k2)
        return jnp.einsum('bhqk,bkhe->bqhe', attn, vf)

    o = lax.map(block, (to_blocks(q[:, :, :, 0]), to_blocks(q[:, :, :, 1]),
                        pos.reshape(n_blk, Q_BLOCK)))
    o = jnp.moveaxis(o, 0, 1).reshape(B, S, ATTN_HEADS, 2 * HEAD_DIM)
    o = o * lax.rsqrt(jnp.mean(jnp.square(o), axis=-1, keepdims=True) + RMS_EPS)
    o = o * subln_g.astype(jnp.float32) * (1.0 - lam_init)
    return o.reshape(B, S, ATTN_W).astype(v.dtype)


def causal_conv(xb, w, b):
    S = xb.shape[1]
    xp = jnp.pad(xb, ((0, 0), (CONV_W - 1, 0), (0, 0)))
    y = b + xp[:, 0:S] * w[0]
    for tap in range(1, CONV_W):
        y = y + xp[:, tap:tap + S] * w[tap]
    return y


def rg_lru(xb, gate_a_w, gate_a_b, gate_x_w, gate_x_b, lru_lambda):
    B, S, W = xb.shape
    xg = xb.reshape(B, S, LRU_BLOCKS, LRU_BW)
    r = jax.nn.sigmoid(jnp.einsum('bsgi,gij->bsgj', xg, gate_a_w).reshape(B, S, W) + gate_a_b)
    i = jax.nn.sigmoid(jnp.einsum('bsgi,gij->bsgj', xg, gate_x_w).reshape(B, S, W) + gate_x_b)
    log_a = -LRU_C * r.astype(jnp.float32) * jax.nn.softplus(-lru_lambda.astype(jnp.float32))
    a = jnp.exp(log_a)
    first = (jnp.arange(S) == 0)[None, :, None]
    mult = jnp.where(first, 1.0, jnp.sqrt(-jnp.expm1(2.0 * log_a)))
    u = mult * (i * xb).astype(jnp.float32)

    def combine(c1, c2):
        a1, b1 = c1
        a2, b2 = c2
        return a1 * a2, a2 * b1 + b2

    _, h = lax.associative_scan(combine, (a, u), axis=1)
    return h.astype(xb.dtype)


def token_mixer(x, w_in, lambda_qk, subln_g, conv_w, conv_b, gate_a_w, gate_a_b,
                gate_x_w, gate_x_b, lru_lambda, w_br_attn, w_br_lru, w_out, lam_init):
    B, S, _ = x.shape
    m = x @ w_in
    cuts = [ATTN_W, 2 * ATTN_W, 3 * ATTN_W, 3 * ATTN_W + LRU_W, 3 * ATTN_W + 2 * LRU_W]
    q, k, v, xr, gr, gates = jnp.split(m, cuts, axis=-1)
    q = q.reshape(B, S, ATTN_HEADS, 2, HEAD_DIM)
    k = k.reshape(B, S, ATTN_HEADS, 2, HEAD_DIM)
    v = v.reshape(B, S, ATTN_HEADS, 2 * HEAD_DIM)
    ya = diff_attention(q, k, v, lambda_qk, subln_g, lam_init)
    h = rg_lru(causal_conv(xr, conv_w, conv_b), gate_a_w, gate_a_b, gate_x_w, gate_x_b, lru_lambda)
    yr = h * jax.nn.gelu(gr)
    g = jax.nn.sigmoid(gates.reshape(B, S, N_BRANCH, D_MODEL))
    merged = g[:, :, 0] * (ya @ w_br_attn) + g[:, :, 1] * (yr @ w_br_lru)
    return merged @ w_out


def peer(x, peer_wq, peer_subkeys, peer_u, peer_v):
    B, S, D = x.shape
    T = B * S
    xt = x.reshape(T, D)
    q = (xt @ peer_wq).reshape(T, PEER_HEADS, 2, D_KEY_HALF)
    sc = jnp.einsum('thcd,hcnd->thcn', q, peer_subkeys).astype(jnp.float32)
    s1, i1 = lax.top_k(sc[:, :, 0], PEER_TOPK)
    s2, i2 = lax.top_k(sc[:, :, 1], PEER_TOPK)
    cand = (s1[..., :, None] + s2[..., None, :]).reshape(T, PEER_HEADS, PEER_TOPK * PEER_TOPK)
    top_s, top_c = lax.top_k(cand, PEER_TOPK)
    e = (jnp.take_along_axis(i1, top_c // PEER_TOPK, axis=-1) * N_KEYS
         + jnp.take_along_axis(i2, top_c % PEER_TOPK, axis=-1))
    g = jax.nn.softmax(top_s, axis=-1).astype(x.dtype)
    n_chunk = T // PEER_CHUNK
    HK = PEER_HEADS * PEER_TOPK

    def chunk(args):
        xc, ec, gc = args
        uc = jnp.take(peer_u, ec, axis=0)
        vc = jnp.take(peer_v, ec, axis=0)
        act = jax.nn.gelu(jnp.einsum('cd,ced->ce', xc, uc))
        return jnp.einsum('ce,ced->cd', gc * act, vc)

    out = lax.map(chunk, (xt.reshape(n_chunk, PEER_CHUNK, D),
                          e.reshape(n_chunk, PEER_CHUNK, HK),
                          g.reshape(n_chunk, PEER_CHUNK, HK)))
    return out.reshape(B, S, D)


def setup_inputs(seed: int = 0) -> dict:
    key = jax.random.key(seed)
    ks = jax.random.split(key, 24)
    f32 = jnp.float32

    def nrm(k, shape, scale):
        return jax.random.normal(k, shape, f32) * scale

    lam_u = jax.random.uniform(ks[10], (DEPTH, LRU_W), f32, 0.9, 0.999)
    return {
        "x": nrm(ks[0], (BATCH, SEQ, D_MODEL), 1.0),
        "w_in": nrm(ks[1], (DEPTH, D_MODEL, IN_COLS), D_MODEL ** -0.5),
        "lambda_qk": nrm(ks[2], (DEPTH, 4, HEAD_DIM), 0.1),
        "subln_g": 1.0 + nrm(ks[3], (DEPTH, 2 * HEAD_DIM), 0.02),
        "conv_w": nrm(ks[4], (DEPTH, CONV_W, LRU_W), CONV_W ** -0.5),
        "conv_b": nrm(ks[5], (DEPTH, LRU_W), 0.02),
        "gate_a_w": nrm(ks[6], (DEPTH, LRU_BLOCKS, LRU_BW, LRU_BW), LRU_BW ** -0.5),
        "gate_a_b": nrm(ks[7], (DEPTH, LRU_W), 0.02),
        "gate_x_w": nrm(ks[8], (DEPTH, LRU_BLOCKS, LRU_BW, LRU_BW), LRU_BW ** -0.5),
        "gate_x_b": nrm(ks[9], (DEPTH, LRU_W), 0.02),
        "lru_lambda": jnp.log(lam_u) - jnp.log1p(-lam_u),
        "w_br_attn": nrm(ks[11], (DEPTH, ATTN_W, D_MODEL), ATTN_W ** -0.5),
        "w_br_lru": nrm(ks[12], (DEPTH, LRU_W, D_MODEL), LRU_W ** -0.5),
        "w_out": nrm(ks[13], (DEPTH, D_MODEL, D_MODEL), D_MODEL ** -0.5 * DEEPNORM_BETA),
        "ln1_g": 1.0 + nrm(ks[14], (DEPTH, D_MODEL), 0.02),
        "ln1_b": nrm(ks[15], (DEPTH, D_MODEL), 0.02),
        "peer_wq": nrm(ks[16], (DEPTH, D_MODEL, PEER_HEADS * D_KEY), D_MODEL ** -0.5),
        "peer_subkeys": nrm(ks[17], (DEPTH, PEER_HEADS, 2, N_KEYS, D_KEY_HALF), D_KEY_HALF ** -0.5),
        "peer_u": nrm(ks[18], (DEPTH, N_EXPERTS, D_MODEL), D_MODEL ** -0.5),
        "peer_v": nrm(ks[19], (DEPTH, N_EXPERTS, D_MODEL), PEER_TOPK ** -0.5 * DEEPNORM_BETA),
        "ln2_g": 1.0 + nrm(ks[20], (DEPTH, D_MODEL), 0.02),
        "ln2_b": nrm(ks[21], (DEPTH, D_MODEL), 0.02),
    }


def reference(x, w_in, lambda_qk, subln_g, conv_w, conv_b, gate_a_w, gate_a_b, gate_x_w,
              gate_x_b, lru_lambda, w_br_attn, w_br_lru, w_out, ln1_g, ln1_b, peer_wq,
              peer_subkeys, peer_u, peer_v, ln2_g, ln2_b):
    for l in range(DEPTH):
        lam_init = 0.8 - 0.6 * math.exp(-0.3 * l)
        mix = token_mixer(x, w_in[l], lambda_qk[l], subln_g[l], conv_w[l], conv_b[l],
                          gate_a_w[l], gate_a_b[l], gate_x_w[l], gate_x_b[l], lru_lambda[l],
                          w_br_attn[l], w_br_lru[l], w_out[l], lam_init)
        x = layer_norm(DEEPNORM_ALPHA * x + mix, ln1_g[l], ln1_b[l])
        ffn = peer(x, peer_wq[l], peer_subkeys[l], peer_u[l], peer_v[l])
        x = layer_norm(DEEPNORM_ALPHA * x + ffn, ln2_g[l], ln2_b[l])
    return x
```

```python
import math
import numpy as np
import concourse.bass as bass
import concourse.mybir as mybir
from concourse.bass_utils import run_bass_kernel_spmd
from contextlib import ExitStack

F32 = mybir.dt.float32
BF16 = mybir.dt.bfloat16
U32 = mybir.dt.uint32
I32 = mybir.dt.int32
AF = mybir.ActivationFunctionType
ALU = mybir.AluOpType
AX = mybir.AxisListType

D = 1024
S = 4096
L = 2
NTB = S // 128
ALPHA = (2.0 * L) ** 0.25
LN_EPS = 1e-5
RMS_EPS = 1e-6
NEXP = 16384


class Op:
    __slots__ = ("eng", "fn", "deps", "is_dma", "sem", "count", "signals", "waits", "idx", "barrier")


class Sched:
    def __init__(self, nc, stack, same_engine_sync=True):
        self.nc = nc
        self.stack = stack
        self.ops = []
        self.last_w = {}
        self.readers = {}
        self.same = same_engine_sync
        self.semh = {}

    def sem(self, key):
        if key not in self.semh:
            self.semh[key] = self.stack.enter_context(self.nc.semaphore("s_%d" % len(self.semh)))
        return self.semh[key]

    def add(self, eng, fn, reads=(), writes=(), dma=None):
        op = Op()
        op.eng = eng
        op.fn = fn
        op.barrier = False
        op.is_dma = dma is not None
        op.sem = ("dma", dma) if dma is not None else ("eng", eng)
        op.signals = op.is_dma
        op.count = 0
        op.idx = len(self.ops)
        deps = set()
        for r in reads:
            w = self.last_w.get(r)
            if w is not None:
                deps.add(w)
        for w_ in writes:
            w = self.last_w.get(w_)
            if w is not None:
                deps.add(w)
            for rd in self.readers.get(w_, ()):
                deps.add(rd)
        op.deps = deps
        for r in reads:
            self.readers.setdefault(r, []).append(op.idx)
        for w_ in writes:
            self.last_w[w_] = op.idx
            self.readers[w_] = []
        self.ops.append(op)
        return op

    def barrier(self):
        last = {}
        for op in self.ops:
            if not op.barrier and not op.is_dma:
                last[op.eng] = op
        for op in last.values():
            op.signals = True
        for eng in ("sp", "act", "dve", "pool", "pe"):
            op = Op()
            op.eng = eng
            op.fn = None
            op.barrier = True
            op.is_dma = False
            op.sem = None
            op.signals = False
            op.count = 0
            op.idx = len(self.ops)
            op.deps = set()
            self.ops.append(op)
        self.last_w = {}
        self.readers = {}

    def _skip(self, dop, op):
        return (not dop.is_dma) and dop.eng == op.eng and (not op.is_dma) and (dop.eng == "pe" or not self.same)

    def finalize(self, block):
        ops = self.ops
        for op in ops:
            for d in op.deps:
                dop = ops[d]
                if dop.is_dma or self._skip(dop, op):
                    continue
                dop.signals = True
        cnt = {}
        for op in ops:
            if op.barrier:
                op.count = dict(cnt)
                continue
            if op.signals:
                inc = 16 if op.is_dma else 1
                cnt[op.sem] = cnt.get(op.sem, 0) + inc
                op.count = cnt[op.sem]
        waited = {}
        for op in ops:
            w = waited.setdefault(op.eng, {})
            op.waits = []
            if op.barrier:
                need = {s: v for s, v in op.count.items() if s != ("eng", op.eng)}
            else:
                need = {}
                for d in op.deps:
                    dop = ops[d]
                    if self._skip(dop, op):
                        continue
                    if need.get(dop.sem, 0) < dop.count:
                        need[dop.sem] = dop.count
            for s, v in need.items():
                if w.get(s, 0) < v:
                    w[s] = v
                    op.waits.append((s, v))
        final_waits = [(s, v) for s, v in cnt.items() if s[0] == "dma"]
        for s in cnt:
            self.sem(s)
        per_eng = {}
        for op in ops:
            per_eng.setdefault(op.eng, []).append(op)

        def emit(engname, eng_obj, final=False):
            for op in per_eng.get(engname, []):
                for s, v in op.waits:
                    eng_obj.wait_ge(self.sem(s), v)
                if op.fn is None:
                    continue
                ins = op.fn(eng_obj)
                if op.signals:
                    ins.then_inc(self.sem(op.sem), 16 if op.is_dma else 1)
            if final:
                for s, v in final_waits:
                    eng_obj.wait_ge(self.sem(s), v)

        @block.sync
        def _(e):
            emit("sp", e, final=True)

        @block.scalar
        def _(e):
            emit("act", e)

        @block.vector
        def _(e):
            emit("dve", e)

        @block.gpsimd
        def _(e):
            emit("pool", e)

        @block.tensor
        def _(e):
            emit("pe", e)
        return len(ops)


class Arena:
    def __init__(self, tensor, ncols):
        self.t = tensor
        self.n = ncols
        self.off = 0

    def reset(self):
        self.off = 0

    def alloc(self, shape, dt):
        n = 1
        for s_ in shape:
            n *= s_
        if dt == BF16:
            ncol = (n + 1) // 2
        else:
            ncol = n
        ncol = (ncol + 15) // 16 * 16
        assert self.off + ncol <= self.n, ("arena overflow", self.off, ncol, self.n)
        v = self.t[:, self.off:self.off + ncol]
        self.off += ncol
        if dt != F32:
            v = v.bitcast(dt)
        v = v[:, 0:n]
        if len(shape) == 2:
            v = v.rearrange("p (a b) -> p a b", a=shape[0])
        elif len(shape) == 3:
            v = v.rearrange("p (a b c) -> p a b c", a=shape[0], b=shape[1])
        return v


def build_program(n_layers=L, debug=False, phases=("A0", "A1", "A2", "A3", "B")):
    nc = bass.Bass("TRN2", target_bir_lowering=False)

    def din(name, shape, dt=F32):
        return nc.dram_tensor(name, list(shape), dt, kind="ExternalInput").ap()

    def dscr(name, shape, dt):
        return nc.dram_tensor(name, list(shape), dt, kind="ExternalOutput" if debug else "Internal").ap()

    x_d = din("x", [S, D])
    win_d = din("w_in", [L, 56, 128, 1024])
    wba_d = din("w_br_attn", [L, 128, 8, 1024])
    wbl_d = din("w_br_lru", [L, 128, 8, 1024])
    wo_d = din("w_out", [L, 128, 8, 1024])
    wq_d = din("peer_wq", [L, 16, 128, 1024])
    sk_d = din("peer_skT", [L, 128, 16 * 128])
    pu_ds = [din("peer_u%d" % i, [NEXP, D]) for i in range(L)]
    pv_ds = [din("peer_v%d" % i, [NEXP, D]) for i in range(L)]
    iota_d = din("iota256", [256])
    gaw_d = din("gate_a_w", [L, 128, 8 * 128])
    gxw_d = din("gate_x_w", [L, 128, 8 * 128])
    chp_d = din("chp", [L, 128, 64])
    lq_d = din("lambda_qk", [L, 256])
    sg_d = din("subln_g", [L, 128])
    ln1g_d = din("ln1_g", [L, D])
    ln1b_d = din("ln1_b", [L, D])
    ln2g_d = din("ln2_g", [L, D])
    ln2b_d = din("ln2_b", [L, D])
    augk_d = din("aug_k", [3, 128])
    augq_d = din("aug_q", [3, 8 * 512])
    y_d = nc.dram_tensor("y", [S, D], F32, kind="ExternalOutput").ap()

    xT_d = dscr("xT_s", [D, S], BF16)
    yaT_d = dscr("yaT_s", [D, S], BF16)
    yrT_d = dscr("yrT_s", [D, S], BF16)
    x1_d = dscr("x1_s", [S, D], F32)
    x2_d = dscr("x2_s", [S, D], F32)

    with ExitStack() as st:
        ARN = 49000
        arena_t = st.enter_context(nc.sbuf_tensor("arena", [128, ARN], F32))
        cst_t = st.enter_context(nc.sbuf_tensor("cst", [128, 3200], F32))
        pf = [st.enter_context(nc.psum_tensor("pf%d" % i, [128, 512], F32)) for i in range(7)]
        pb = st.enter_context(nc.psum_tensor("pbb", [128, 1024], BF16))
        block = st.enter_context(nc.Block())
        SC = Sched(nc, st)
        AR = Arena(arena_t, ARN)
        CA = Arena(cst_t, 3200)

        def DMA(out, in_, reads, writes, key, q="sp"):
            SC.add(q, lambda e: e.dma_start(out=out, in_=in_), reads, writes, dma=key)

        def MM(out, lhsT, rhs, start, stop, reads, writes):
            SC.add("pe", lambda e: e.matmul(out, lhsT=lhsT, rhs=rhs, start=start, stop=stop), reads, writes)

        def TR(out, in_, reads, writes):
            SC.add("pe", lambda e: e.transpose(out=out, in_=in_, identity=ident), list(reads) + ["ident"], writes)

        def ACT(out, in_, func, reads, writes, bias=None, scale=None, accum=None):
            kw = {}
            if bias is not None:
                kw["bias"] = bias
            if scale is not None:
                kw["scale"] = scale
            if accum is not None:
                kw["accum_out"] = accum
            SC.add("act", lambda e: e.activation(out=out, in_=in_, func=func, **kw), reads, writes)

        def CP(eng, out, in_, reads, writes):
            if eng == "act":
                SC.add("act", lambda e: e.activation(out=out, in_=in_, func=AF.Copy), reads, writes)
            else:
                SC.add(eng, lambda e: e.tensor_copy(out=out, in_=in_), reads, writes)

        def TT(eng, out, in0, in1, op, reads, writes):
            SC.add(eng, lambda e: e.tensor_tensor(out=out, in0=in0, in1=in1, op=op), reads, writes)

        def TS(eng, out, in0, s1, s2, op0, op1, reads, writes, accum=None):
            if accum is None:
                if s2 is None:
                    SC.add(eng, lambda e: e.tensor_scalar(out=out, in0=in0, scalar1=s1, scalar2=None, op0=op0), reads, writes)
                else:
                    SC.add(eng, lambda e: e.tensor_scalar(out=out, in0=in0, scalar1=s1, scalar2=s2, op0=op0, op1=op1), reads, writes)
            else:
                SC.add(eng, lambda e: e.tensor_scalar(out=out, in0=in0, scalar1=s1, scalar2=s2, op0=op0, op1=op1, accum_out=accum), reads, writes)

        def STT(out, in0, scalar, in1, op0, op1, reads, writes, accum=None):
            if accum is None:
                SC.add("dve", lambda e: e.scalar_tensor_tensor(out=out, in0=in0, scalar=scalar, in1=in1, op0=op0, op1=op1), reads, writes)
            else:
                SC.add("dve", lambda e: e.scalar_tensor_tensor(out=out, in0=in0, scalar=scalar, in1=in1, op0=op0, op1=op1, accum_out=accum), reads, writes)

        def MEMSET(eng, out, val, writes):
            SC.add(eng, lambda e: e.memset(out, val), (), writes)

        identf = CA.alloc([128], F32)
        ident = CA.alloc([128], BF16)
        trif = CA.alloc([128], F32)
        tri = CA.alloc([128], BF16)
        augk_f = CA.alloc([128], F32)
        augq_f = AR.alloc([8 * 512], F32)
        augk = CA.alloc([128], BF16)
        augq = CA.alloc([8, 512], BF16)
        MEMSET("pool", identf, 1.0, ["identf"])
        SC.add("pool", lambda e: e.affine_select(out=identf, in_=identf, pattern=[[-1, 128]], compare_op=ALU.is_equal,
                                                 fill=0.0, base=0, channel_multiplier=1), ["identf"], ["identf"])
        CP("dve", ident, identf, ["identf"], ["ident"])
        MEMSET("pool", trif, 1.0, ["trif"])
        SC.add("pool", lambda e: e.affine_select(out=trif, in_=trif, pattern=[[1, 128]], compare_op=ALU.is_ge,
                                                 fill=0.0, base=0, channel_multiplier=-1), ["trif"], ["trif"])
        CP("dve", tri, trif, ["trif"], ["tri"])
        zrow = CA.alloc([512], BF16)
        MEMSET("pool", zrow, 0.0, ["zrow"])
        iota = CA.alloc([256], F32)
        DMA(iota, iota_d.partition_broadcast(128), [], ["iota"], "c2")
        DMA(augk_f[0:3, :], augk_d, [], ["augk_f"], "c0")
        DMA(augq_f[0:3, :], augq_d, [], ["augq_f"], "c1")
        CP("dve", augk[0:3, :], augk_f[0:3, :], ["augk_f"], ["augk"])
        CP("dve", augq[0:3, :, :], augq_f[0:3, :].rearrange("p (a b) -> p a b", a=8), ["augq_f"], ["augq"])

        def layernorm(y, g_bc, b_bc, out, wk, rkeys, wkeys, tagk):
            stt, mv, lnv, rstd = wk["st"], wk["mv"], wk["lnv"], wk["rstd"]
            SC.add("dve", lambda e: e.bn_stats(out=stt[:, 0:6], in_=y[:, 0:512]), rkeys, [tagk + "st"])
            SC.add("dve", lambda e: e.bn_stats(out=stt[:, 6:12], in_=y[:, 512:1024]), rkeys, [tagk + "st"])
            SC.add("dve", lambda e: e.bn_aggr(out=mv, in_=stt), [tagk + "st"], [tagk + "mv"])
            ACT(lnv, mv[:, 1:2], AF.Ln, [tagk + "mv", "eps"], [tagk + "lnv"], bias=wk["eps"])
            ACT(rstd, lnv, AF.Exp, [tagk + "lnv"], [tagk + "rstd"], scale=-0.5)
            TS("dve", y, y, mv[:, 0:1], rstd, ALU.subtract, ALU.mult, list(rkeys) + [tagk + "mv", tagk + "rstd"], wkeys_y(rkeys))
            TT("pool", y, y, g_bc, ALU.mult, list(rkeys) + ["lnp"], wkeys_y(rkeys))
            TT("pool", out, y, b_bc, ALU.add, list(rkeys) + ["lnp"], wkeys)

        def wkeys_y(rkeys):
            return list(rkeys)

        for l in range(n_layers):
            lam_init = 0.8 - 0.6 * math.exp(-0.3 * l)
            x_src = x_d if l == 0 else x2_d
            x_dst = y_d if l == n_layers - 1 else x2_d
            xsrc_key = "x2d"
            SC.barrier()
            AR.reset()
            xT = AR.alloc([8, S], BF16)
            a1_mark = AR.off
            if "A0" in phases:
                xs = [AR.alloc([D], F32) for _ in range(2)]
                xb = [AR.alloc([D], BF16) for _ in range(2)]
                for tb in range(NTB):
                    b = tb % 2
                    DMA(xs[b], x_src[tb * 128:(tb + 1) * 128, :], [xsrc_key], [("xs", b)], "xs%d" % b)
                    CP("act", xb[b], xs[b], [("xs", b)], [("xb", b)])
                    for dc in range(8):
                        TR(pb[:, dc * 128:(dc + 1) * 128], xb[b][:, dc * 128:(dc + 1) * 128], [("xb", b)], ["pb"])
                    CP("dve", xT[:, :, tb * 128:(tb + 1) * 128], pb[:, :].rearrange("p (a b) -> p a b", a=8), ["pb"], ["xT"])
                for dc in range(8):
                    DMA(xT_d[dc * 128:(dc + 1) * 128, :], xT[:, dc, :], ["xT"], ["xTd"], "xTd")

            if "A1" in phases:
                AR.off = a1_mark
                wst = [AR.alloc([3, 1024], F32) for _ in range(2)]
                wbf = [AR.alloc([3, 1024], BF16) for _ in range(2)]
                qT = AR.alloc([S], BF16)
                kT = AR.alloc([S], BF16)
                Va = AR.alloc([NTB, 130], BF16)
                Eb = [AR.alloc([512], BF16) for _ in range(4)]
                Osb = AR.alloc([4, 512], F32)
                obuf = AR.alloc([4, 128], F32)
                junk = AR.alloc([128], F32)
                yab = AR.alloc([4, 128], BF16)
                yst = [AR.alloc([512], BF16) for _ in range(2)]
                lq = AR.alloc([256], F32)
                sgb = AR.alloc([128], F32)
                gsc = AR.alloc([128], F32)
                sm = AR.alloc([32], F32)
                neglam = sm[:, 0:1]
                s12 = sm[:, 1:3]
                e12 = sm[:, 3:5]
                rz = sm[:, 8:16]
                rz2l = sm[:, 16:20]
                ss = sm[:, 20:24]
                lnv4 = sm[:, 24:28]
                rstd4 = sm[:, 28:32]
                epsr = AR.alloc([1], F32)
                MEMSET("pool", epsr, RMS_EPS, ["epsr"])
                MEMSET("pool", Va[:, :, 128:130], 1.0, ["Va1"])
                DMA(lq, lq_d[l].partition_broadcast(128), [], ["lq"], "p0")
                DMA(sgb, sg_d[l].partition_broadcast(128), [], ["sgb"], "p1")
                STT(junk[:, 0:64], lq[:, 0:64], 1.0, lq[:, 64:128], ALU.mult, ALU.mult, ["lq"], ["junk", "s1"], accum=s12[:, 0:1])
                STT(junk[:, 0:64], lq[:, 128:192], 1.0, lq[:, 192:256], ALU.mult, ALU.mult, ["lq"], ["junk", "s2"], accum=s12[:, 1:2])
                ACT(e12, s12, AF.Exp, ["s1", "s2"], ["e12"])
                TT("dve", neglam, e12[:, 1:2], e12[:, 0:1], ALU.subtract, ["e12"], ["neglam"])
                TS("dve", neglam, neglam, -lam_init, None, ALU.add, None, ["neglam"], ["neglam"])
                TS("dve", gsc, sgb, 1.0 - lam_init, None, ALU.mult, None, ["sgb"], ["gsc"])

                def load_w(h, slot):
                    for i, cb in enumerate((h, 8 + h, 16 + h)):
                        DMA(wst[slot][:, i, :], win_d[l, cb], [], [("wst", slot)], "wst%d" % slot)
                    CP("pool", wbf[slot], wst[slot], [("wst", slot)], [("wbf", slot)])

                load_w(0, 0)
                for h in range(8):
                    slot = h % 2
                    if h + 1 < 8:
                        load_w(h + 1, 1 - slot)
                    slope = 2.0 ** (-(h + 1))
                    for tq in range(8):
                        for (wi, dst, key) in ((0, qT, "qT"), (1, kT, "kT")):
                            for dc in range(8):
                                MM(pf[6][:, :], wbf[slot][:, wi, dc * 128:(dc + 1) * 128], xT[:, dc, tq * 512:(tq + 1) * 512],
                                   dc == 0, dc == 7, [("wbf", slot), "xT"], ["pf6"])
                            CP("dve", dst[:, tq * 512:(tq + 1) * 512], pf[6][:, :], ["pf6"], [key])
                    for tb4 in range(8):
                        for t in range(4):
                            tb = tb4 * 4 + t
                            for dc in range(8):
                                MM(pf[6][:, t * 128:(t + 1) * 128], xT[:, dc, tb * 128:(tb + 1) * 128],
                                   wbf[slot][:, 2, dc * 128:(dc + 1) * 128], dc == 0, dc == 7, [("wbf", slot), "xT"], ["pf6"])
                        CP("dve", Va[:, tb4 * 4:(tb4 + 1) * 4, 0:128], pf[6][:, :].rearrange("p (a b) -> p a b", a=4), ["pf6"], ["Va"])
                    ei = 0
                    si = 0
                    for j in range(8):
                        for bnk in range(4):
                            MM(pf[bnk][:, :], zrow[0:1, 0:128], zrow[0:1, 0:512], True, False, ["zrow"], [("O", bnk)])
                        for c in range(2):
                            for kb in range(4 * j + 4):
                                r = kb - 4 * j
                                nq0 = max(0, r) * 128
                                sb_ = pf[4 + si % 2]
                                skey = ("S", si % 2)
                                si += 1
                                MM(sb_[:, nq0:512], kT[64 * c:64 * c + 64, kb * 128:(kb + 1) * 128],
                                   qT[64 * c:64 * c + 64, j * 512 + nq0:(j + 1) * 512], True, False, ["qT", "kT"], [skey])
                                MM(sb_[:, nq0:512], augk[0:3, :], augq[0:3, h, nq0:512], False, True, ["augk", "augq"], [skey])
                                E = Eb[ei % 4]
                                ekey = ("E", ei % 4)
                                ei += 1
                                ACT(E[:, nq0:512], sb_[:, nq0:512], AF.Exp, [skey], [ekey], scale=0.125,
                                    bias=float(slope * (kb * 128 - j * 512)))
                                if r >= 0:
                                    TT("pool", E[:, r * 128:(r + 1) * 128], E[:, r * 128:(r + 1) * 128], tri, ALU.mult, [ekey, "tri"], [ekey])
                                for qs in range(max(0, r), 4):
                                    ob = pf[c * 2 + qs // 2]
                                    MM(ob[:, (qs % 2) * 256:(qs % 2) * 256 + 129], E[:, qs * 128:(qs + 1) * 128], Va[:, kb, 0:129],
                                       False, kb == 4 * j + qs, [ekey, "Va", "Va1"], [("O", c * 2 + qs // 2)])
                        for bnk in range(4):
                            CP("dve", Osb[:, bnk, :], pf[bnk][:, :], [("O", bnk)], ["Osb"])
                        SC.add("dve", lambda e: e.reciprocal(out=rz.rearrange("p (a b) -> p a b", a=4),
                                                             in_=Osb[:, :, 128:512:256]), ["Osb"], ["rz"])
                        TS("dve", rz2l, rz[:, 4:8], neglam, None, ALU.mult, None, ["rz", "neglam"], ["rz2l"])
                        for qs in range(4):
                            o1 = Osb[:, qs // 2, (qs % 2) * 256:(qs % 2) * 256 + 128]
                            o2 = Osb[:, 2 + qs // 2, (qs % 2) * 256:(qs % 2) * 256 + 128]
                            TS("dve", obuf[:, qs, :], o1, rz[:, qs:qs + 1], None, ALU.mult, None, ["Osb", "rz"], ["obuf"])
                            STT(obuf[:, qs, :], o2, rz2l[:, qs:qs + 1], obuf[:, qs, :], ALU.mult, ALU.add, ["Osb", "rz2l", "obuf"], ["obuf"])
                            STT(junk, obuf[:, qs, :], 1.0, obuf[:, qs, :], ALU.mult, ALU.mult, ["obuf"], ["junk", "ss"], accum=ss[:, qs:qs + 1])
                        ACT(lnv4, ss, AF.Ln, ["ss"], ["lnv4"], scale=1.0 / 128.0, bias=epsr)
                        ACT(rstd4, lnv4, AF.Exp, ["lnv4"], ["rstd4"], scale=-0.5)
                        for qs in range(4):
                            STT(yab[:, qs, :], obuf[:, qs, :], rstd4[:, qs:qs + 1], gsc, ALU.mult, ALU.mult, ["obuf", "rstd4", "gsc"], ["yab"])
                        for qs in range(4):
                            TR(pb[:, qs * 128:(qs + 1) * 128], yab[:, qs, :], ["yab"], ["pb"])
                        ys = yst[j % 2]
                        CP("dve", ys, pb[:, 0:512], ["pb"], [("yst", j % 2)])
                        DMA(yaT_d[h * 128:(h + 1) * 128, j * 512:(j + 1) * 512], ys, [("yst", j % 2)], ["yaTd"], "yst%d" % (j % 2))

            if "A2" in phases:
                SC.barrier()
                AR.off = a1_mark
                B0 = AR.alloc([S + 16], F32)
                B1 = AR.alloc([S], F32)
                B2 = AR.alloc([S], F32)
                B3 = AR.alloc([S], F32)
                xcb = AR.alloc([S], BF16)
                Yb = AR.alloc([S], BF16)
                wst2 = [AR.alloc([2, 1024], F32) for _ in range(2)]
                wbf2 = [AR.alloc([2, 1024], BF16) for _ in range(2)]
                gwf = AR.alloc([2, 1024], F32)
                gwb = AR.alloc([2, 8, 128], BF16)
                chp = AR.alloc([8, 8], F32)
                cc = AR.alloc([8, 4], F32)
                DMA(gwf[:, 0, :], gaw_d[l], [], ["gwf"], "p2")
                DMA(gwf[:, 1, :], gxw_d[l], [], ["gwf"], "p2")
                CP("dve", gwb, gwf.rearrange("p a (g j) -> p a g j", g=8), ["gwf"], ["gwb"])
                DMA(chp, chp_d[l].rearrange("p (g f) -> p g f", g=8), [], ["chp"], "p3")
                ACT(cc[:, :, 2], chp[:, :, 7], AF.Exp, ["chp"], ["cc"], scale=-1.0)
                ACT(cc[:, :, 3], cc[:, :, 2], AF.Ln, ["cc"], ["cc"], bias=1.0)
                TS("dve", cc[:, :, 0], cc[:, :, 3], -8.0, None, ALU.mult, None, ["cc"], ["cc"])
                TS("dve", cc[:, :, 1], cc[:, :, 3], -16.0, None, ALU.mult, None, ["cc"], ["cc"])

                def load_w2(g, slot):
                    DMA(wst2[slot][:, 0, :], win_d[l, 24 + g], [], [("wst2", slot)], "wst2%d" % slot)
                    DMA(wst2[slot][:, 1, :], win_d[l, 32 + g], [], [("wst2", slot)], "wst2%d" % slot)
                    CP("pool", wbf2[slot], wst2[slot], [("wst2", slot)], [("wbf2", slot)])

                load_w2(0, 0)
                pi = 0
                for g in range(8):
                    slot = g % 2
                    if g + 1 < 8:
                        load_w2(g + 1, 1 - slot)
                    MEMSET("pool", B0[:, 0:3], 0.0, ["B0"])
                    for tq in range(8):
                        pp = pf[pi % 7]
                        pk = ("pf", pi % 7)
                        pi += 1
                        for dc in range(8):
                            MM(pp[:, :], wbf2[slot][:, 0, dc * 128:(dc + 1) * 128], xT[:, dc, tq * 512:(tq + 1) * 512], dc == 0, dc == 7,
                               [("wbf2", slot), "xT"], [pk])
                        CP("act", B0[:, 3 + tq * 512:3 + (tq + 1) * 512], pp[:, :], [pk], ["B0"])
                    TS("dve", B1, B0[:, 0:S], chp[:, g, 0:1], chp[:, g, 4:5], ALU.mult, ALU.add, ["B0", "chp"], ["B1"])
                    for k in range(1, 4):
                        STT(B1, B0[:, k:k + S], chp[:, g, k:k + 1], B1, ALU.mult, ALU.add, ["B0", "chp", "B1"], ["B1"])
                    CP("pool", xcb, B1, ["B1"], ["xcb"])
                    for (wi, dst, off, key, bcol) in ((0, B2, 0, "B2", 5), (1, B0, 3, "B0", 6)):
                        for tq in range(8):
                            pp = pf[pi % 7]
                            pk = ("pf", pi % 7)
                            pi += 1
                            MM(pp[:, :], gwb[:, wi, g, :], xcb[:, tq * 512:(tq + 1) * 512], True, True, ["gwb", "xcb"], [pk])
                            ACT(dst[:, off + tq * 512:off + (tq + 1) * 512], pp[:, :], AF.Sigmoid, [pk, "chp", "B1"], [key],
                                bias=chp[:, g, bcol:bcol + 1])
                    ACT(B3, B2, AF.Exp, ["B2", "cc"], ["B3"], scale=cc[:, g, 1:2])
                    ACT(B3, B3, AF.Sqrt, ["B3"], ["B3"], scale=-1.0, bias=1.0)
                    MEMSET("pool", B3[:, 0:1], 1.0, ["B3"])
                    ACT(B2, B2, AF.Exp, ["B2", "cc"], ["B2"], scale=cc[:, g, 0:1])
                    TT("dve", B0[:, 3:3 + S], B0[:, 3:3 + S], B1, ALU.mult, ["B0", "B1"], ["B0"])
                    TT("dve", B0[:, 3:3 + S], B0[:, 3:3 + S], B3, ALU.mult, ["B0", "B3"], ["B0"])
                    SC.add("dve", lambda e: e.tensor_tensor_scan(out=B3, data0=B2, data1=B0[:, 3:3 + S], initial=0.0,
                                                                 op0=ALU.mult, op1=ALU.add), ["B2", "B0", "B3"], ["B3"])
                    for tq in range(8):
                        pp = pf[pi % 7]
                        pk = ("pf", pi % 7)
                        pi += 1
                        for dc in range(8):
                            MM(pp[:, :], wbf2[slot][:, 1, dc * 128:(dc + 1) * 128], xT[:, dc, tq * 512:(tq + 1) * 512], dc == 0, dc == 7,
                               [("wbf2", slot), "xT"], [pk])
                        ACT(B1[:, tq * 512:(tq + 1) * 512], pp[:, :], AF.Gelu_apprx_tanh, [pk, "B0"], ["B1"])
                    TT("dve", Yb, B3, B1, ALU.mult, ["B3", "B1", "yrTd"], ["Yb"])
                    DMA(yrT_d[g * 128:(g + 1) * 128, :], Yb, ["Yb"], ["yrTd"], "yrTd")

            if "A3" in phases:
                SC.barrier()
                AR.reset()
                wg = AR.alloc([16, 8, 128], BF16)
                wba = AR.alloc([8, 1024], BF16)
                wbl = AR.alloc([8, 1024], BF16)
                wo = AR.alloc([8, 1024], BF16)
                stg = [AR.alloc([1024], F32) for _ in range(2)]
                lng = AR.alloc([D], F32)
                lnb = AR.alloc([D], F32)
                DMA(lng, ln1g_d[l].partition_broadcast(128), [], ["lnp"], "p4")
                DMA(lnb, ln1b_d[l].partition_broadcast(128), [], ["lnp"], "p4")
                si_ = 0
                for cb in range(16):
                    b = si_ % 2
                    si_ += 1
                    DMA(stg[b], win_d[l, 40 + cb], [], [("stg", b)], "stg%d" % b)
                    CP("dve" if cb % 2 else "pool", wg[:, cb, :, :], stg[b].rearrange("p (a b) -> p a b", a=8), [("stg", b)], ["wg"])
                for (src, dst, key) in ((wba_d, wba, "wba"), (wbl_d, wbl, "wbl"), (wo_d, wo, "wo")):
                    for kc in range(8):
                        b = si_ % 2
                        si_ += 1
                        DMA(stg[b], src[l, :, kc, :], [], [("stg", b)], "stg%d" % b)
                        CP("dve" if kc % 2 else "pool", dst[:, kc, :], stg[b], [("stg", b)], [key])
                xTb = [AR.alloc([8, 512], BF16) for _ in range(2)]
                yaTb = [AR.alloc([8, 512], BF16) for _ in range(2)]
                yrTb = [AR.alloc([8, 512], BF16) for _ in range(2)]
                mT = AR.alloc([8, 512], BF16)
                sga = [AR.alloc([512], F32) for _ in range(2)]
                sgr = [AR.alloc([512], F32) for _ in range(2)]
                xres = [AR.alloc([D], F32) for _ in range(2)]
                yln = [AR.alloc([D], F32) for _ in range(2)]
                wk = {"st": AR.alloc([12], F32), "mv": AR.alloc([2], F32), "lnv": AR.alloc([1], F32),
                      "rstd": AR.alloc([1], F32), "eps": AR.alloc([1], F32)}
                MEMSET("pool", wk["eps"], LN_EPS, ["eps"])
                ti = 0
                for tq in range(8):
                    b = tq % 2
                    DMA(xTb[b], xT_d[:, tq * 512:(tq + 1) * 512].rearrange("(a p) t -> p a t", p=128), ["xTd"], [("xTb", b)], "xTb%d" % b)
                    DMA(yaTb[b], yaT_d[:, tq * 512:(tq + 1) * 512].rearrange("(a p) t -> p a t", p=128), ["yaTd"], [("yaTb", b)], "yaTb%d" % b)
                    DMA(yrTb[b], yrT_d[:, tq * 512:(tq + 1) * 512].rearrange("(a p) t -> p a t", p=128), ["yrTd"], [("yrTb", b)], "yrTb%d" % b)
                    for jb in range(8):
                        for dc in range(8):
                            MM(pf[0][:, :], wg[:, jb, dc, :], xTb[b][:, dc, :], dc == 0, dc == 7, ["wg", ("xTb", b)], ["pf0"])
                        for dc in range(8):
                            MM(pf[1][:, :], wg[:, 8 + jb, dc, :], xTb[b][:, dc, :], dc == 0, dc == 7, ["wg", ("xTb", b)], ["pf1"])
                        for kc in range(8):
                            MM(pf[2][:, :], wba[:, kc, jb * 128:(jb + 1) * 128], yaTb[b][:, kc, :], kc == 0, kc == 7, ["wba", ("yaTb", b)], ["pf2"])
                        for kc in range(8):
                            MM(pf[3][:, :], wbl[:, kc, jb * 128:(jb + 1) * 128], yrTb[b][:, kc, :], kc == 0, kc == 7, ["wbl", ("yrTb", b)], ["pf3"])
                        sb2 = jb % 2
                        ACT(sga[sb2], pf[0][:, :], AF.Sigmoid, ["pf0"], [("sga", sb2)])
                        ACT(sgr[sb2], pf[1][:, :], AF.Sigmoid, ["pf1"], [("sgr", sb2)])
                        TT("dve", sga[sb2], sga[sb2], pf[2][:, :], ALU.mult, [("sga", sb2), "pf2"], [("sga", sb2)])
                        TT("dve", sgr[sb2], sgr[sb2], pf[3][:, :], ALU.mult, [("sgr", sb2), "pf3"], [("sgr", sb2)])
                        TT("pool", mT[:, jb, :], sga[sb2], sgr[sb2], ALU.add, [("sga", sb2), ("sgr", sb2)], ["mT"])
                    for ts_ in range(4):
                        tb = tq * 4 + ts_
                        xb_ = ti % 2
                        ti += 1
                        DMA(xres[xb_], x_src[tb * 128:(tb + 1) * 128, :], [xsrc_key], [("xres", xb_)], "xres%d" % xb_)
                        for nh in range(2):
                            for jb in range(8):
                                MM(pf[4 + nh][:, :], mT[:, jb, ts_ * 128:(ts_ + 1) * 128], wo[:, jb, nh * 512:(nh + 1) * 512], jb == 0, jb == 7,
                                   ["mT", "wo"], [("pf", 4 + nh)])
                            STT(yln[xb_][:, nh * 512:(nh + 1) * 512], xres[xb_][:, nh * 512:(nh + 1) * 512], ALPHA, pf[4 + nh][:, :],
                                ALU.mult, ALU.add, [("xres", xb_), ("pf", 4 + nh)], [("yln", xb_)])
                        layernorm(yln[xb_], lng, lnb, yln[xb_], wk, [("yln", xb_)], [("yln", xb_)], "ln")
                        DMA(x1_d[tb * 128:(tb + 1) * 128, :], yln[xb_], [("yln", xb_)], ["x1d"], "x1st%d" % xb_)

            if "B" in phases:
                SC.barrier()
                AR.reset()
                wq = AR.alloc([16, 8, 128], BF16)
                skT = AR.alloc([16, 128], BF16)
                stg = [AR.alloc([2048], F32) for _ in range(2)]
                lng = AR.alloc([D], F32)
                lnb = AR.alloc([D], F32)
                DMA(lng, ln2g_d[l].partition_broadcast(128), [], ["lnp"], "p4")
                DMA(lnb, ln2b_d[l].partition_broadcast(128), [], ["lnp"], "p4")
                for cb in range(16):
                    b = cb % 2
                    DMA(stg[b][:, 0:1024], wq_d[l, cb], [], [("stg", b)], "stg%d" % b)
                    CP("dve" if cb % 2 else "pool", wq[:, cb, :, :], stg[b][:, 0:1024].rearrange("p (a b) -> p a b", a=8), [("stg", b)], ["wq"])
                DMA(stg[0], sk_d[l], [], [("stg", 0)], "stg0")
                CP("dve", skT, stg[0].rearrange("p (a b) -> p a b", a=16), [("stg", 0)], ["skT"])
                NRB = 8
                x1 = [AR.alloc([D], F32) for _ in range(2)]
                x1b = AR.alloc([D], BF16)
                x1T = AR.alloc([8, 128], BF16)
                qTs = AR.alloc([16, 128], BF16)
                scs = AR.alloc([16, 128], F32)
                top = AR.alloc([16, 16], F32)
                tix = AR.alloc([16, 16], U32)
                tixf = AR.alloc([16, 16], F32)
                work = AR.alloc([256], F32)
                cand = AR.alloc([8, 256], F32)
                eid = AR.alloc([8, 256], F32)
                tsv = AR.alloc([8, 16], F32)
                pos = AR.alloc([8, 16], U32)
                posf = AR.alloc([8, 16], F32)
                ef = AR.alloc([128], F32)
                eidx = [AR.alloc([128], I32) for _ in range(2)]
                gt = [AR.alloc([8, 16], F32) for _ in range(2)]
                dsm = AR.alloc([8, 16], F32)
                zs = AR.alloc([8], F32)
                actv = AR.alloc([128], F32)
                wgt = AR.alloc([128], F32)
                acc = AR.alloc([D], F32)
                junkb = AR.alloc([D], F32)
                Ru = [AR.alloc([D], F32) for _ in range(NRB)]
                Rv = [AR.alloc([D], F32) for _ in range(NRB)]
                wk = {"st": AR.alloc([12], F32), "mv": AR.alloc([2], F32), "lnv": AR.alloc([1], F32),
                      "rstd": AR.alloc([1], F32), "eps": AR.alloc([1], F32)}
                MEMSET("pool", wk["eps"], LN_EPS, ["eps"])
                pu_l = pu_ds[l]
                pv_l = pv_ds[l]

                def routing(tb):
                    b = tb % 2
                    DMA(x1[b], x1_d[tb * 128:(tb + 1) * 128, :], ["x1d"], [("x1", b)], "x1ld%d" % b)
                    CP("act", x1b, x1[b], [("x1", b)], ["x1b"])
                    for dc in range(8):
                        TR(pb[:, dc * 128:(dc + 1) * 128], x1b[:, dc * 128:(dc + 1) * 128], ["x1b"], ["pb"])
                    CP("act", x1T, pb[:, :].rearrange("p (a b) -> p a b", a=8), ["pb"], ["x1T"])
                    for c4 in range(4):
                        pp = pf[c4 % 2]
                        pk = ("pf", c4 % 2)
                        for ci in range(4):
                            cb = c4 * 4 + ci
                            for dc in range(8):
                                MM(pp[:, ci * 128:(ci + 1) * 128], wq[:, cb, dc, :], x1T[:, dc, :], dc == 0, dc == 7, ["wq", "x1T"], [pk])
                        CP("act", qTs[:, c4 * 4:(c4 + 1) * 4, :], pp[:, :].rearrange("p (a b) -> p a b", a=4), [pk], ["qTs"])
                    for c4 in range(4):
                        pp = pf[2 + c4]
                        pk = ("pf", 2 + c4)
                        for ci in range(4):
                            cb = c4 * 4 + ci
                            MM(pp[:, ci * 128:(ci + 1) * 128], qTs[:, cb, :], skT[:, cb, :], True, True, ["qTs", "skT"], [pk])
                        CP("act", scs[:, c4 * 4:(c4 + 1) * 4, :], pp[:, :].rearrange("p (a b) -> p a b", a=4), [pk], ["scs"])
                    for g in range(16):
                        SC.add("dve", lambda e, g=g: e.max(out=top[:, g, 0:8], in_=scs[:, g, :]), ["scs"], ["top"])
                        SC.add("dve", lambda e, g=g: e.max_index(out=tix[:, g, 0:8], in_max=top[:, g, 0:8], in_values=scs[:, g, :]), ["scs", "top"], ["tix"])
                        SC.add("dve", lambda e, g=g: e.match_replace(out=work[:, 0:128], in_to_replace=top[:, g, 0:8], in_values=scs[:, g, :],
                                                                     imm_value=-1e30), ["scs", "top"], ["work"])
                        SC.add("dve", lambda e, g=g: e.max(out=top[:, g, 8:16], in_=work[:, 0:128]), ["work"], ["top"])
                        SC.add("dve", lambda e, g=g: e.max_index(out=tix[:, g, 8:16], in_max=top[:, g, 8:16], in_values=work[:, 0:128]), ["work", "top"], ["tix"])
                    CP("dve", tixf, tix, ["tix"], ["tixf"])
                    top4 = top.rearrange("p (h c) k -> p h c k", c=2)
                    tix4 = tixf.rearrange("p (h c) k -> p h c k", c=2)
                    cand4 = cand.rearrange("p h (a b) -> p h a b", a=16)
                    eid4 = eid.rearrange("p h (a b) -> p h a b", a=16)
                    TT("dve", cand4, top4[:, :, 0, :].unsqueeze(3).broadcast_to([128, 8, 16, 16]),
                       top4[:, :, 1, :].unsqueeze(2).broadcast_to([128, 8, 16, 16]), ALU.add, ["top"], ["cand"])
                    TS("dve", tix4[:, :, 0, :], tix4[:, :, 0, :], 128.0, None, ALU.mult, None, ["tixf"], ["tixf"])
                    TT("dve", eid4, tix4[:, :, 0, :].unsqueeze(3).broadcast_to([128, 8, 16, 16]),
                       tix4[:, :, 1, :].unsqueeze(2).broadcast_to([128, 8, 16, 16]), ALU.add, ["tixf"], ["eid"])
                    for h in range(8):
                        SC.add("dve", lambda e, h=h: e.max(out=tsv[:, h, 0:8], in_=cand[:, h, :]), ["cand"], ["tsv"])
                        SC.add("dve", lambda e, h=h: e.max_index(out=pos[:, h, 0:8], in_max=tsv[:, h, 0:8], in_values=cand[:, h, :]), ["cand", "tsv"], ["pos"])
                        SC.add("dve", lambda e, h=h: e.match_replace(out=work, in_to_replace=tsv[:, h, 0:8], in_values=cand[:, h, :],
                                                                     imm_value=-1e30), ["cand", "tsv"], ["work"])
                        SC.add("dve", lambda e, h=h: e.max(out=tsv[:, h, 8:16], in_=work), ["work"], ["tsv"])
                        SC.add("dve", lambda e, h=h: e.max_index(out=pos[:, h, 8:16], in_max=tsv[:, h, 8:16], in_values=work), ["work", "tsv"], ["pos"])
                    CP("dve", posf, pos, ["pos"], ["posf"])
                    for h in range(8):
                        for k in range(16):
                            STT(work, iota, posf[:, h, k:k + 1], eid[:, h, :], ALU.is_equal, ALU.mult,
                                ["iota", "posf", "eid"], ["work", "ef"], accum=ef[:, h * 16 + k:h * 16 + k + 1])
                    CP("dve", eidx[b], ef, ["ef"], [("eidx", b)])
                    TT("dve", dsm, tsv, tsv[:, :, 0:1].broadcast_to([128, 8, 16]), ALU.subtract, ["tsv"], ["dsm"])
                    ACT(dsm, dsm, AF.Exp, ["dsm"], ["dsm"])
                    SC.add("dve", lambda e: e.tensor_reduce(out=zs, in_=dsm, axis=AX.X, op=ALU.add), ["dsm"], ["zs"])
                    SC.add("dve", lambda e: e.reciprocal(out=zs, in_=zs), ["zs"], ["zs"])
                    TT("dve", gt[b], dsm, zs.unsqueeze(2).broadcast_to([128, 8, 16]), ALU.mult, ["dsm", "zs"], [("gt", b)])

                gi = [0, 0]

                def evaluate(tb):
                    b = tb % 2
                    for s_ in range(128):
                        rb = gi[0] % NRB
                        gi[0] += 1
                        SC.add("pool", lambda e, s_=s_, rb=rb, pu_l=pu_l, b=b: e.indirect_dma_start(
                            out=Ru[rb], out_offset=None, in_=pu_l, in_offset=bass.IndirectOffsetOnAxis(ap=eidx[b][:, s_:s_ + 1], axis=0)),
                            [("eidx", b)], [("Ru", rb)], dma="gu%d" % rb)
                        STT(junkb, Ru[rb], 1.0, x1[b], ALU.mult, ALU.mult, [("Ru", rb), ("x1", b)], ["junkb", "actv"], accum=actv[:, s_:s_ + 1])
                    ACT(wgt, actv, AF.Gelu_apprx_tanh, ["actv"], ["wgt"])
                    TT("dve", wgt, wgt, gt[b].rearrange("p h k -> p (h k)"), ALU.mult, ["wgt", ("gt", b)], ["wgt"])
                    for s_ in range(128):
                        rb = gi[1] % NRB
                        gi[1] += 1
                        SC.add("pool", lambda e, s_=s_, rb=rb, pv_l=pv_l, b=b: e.indirect_dma_start(
                            out=Rv[rb], out_offset=None, in_=pv_l, in_offset=bass.IndirectOffsetOnAxis(ap=eidx[b][:, s_:s_ + 1], axis=0)),
                            [("eidx", b)], [("Rv", rb)], dma="gv%d" % rb)
                        if s_ == 0:
                            TS("dve", acc, Rv[rb], wgt[:, 0:1], None, ALU.mult, None, [("Rv", rb), "wgt"], ["acc"])
                        else:
                            STT(acc, Rv[rb], wgt[:, s_:s_ + 1], acc, ALU.mult, ALU.add, [("Rv", rb), "wgt", "acc"], ["acc"])
                    STT(acc, x1[b], ALPHA, acc, ALU.mult, ALU.add, [("x1", b), "acc"], ["acc"])
                    layernorm(acc, lng, lnb, acc, wk, ["acc"], ["acc"], "ln")
                    DMA(x_dst[tb * 128:(tb + 1) * 128, :], acc, ["acc"], ["x2d"], "x2st")

                routing(0)
                for tb in range(NTB):
                    if tb + 1 < NTB:
                        routing(tb + 1)
                    evaluate(tb)

        n = SC.finalize(block)
    return nc, n


def prep_shared(inp):
    f = lambda a: np.ascontiguousarray(np.asarray(a, dtype=np.float32))
    w_in = np.asarray(inp["w_in"], np.float32)
    out = {}
    out["w_in"] = f(w_in.reshape(L, 8, 128, 56, 128).transpose(0, 3, 2, 1, 4).reshape(L, 56, 128, 1024))
    for k in ("w_br_attn", "w_br_lru", "w_out"):
        out[k] = f(np.asarray(inp[k], np.float32).reshape(L, 8, 128, 1024).transpose(0, 2, 1, 3))
    wq = np.asarray(inp["peer_wq"], np.float32)
    out["peer_wq"] = f(wq.reshape(L, 8, 128, 16, 128).transpose(0, 3, 2, 1, 4).reshape(L, 16, 128, 1024))
    sk = np.asarray(inp["peer_subkeys"], np.float32)
    out["peer_skT"] = f(sk.transpose(0, 4, 1, 2, 3).reshape(L, 128, 16 * 128))
    for i in range(L):
        out["peer_u%d" % i] = f(np.asarray(inp["peer_u"][i], np.float32))
        out["peer_v%d" % i] = f(np.asarray(inp["peer_v"][i], np.float32))
    out["iota256"] = np.arange(256, dtype=np.float32)
    out["gate_a_w"] = f(np.asarray(inp["gate_a_w"], np.float32).transpose(0, 2, 1, 3).reshape(L, 128, 1024))
    out["gate_x_w"] = f(np.asarray(inp["gate_x_w"], np.float32).transpose(0, 2, 1, 3).reshape(L, 128, 1024))
    chp = np.zeros((L, 8, 128, 8), np.float32)
    cw = np.asarray(inp["conv_w"], np.float32)
    for k in range(4):
        chp[:, :, :, k] = cw[:, k, :].reshape(L, 8, 128)
    chp[:, :, :, 4] = np.asarray(inp["conv_b"], np.float32).reshape(L, 8, 128)
    chp[:, :, :, 5] = np.asarray(inp["gate_a_b"], np.float32).reshape(L, 8, 128)
    chp[:, :, :, 6] = np.asarray(inp["gate_x_b"], np.float32).reshape(L, 8, 128)
    chp[:, :, :, 7] = np.asarray(inp["lru_lambda"], np.float32).reshape(L, 8, 128)
    out["chp"] = f(chp.transpose(0, 2, 1, 3).reshape(L, 128, 64))
    out["lambda_qk"] = f(np.asarray(inp["lambda_qk"], np.float32).reshape(L, 256))
    for k in ("subln_g", "ln1_g", "ln1_b", "ln2_g", "ln2_b"):
        out[k] = f(inp[k])
    augk = np.zeros((3, 128), np.float32)
    augk[0] = np.arange(128)
    augk[1] = 1.0
    augk[2] = 1.0
    augq = np.zeros((3, 8, 512), np.float32)
    qq = np.arange(512)
    for h in range(8):
        ch = 8.0 * 2.0 ** (-(h + 1))
        augq[0, h] = ch
        augq[1, h] = -ch * 128.0 * (qq // 128)
        augq[2, h] = -ch * (qq % 128)
    out["aug_k"] = augk
    out["aug_q"] = f(augq.reshape(3, 8 * 512))
    return out


_CACHE = {}


def kernel(**inputs):
    shared = prep_shared(inputs)
    x = np.asarray(inputs["x"], np.float32)
    nb = x.shape[0]
    if "nc" not in _CACHE:
        _CACHE["nc"] = build_program()[0]
    nc = _CACHE["nc"]
    in_maps = []
    for b in range(nb):
        m = dict(shared)
        m["x"] = np.ascontiguousarray(x[b])
        in_maps.append(m)
    res = run_bass_kernel_spmd(nc, in_maps, core_ids=list(range(nb)))
    return np.stack([np.asarray(r["y"], np.float32) for r in res.results], axis=0)
```

```python
import math
import numpy as np
import concourse.bass as bass
import concourse.mybir as mybir
from concourse.bass_utils import run_bass_kernel_spmd
from contextlib import ExitStack

F32 = mybir.dt.float32
BF16 = mybir.dt.bfloat16
U32 = mybir.dt.uint32
I32 = mybir.dt.int32
AF = mybir.ActivationFunctionType
ALU = mybir.AluOpType
AX = mybir.AxisListType

D = 1024
S = 4096
L = 2
NTB = S // 128
ALPHA = (2.0 * L) ** 0.25
LN_EPS = 1e-5
RMS_EPS = 1e-6
NEXP = 16384


class Op:
    __slots__ = ("eng", "fn", "deps", "is_dma", "sem", "count", "signals", "waits", "idx", "barrier")


class Sched:
    def __init__(self, nc, stack, same_engine_sync=True):
        self.nc = nc
        self.stack = stack
        self.ops = []
        self.last_w = {}
        self.readers = {}
        self.same = same_engine_sync
        self.semh = {}

    def sem(self, key):
        if key not in self.semh:
            self.semh[key] = self.stack.enter_context(self.nc.semaphore("s_%d" % len(self.semh)))
        return self.semh[key]

    def add(self, eng, fn, reads=(), writes=(), dma=None):
        op = Op()
        op.eng = eng
        op.fn = fn
        op.barrier = False
        op.is_dma = dma is not None
        op.sem = ("dma", dma) if dma is not None else ("eng", eng)
        op.signals = op.is_dma
        op.count = 0
        op.idx = len(self.ops)
        deps = set()
        for r in reads:
            w = self.last_w.get(r)
            if w is not None:
                deps.add(w)
        for w_ in writes:
            w = self.last_w.get(w_)
            if w is not None:
                deps.add(w)
            for rd in self.readers.get(w_, ()):
                deps.add(rd)
        op.deps = deps
        for r in reads:
            self.readers.setdefault(r, []).append(op.idx)
        for w_ in writes:
            self.last_w[w_] = op.idx
            self.readers[w_] = []
        self.ops.append(op)
        return op

    def barrier(self, exclude=("cv",)):
        last = {}
        for op in self.ops:
            if not op.barrier and not op.is_dma:
                last[op.eng] = op
        for op in last.values():
            op.signals = True
        for eng in ("sp", "act", "dve", "pool", "pe"):
            op = Op()
            op.eng = eng
            op.fn = None
            op.barrier = True
            op.is_dma = False
            op.sem = None
            op.signals = False
            op.count = 0
            op.idx = len(self.ops)
            op.deps = set()
            op.waits = tuple(("dma", k) for k in exclude)
            self.ops.append(op)
        keep = {k: v for k, v in self.last_w.items() if self.ops[v].is_dma and self.ops[v].sem[1] in exclude}
        self.last_w = keep
        self.readers = {}

    def _skip(self, dop, op):
        return (not dop.is_dma) and dop.eng == op.eng and (not op.is_dma) and (dop.eng == "pe" or not self.same)

    def finalize(self, block):
        ops = self.ops
        for op in ops:
            for d in op.deps:
                dop = ops[d]
                if dop.is_dma or self._skip(dop, op):
                    continue
                dop.signals = True
        cnt = {}
        for op in ops:
            if op.barrier:
                op.count = dict(cnt)
                continue
            if op.signals:
                inc = 16 if op.is_dma else 1
                cnt[op.sem] = cnt.get(op.sem, 0) + inc
                op.count = cnt[op.sem]
        waited = {}
        for op in ops:
            w = waited.setdefault(op.eng, {})
            if op.barrier:
                excl = op.waits
                need = {s: v for s, v in op.count.items() if s != ("eng", op.eng) and s not in excl}
                op.waits = []
            else:
                op.waits = []
                need = {}
                for d in op.deps:
                    dop = ops[d]
                    if self._skip(dop, op):
                        continue
                    if need.get(dop.sem, 0) < dop.count:
                        need[dop.sem] = dop.count
            for s, v in need.items():
                if w.get(s, 0) < v:
                    w[s] = v
                    op.waits.append((s, v))
        final_waits = [(s, v) for s, v in cnt.items() if s[0] == "dma"]
        for s in cnt:
            self.sem(s)
        per_eng = {}
        for op in ops:
            per_eng.setdefault(op.eng, []).append(op)

        def emit(engname, eng_obj, final=False):
            for op in per_eng.get(engname, []):
                for s, v in op.waits:
                    eng_obj.wait_ge(self.sem(s), v)
                if op.fn is None:
                    continue
                ins = op.fn(eng_obj)
                if op.signals:
                    ins.then_inc(self.sem(op.sem), 16 if op.is_dma else 1)
            if final:
                for s, v in final_waits:
                    eng_obj.wait_ge(self.sem(s), v)

        @block.sync
        def _(e):
            emit("sp", e, final=True)

        @block.scalar
        def _(e):
            emit("act", e)

        @block.vector
        def _(e):
            emit("dve", e)

        @block.gpsimd
        def _(e):
            emit("pool", e)

        @block.tensor
        def _(e):
            emit("pe", e)
        return len(ops)


class Arena:
    def __init__(self, tensor, ncols):
        self.t = tensor
        self.n = ncols
        self.off = 0

    def reset(self):
        self.off = 0

    def alloc(self, shape, dt):
        n = 1
        for s_ in shape:
            n *= s_
        if dt == BF16:
            ncol = (n + 1) // 2
        else:
            ncol = n
        ncol = (ncol + 15) // 16 * 16
        assert self.off + ncol <= self.n, ("arena overflow", self.off, ncol, self.n)
        v = self.t[:, self.off:self.off + ncol]
        self.off += ncol
        if dt != F32:
            v = v.bitcast(dt)
        v = v[:, 0:n]
        if len(shape) == 2:
            v = v.rearrange("p (a b) -> p a b", a=shape[0])
        elif len(shape) == 3:
            v = v.rearrange("p (a b c) -> p a b c", a=shape[0], b=shape[1])
        return v


def build_program(n_layers=L, debug=False, phases=("A0", "A1", "A2", "A3", "B")):
    nc = bass.Bass("TRN2", target_bir_lowering=False)

    def din(name, shape, dt=F32):
        return nc.dram_tensor(name, list(shape), dt, kind="ExternalInput").ap()

    def dscr(name, shape, dt):
        return nc.dram_tensor(name, list(shape), dt, kind="ExternalOutput" if debug else "Internal").ap()

    x_d = din("x", [S, D])
    win_d = din("w_in", [L, 56, 128, 1024])
    wba_d = din("w_br_attn", [L, 128, 8, 1024])
    wbl_d = din("w_br_lru", [L, 128, 8, 1024])
    wo_d = din("w_out", [L, 128, 8, 1024])
    wq_d = din("peer_wq", [L, 16, 128, 1024])
    sk_d = din("peer_skT", [L, 128, 16 * 128])
    pu_ds = [din("peer_u%d" % i, [NEXP, D]) for i in range(L)]
    pv_ds = [din("peer_v%d" % i, [NEXP, D]) for i in range(L)]
    iota_d = din("iota256", [256])
    gaw_d = din("gate_a_w", [L, 128, 8 * 128])
    gxw_d = din("gate_x_w", [L, 128, 8 * 128])
    chp_d = din("chp", [L, 128, 64])
    lq_d = din("lambda_qk", [L, 256])
    sg_d = din("subln_g", [L, 128])
    ln1g_d = din("ln1_g", [L, D])
    ln1b_d = din("ln1_b", [L, D])
    ln2g_d = din("ln2_g", [L, D])
    ln2b_d = din("ln2_b", [L, D])
    augk_d = din("aug_k", [3, 128])
    augq_d = din("aug_q", [3, 8 * 512])
    y_d = nc.dram_tensor("y", [S, D], F32, kind="ExternalOutput").ap()

    xT_d = dscr("xT_s", [D, S], BF16)
    yaT_d = dscr("yaT_s", [D, S], BF16)
    yrT_d = dscr("yrT_s", [D, S], BF16)
    x1_d = dscr("x1_s", [S, D], F32)
    x2_d = dscr("x2_s", [S, D], F32)
    tb16 = [nc.dram_tensor("tb16_%d" % i, [NEXP, 2 * D], BF16, kind="Internal").ap() for i in range(L)]

    with ExitStack() as st:
        ARN = 49000
        arena_t = st.enter_context(nc.sbuf_tensor("arena", [128, ARN], F32))
        cst_t = st.enter_context(nc.sbuf_tensor("cst", [128, 3200], F32))
        pf = [st.enter_context(nc.psum_tensor("pf%d" % i, [128, 512], F32)) for i in range(7)]
        pb = st.enter_context(nc.psum_tensor("pbb", [128, 1024], BF16))
        block = st.enter_context(nc.Block())
        SC = Sched(nc, st)
        AR = Arena(arena_t, ARN)
        CA = Arena(cst_t, 3200)

        def DMA(out, in_, reads, writes, key, q="sp"):
            SC.add(q, lambda e: e.dma_start(out=out, in_=in_), reads, writes, dma=key)

        def MM(out, lhsT, rhs, start, stop, reads, writes):
            SC.add("pe", lambda e: e.matmul(out, lhsT=lhsT, rhs=rhs, start=start, stop=stop), reads, writes)

        def TR(out, in_, reads, writes):
            SC.add("pe", lambda e: e.transpose(out=out, in_=in_, identity=ident), list(reads) + ["ident"], writes)

        def ACT(out, in_, func, reads, writes, bias=None, scale=None, accum=None):
            kw = {}
            if bias is not None:
                kw["bias"] = bias
            if scale is not None:
                kw["scale"] = scale
            if accum is not None:
                kw["accum_out"] = accum
            SC.add("act", lambda e: e.activation(out=out, in_=in_, func=func, **kw), reads, writes)

        def CP(eng, out, in_, reads, writes):
            if eng == "act":
                SC.add("act", lambda e: e.activation(out=out, in_=in_, func=AF.Copy), reads, writes)
            else:
                SC.add(eng, lambda e: e.tensor_copy(out=out, in_=in_), reads, writes)

        def TT(eng, out, in0, in1, op, reads, writes):
            SC.add(eng, lambda e: e.tensor_tensor(out=out, in0=in0, in1=in1, op=op), reads, writes)

        def TS(eng, out, in0, s1, s2, op0, op1, reads, writes, accum=None):
            if accum is None:
                if s2 is None:
                    SC.add(eng, lambda e: e.tensor_scalar(out=out, in0=in0, scalar1=s1, scalar2=None, op0=op0), reads, writes)
                else:
                    SC.add(eng, lambda e: e.tensor_scalar(out=out, in0=in0, scalar1=s1, scalar2=s2, op0=op0, op1=op1), reads, writes)
            else:
                SC.add(eng, lambda e: e.tensor_scalar(out=out, in0=in0, scalar1=s1, scalar2=s2, op0=op0, op1=op1, accum_out=accum), reads, writes)

        def STT(out, in0, scalar, in1, op0, op1, reads, writes, accum=None):
            if accum is None:
                SC.add("dve", lambda e: e.scalar_tensor_tensor(out=out, in0=in0, scalar=scalar, in1=in1, op0=op0, op1=op1), reads, writes)
            else:
                SC.add("dve", lambda e: e.scalar_tensor_tensor(out=out, in0=in0, scalar=scalar, in1=in1, op0=op0, op1=op1, accum_out=accum), reads, writes)

        def MEMSET(eng, out, val, writes):
            SC.add(eng, lambda e: e.memset(out, val), (), writes)

        identf = CA.alloc([128], F32)
        ident = CA.alloc([128], BF16)
        trif = CA.alloc([128], F32)
        tri = CA.alloc([128], BF16)
        augk_f = CA.alloc([128], F32)
        augq_f = AR.alloc([8 * 512], F32)
        augk = CA.alloc([128], BF16)
        augq = CA.alloc([8, 512], BF16)
        MEMSET("pool", identf, 1.0, ["identf"])
        SC.add("pool", lambda e: e.affine_select(out=identf, in_=identf, pattern=[[-1, 128]], compare_op=ALU.is_equal,
                                                 fill=0.0, base=0, channel_multiplier=1), ["identf"], ["identf"])
        CP("dve", ident, identf, ["identf"], ["ident"])
        MEMSET("pool", trif, 1.0, ["trif"])
        SC.add("pool", lambda e: e.affine_select(out=trif, in_=trif, pattern=[[1, 128]], compare_op=ALU.is_ge,
                                                 fill=0.0, base=0, channel_multiplier=-1), ["trif"], ["trif"])
        CP("dve", tri, trif, ["trif"], ["tri"])
        zrow = CA.alloc([512], BF16)
        MEMSET("pool", zrow, 0.0, ["zrow"])
        iota = CA.alloc([256], F32)
        DMA(iota, iota_d.partition_broadcast(128), [], ["iota"], "c2")
        DMA(augk_f[64:67, :], augk_d, [], ["augk_f"], "c0")
        DMA(augq_f[64:67, :], augq_d, [], ["augq_f"], "c1")
        CP("dve", augk[64:67, :], augk_f[64:67, :], ["augk_f"], ["augk"])
        CP("dve", augq[64:67, :, :], augq_f[64:67, :].rearrange("p (a b) -> p a b", a=8), ["augq_f"], ["augq"])

        def conv_dma(dst, src, key):
            SC.add("pool", lambda e: e.dma_start(out=dst, in_=src), [], [key], dma="cv")
        if "B" in phases:
            for l_ in range(n_layers):
                for (src, c0) in ((pu_ds[l_], 0), (pv_ds[l_], D)):
                    for ch in range(4):
                        conv_dma(tb16[l_][ch * 4096:(ch + 1) * 4096, c0:c0 + D], src[ch * 4096:(ch + 1) * 4096, :], ("tb16", l_))

        def layernorm(y, g_bc, b_bc, out, wk, rkeys, wkeys, tagk):
            stt, mv, lnv, rstd = wk["st"], wk["mv"], wk["lnv"], wk["rstd"]
            SC.add("dve", lambda e: e.bn_stats(out=stt[:, 0:6], in_=y[:, 0:512]), rkeys, [tagk + "st"])
            SC.add("dve", lambda e: e.bn_stats(out=stt[:, 6:12], in_=y[:, 512:1024]), rkeys, [tagk + "st"])
            SC.add("dve", lambda e: e.bn_aggr(out=mv, in_=stt), [tagk + "st"], [tagk + "mv"])
            ACT(lnv, mv[:, 1:2], AF.Ln, [tagk + "mv", "eps"], [tagk + "lnv"], bias=wk["eps"])
            ACT(rstd, lnv, AF.Exp, [tagk + "lnv"], [tagk + "rstd"], scale=-0.5)
            TS("dve", y, y, mv[:, 0:1], rstd, ALU.subtract, ALU.mult, list(rkeys) + [tagk + "mv", tagk + "rstd"], wkeys_y(rkeys))
            TT("pool", y, y, g_bc, ALU.mult, list(rkeys) + ["lnp"], wkeys_y(rkeys))
            TT("pool", out, y, b_bc, ALU.add, list(rkeys) + ["lnp"], wkeys)

        def wkeys_y(rkeys):
            return list(rkeys)

        for l in range(n_layers):
            lam_init = 0.8 - 0.6 * math.exp(-0.3 * l)
            x_src = x_d if l == 0 else x2_d
            x_dst = y_d if l == n_layers - 1 else x2_d
            xsrc_key = "x2d"
            SC.barrier()
            AR.reset()
            xT = AR.alloc([8, S], BF16)
            a1_mark = AR.off
            if "A0" in phases:
                xs = [AR.alloc([D], F32) for _ in range(2)]
                xb = [AR.alloc([D], BF16) for _ in range(2)]
                for tb in range(NTB):
                    b = tb % 2
                    DMA(xs[b], x_src[tb * 128:(tb + 1) * 128, :], [xsrc_key], [("xs", b)], "xs%d" % b)
                    CP("act", xb[b], xs[b], [("xs", b)], [("xb", b)])
                    for dc in range(8):
                        TR(pb[:, dc * 128:(dc + 1) * 128], xb[b][:, dc * 128:(dc + 1) * 128], [("xb", b)], ["pb"])
                    CP("dve", xT[:, :, tb * 128:(tb + 1) * 128], pb[:, :].rearrange("p (a b) -> p a b", a=8), ["pb"], ["xT"])
                for dc in range(8):
                    DMA(xT_d[dc * 128:(dc + 1) * 128, :], xT[:, dc, :], ["xT"], ["xTd"], "xTd")

            if "A1" in phases:
                AR.off = a1_mark
                wst = [AR.alloc([3, 1024], F32) for _ in range(2)]
                wbf = [AR.alloc([3, 1024], BF16) for _ in range(2)]
                qT2 = AR.alloc([2, S], BF16)
                kT2 = AR.alloc([2, S], BF16)
                Va = AR.alloc([NTB, 130], BF16)
                NE = 6
                Eb = [AR.alloc([512], BF16) for _ in range(NE)]
                Osb = AR.alloc([4, 512], F32)
                obuf = AR.alloc([4, 128], F32)
                junk = AR.alloc([128], F32)
                yab = AR.alloc([4, 128], BF16)
                yst = [AR.alloc([512], BF16) for _ in range(2)]
                lq = AR.alloc([256], F32)
                sgb = AR.alloc([128], F32)
                gsc = AR.alloc([128], F32)
                sm = AR.alloc([32], F32)
                neglam = sm[:, 0:1]
                s12 = sm[:, 1:3]
                e12 = sm[:, 3:5]
                rz = sm[:, 8:16]
                rz2l = sm[:, 16:20]
                ss = sm[:, 20:24]
                lnv4 = sm[:, 24:28]
                rstd4 = sm[:, 28:32]
                epsr = AR.alloc([1], F32)
                MEMSET("pool", epsr, RMS_EPS, ["epsr"])
                MEMSET("pool", Va[:, :, 128:130], 1.0, ["Va1"])
                for c in range(2):
                    CP("pool", kT2[64:67, c, :].rearrange("p (a b) -> p a b", a=NTB),
                       augk[64:67, :].unsqueeze(1).broadcast_to([3, NTB, 128]), ["augk"], ["kT"])
                DMA(lq, lq_d[l].partition_broadcast(128), [], ["lq"], "p0")
                DMA(sgb, sg_d[l].partition_broadcast(128), [], ["sgb"], "p1")
                STT(junk[:, 0:64], lq[:, 0:64], 1.0, lq[:, 64:128], ALU.mult, ALU.mult, ["lq"], ["junk", "s1"], accum=s12[:, 0:1])
                STT(junk[:, 0:64], lq[:, 128:192], 1.0, lq[:, 192:256], ALU.mult, ALU.mult, ["lq"], ["junk", "s2"], accum=s12[:, 1:2])
                ACT(e12, s12, AF.Exp, ["s1", "s2"], ["e12"])
                TT("dve", neglam, e12[:, 1:2], e12[:, 0:1], ALU.subtract, ["e12"], ["neglam"])
                TS("dve", neglam, neglam, -lam_init, None, ALU.add, None, ["neglam"], ["neglam"])
                TS("dve", gsc, sgb, 1.0 - lam_init, None, ALU.mult, None, ["sgb"], ["gsc"])

                def load_w(h, slot):
                    for i, cb in enumerate((h, 8 + h, 16 + h)):
                        DMA(wst[slot][:, i, :], win_d[l, cb], [], [("wst", slot)], "wst%d" % slot)
                    CP("pool", wbf[slot], wst[slot], [("wst", slot)], [("wbf", slot)])

                def epilogue2(h, j):
                    SC.add("dve", lambda e: e.reciprocal(out=rz.rearrange("p (a b) -> p a b", a=4),
                                                         in_=Osb[:, :, 128:512:256]), ["Osb"], ["rz"])
                    TS("dve", rz2l, rz[:, 4:8], neglam, None, ALU.mult, None, ["rz", "neglam"], ["rz2l"])
                    for qs in range(4):
                        o1 = Osb[:, qs // 2, (qs % 2) * 256:(qs % 2) * 256 + 128]
                        o2 = Osb[:, 2 + qs // 2, (qs % 2) * 256:(qs % 2) * 256 + 128]
                        TS("dve", obuf[:, qs, :], o1, rz[:, qs:qs + 1], None, ALU.mult, None, ["Osb", "rz"], ["obuf"])
                        STT(obuf[:, qs, :], o2, rz2l[:, qs:qs + 1], obuf[:, qs, :], ALU.mult, ALU.add, ["Osb", "rz2l", "obuf"], ["obuf"])
                        STT(junk, obuf[:, qs, :], 1.0, obuf[:, qs, :], ALU.mult, ALU.mult, ["obuf"], ["junk", "ss"], accum=ss[:, qs:qs + 1])
                    ACT(lnv4, ss, AF.Ln, ["ss"], ["lnv4"], scale=1.0 / 128.0, bias=epsr)
                    ACT(rstd4, lnv4, AF.Exp, ["lnv4"], ["rstd4"], scale=-0.5)
                    for qs in range(4):
                        STT(yab[:, qs, :], obuf[:, qs, :], rstd4[:, qs:qs + 1], gsc, ALU.mult, ALU.mult, ["obuf", "rstd4", "gsc"], ["yab"])
                    for qs in range(4):
                        TR(pb[:, qs * 128:(qs + 1) * 128], yab[:, qs, :], ["yab"], ["pb"])
                    ys = yst[j % 2]
                    CP("dve", ys, pb[:, 0:512], ["pb"], [("yst", j % 2)])
                    DMA(yaT_d[h * 128:(h + 1) * 128, j * 512:(j + 1) * 512], ys, [("yst", j % 2)], ["yaTd"], "yst%d" % (j % 2))

                load_w(0, 0)
                ei = 0
                si = 0
                for h in range(8):
                    slot = h % 2
                    if h + 1 < 8:
                        load_w(h + 1, 1 - slot)
                    slope = 2.0 ** (-(h + 1))
                    for c in range(2):
                        CP("pool", qT2[64:67, c, :].rearrange("p (a b) -> p a b", a=8),
                           augq[64:67, h, :].unsqueeze(1).broadcast_to([3, 8, 512]), ["augq"], ["qT"])
                    for tq in range(8):
                        for (wi, dst, key) in ((0, qT2, "qT"), (1, kT2, "kT")):
                            for dc in range(8):
                                MM(pf[6][:, :], wbf[slot][:, wi, dc * 128:(dc + 1) * 128], xT[:, dc, tq * 512:(tq + 1) * 512],
                                   dc == 0, dc == 7, [("wbf", slot), "xT"], ["pf6"])
                            CP("dve", dst[0:64, 0, tq * 512:(tq + 1) * 512], pf[6][0:64, :], ["pf6"], [key])
                            CP("dve", dst[0:64, 1, tq * 512:(tq + 1) * 512], pf[6][64:128, :], ["pf6"], [key])
                    for tb4 in range(8):
                        for t in range(4):
                            tb = tb4 * 4 + t
                            for dc in range(8):
                                MM(pf[6][:, t * 128:(t + 1) * 128], xT[:, dc, tb * 128:(tb + 1) * 128],
                                   wbf[slot][:, 2, dc * 128:(dc + 1) * 128], dc == 0, dc == 7, [("wbf", slot), "xT"], ["pf6"])
                        CP("dve", Va[:, tb4 * 4:(tb4 + 1) * 4, 0:128], pf[6][:, :].rearrange("p (a b) -> p a b", a=4), ["pf6"], ["Va"])
                    tiles = [(j, c, kb) for j in range(8) for c in range(2) for kb in range(4 * j + 4)]
                    sinfo = {}

                    def emit_S(i):
                        nonlocal si
                        j, c, kb = tiles[i]
                        r = kb - 4 * j
                        nq0 = max(0, r) * 128
                        sb_ = pf[4 + si % 2]
                        skey = ("S", si % 2)
                        si += 1
                        MM(sb_[:, nq0:512], kT2[0:67, c, kb * 128:(kb + 1) * 128],
                           qT2[0:67, c, j * 512 + nq0:(j + 1) * 512], True, True, ["qT", "kT"], [skey])
                        sinfo[i] = (sb_, skey)

                    pending = []

                    def emit_rest(i):
                        nonlocal ei
                        j, c, kb = tiles[i]
                        r = kb - 4 * j
                        nq0 = max(0, r) * 128
                        sb_, skey = sinfo.pop(i)
                        if c == 0 and kb == 0:
                            for bnk in range(4):
                                MM(pf[bnk][:, :], zrow[0:1, 0:128], zrow[0:1, 0:512], True, False, ["zrow"], [("O", bnk)])
                        E = Eb[ei % NE]
                        ekey = ("E", ei % NE)
                        ei += 1
                        ACT(E[:, nq0:512], sb_[:, nq0:512], AF.Exp, [skey], [ekey], scale=0.125,
                            bias=float(slope * (kb * 128 - j * 512)))
                        if r >= 0:
                            TT("pool", E[:, r * 128:(r + 1) * 128], E[:, r * 128:(r + 1) * 128], tri, ALU.mult, [ekey, "tri"], [ekey])
                        for qs in range(max(0, r), 4):
                            ob = pf[c * 2 + qs // 2]
                            MM(ob[:, (qs % 2) * 256:(qs % 2) * 256 + 129], E[:, qs * 128:(qs + 1) * 128], Va[:, kb, 0:129],
                               False, kb == 4 * j + qs, [ekey, "Va", "Va1"], [("O", c * 2 + qs // 2)])
                        if c == 1 and kb == 4 * j + 3:
                            for bnk in range(4):
                                CP("dve", Osb[:, bnk, :], pf[bnk][:, :], [("O", bnk)], ["Osb"])
                            pending.append((i + 5, h, j))
                        while pending and (pending[0][0] <= i or i == len(tiles) - 1):
                            _, hh, jj = pending.pop(0)
                            epilogue2(hh, jj)

                    emit_S(0)
                    for i in range(len(tiles)):
                        if i + 1 < len(tiles):
                            emit_S(i + 1)
                        emit_rest(i)

            if "A2" in phases:
                SC.barrier()
                AR.off = a1_mark
                B0 = AR.alloc([S + 16], F32)
                B1 = AR.alloc([S], F32)
                B2 = AR.alloc([S], F32)
                B3 = AR.alloc([S], F32)
                xcb = AR.alloc([S], BF16)
                Yb = AR.alloc([S], BF16)
                wst2 = [AR.alloc([2, 1024], F32) for _ in range(2)]
                wbf2 = [AR.alloc([2, 1024], BF16) for _ in range(2)]
                gwf = AR.alloc([2, 1024], F32)
                gwb = AR.alloc([2, 8, 128], BF16)
                chp = AR.alloc([8, 8], F32)
                cc = AR.alloc([8, 4], F32)
                DMA(gwf[:, 0, :], gaw_d[l], [], ["gwf"], "p2")
                DMA(gwf[:, 1, :], gxw_d[l], [], ["gwf"], "p2")
                CP("dve", gwb, gwf.rearrange("p a (g j) -> p a g j", g=8), ["gwf"], ["gwb"])
                DMA(chp, chp_d[l].rearrange("p (g f) -> p g f", g=8), [], ["chp"], "p3")
                ACT(cc[:, :, 2], chp[:, :, 7], AF.Exp, ["chp"], ["cc"], scale=-1.0)
                ACT(cc[:, :, 3], cc[:, :, 2], AF.Ln, ["cc"], ["cc"], bias=1.0)
                TS("dve", cc[:, :, 0], cc[:, :, 3], -8.0, None, ALU.mult, None, ["cc"], ["cc"])
                TS("dve", cc[:, :, 1], cc[:, :, 3], -16.0, None, ALU.mult, None, ["cc"], ["cc"])

                def load_w2(g, slot):
                    DMA(wst2[slot][:, 0, :], win_d[l, 24 + g], [], [("wst2", slot)], "wst2%d" % slot)
                    DMA(wst2[slot][:, 1, :], win_d[l, 32 + g], [], [("wst2", slot)], "wst2%d" % slot)
                    CP("pool", wbf2[slot], wst2[slot], [("wst2", slot)], [("wbf2", slot)])

                load_w2(0, 0)
                pi = 0
                for g in range(8):
                    slot = g % 2
                    if g + 1 < 8:
                        load_w2(g + 1, 1 - slot)
                    MEMSET("pool", B0[:, 0:3], 0.0, ["B0"])
                    for tq in range(8):
                        pp = pf[pi % 7]
                        pk = ("pf", pi % 7)
                        pi += 1
                        for dc in range(8):
                            MM(pp[:, :], wbf2[slot][:, 0, dc * 128:(dc + 1) * 128], xT[:, dc, tq * 512:(tq + 1) * 512], dc == 0, dc == 7,
                               [("wbf2", slot), "xT"], [pk])
                        CP("act", B0[:, 3 + tq * 512:3 + (tq + 1) * 512], pp[:, :], [pk], ["B0"])
                    TS("dve", B1, B0[:, 0:S], chp[:, g, 0:1], chp[:, g, 4:5], ALU.mult, ALU.add, ["B0", "chp"], ["B1"])
                    for k in range(1, 4):
                        STT(B1, B0[:, k:k + S], chp[:, g, k:k + 1], B1, ALU.mult, ALU.add, ["B0", "chp", "B1"], ["B1"])
                    CP("pool", xcb, B1, ["B1"], ["xcb"])
                    for (wi, dst, off, key, bcol) in ((0, B2, 0, "B2", 5), (1, B0, 3, "B0", 6)):
                        for tq in range(8):
                            pp = pf[pi % 7]
                            pk = ("pf", pi % 7)
                            pi += 1
                            MM(pp[:, :], gwb[:, wi, g, :], xcb[:, tq * 512:(tq + 1) * 512], True, True, ["gwb", "xcb"], [pk])
                            ACT(dst[:, off + tq * 512:off + (tq + 1) * 512], pp[:, :], AF.Sigmoid, [pk, "chp", "B1"], [key],
                                bias=chp[:, g, bcol:bcol + 1])
                    ACT(B3, B2, AF.Exp, ["B2", "cc"], ["B3"], scale=cc[:, g, 1:2])
                    ACT(B3, B3, AF.Sqrt, ["B3"], ["B3"], scale=-1.0, bias=1.0)
                    MEMSET("pool", B3[:, 0:1], 1.0, ["B3"])
                    ACT(B2, B2, AF.Exp, ["B2", "cc"], ["B2"], scale=cc[:, g, 0:1])
                    TT("dve", B0[:, 3:3 + S], B0[:, 3:3 + S], B1, ALU.mult, ["B0", "B1"], ["B0"])
                    TT("dve", B0[:, 3:3 + S], B0[:, 3:3 + S], B3, ALU.mult, ["B0", "B3"], ["B0"])
                    SC.add("dve", lambda e: e.tensor_tensor_scan(out=B3, data0=B2, data1=B0[:, 3:3 + S], initial=0.0,
                                                                 op0=ALU.mult, op1=ALU.add), ["B2", "B0", "B3"], ["B3"])
                    for tq in range(8):
                        pp = pf[pi % 7]
                        pk = ("pf", pi % 7)
                        pi += 1
                        for dc in range(8):
                            MM(pp[:, :], wbf2[slot][:, 1, dc * 128:(dc + 1) * 128], xT[:, dc, tq * 512:(tq + 1) * 512], dc == 0, dc == 7,
                               [("wbf2", slot), "xT"], [pk])
                        ACT(B1[:, tq * 512:(tq + 1) * 512], pp[:, :], AF.Gelu_apprx_tanh, [pk, "B0"], ["B1"])
                    TT("dve", Yb, B3, B1, ALU.mult, ["B3", "B1", "yrTd"], ["Yb"])
                    DMA(yrT_d[g * 128:(g + 1) * 128, :], Yb, ["Yb"], ["yrTd"], "yrTd")

            if "A3" in phases:
                SC.barrier()
                AR.reset()
                wg = AR.alloc([16, 8, 128], BF16)
                wba = AR.alloc([8, 1024], BF16)
                wbl = AR.alloc([8, 1024], BF16)
                wo = AR.alloc([8, 1024], BF16)
                stg = [AR.alloc([1024], F32) for _ in range(2)]
                lng = AR.alloc([D], F32)
                lnb = AR.alloc([D], F32)
                DMA(lng, ln1g_d[l].partition_broadcast(128), [], ["lnp"], "p4")
                DMA(lnb, ln1b_d[l].partition_broadcast(128), [], ["lnp"], "p4")
                si_ = 0
                for cb in range(16):
                    b = si_ % 2
                    si_ += 1
                    DMA(stg[b], win_d[l, 40 + cb], [], [("stg", b)], "stg%d" % b)
                    CP("dve" if cb % 2 else "pool", wg[:, cb, :, :], stg[b].rearrange("p (a b) -> p a b", a=8), [("stg", b)], ["wg"])
                for (src, dst, key) in ((wba_d, wba, "wba"), (wbl_d, wbl, "wbl"), (wo_d, wo, "wo")):
                    for kc in range(8):
                        b = si_ % 2
                        si_ += 1
                        DMA(stg[b], src[l, :, kc, :], [], [("stg", b)], "stg%d" % b)
                        CP("dve" if kc % 2 else "pool", dst[:, kc, :], stg[b], [("stg", b)], [key])
                xTb = [AR.alloc([8, 512], BF16) for _ in range(2)]
                yaTb = [AR.alloc([8, 512], BF16) for _ in range(2)]
                yrTb = [AR.alloc([8, 512], BF16) for _ in range(2)]
                mT = AR.alloc([8, 512], BF16)
                sga = [AR.alloc([512], F32) for _ in range(2)]
                sgr = [AR.alloc([512], F32) for _ in range(2)]
                xres = [AR.alloc([D], F32) for _ in range(2)]
                yln = [AR.alloc([D], F32) for _ in range(2)]
                wk = {"st": AR.alloc([12], F32), "mv": AR.alloc([2], F32), "lnv": AR.alloc([1], F32),
                      "rstd": AR.alloc([1], F32), "eps": AR.alloc([1], F32)}
                MEMSET("pool", wk["eps"], LN_EPS, ["eps"])
                ti = 0
                for tq in range(8):
                    b = tq % 2
                    DMA(xTb[b], xT_d[:, tq * 512:(tq + 1) * 512].rearrange("(a p) t -> p a t", p=128), ["xTd"], [("xTb", b)], "xTb%d" % b)
                    DMA(yaTb[b], yaT_d[:, tq * 512:(tq + 1) * 512].rearrange("(a p) t -> p a t", p=128), ["yaTd"], [("yaTb", b)], "yaTb%d" % b)
                    DMA(yrTb[b], yrT_d[:, tq * 512:(tq + 1) * 512].rearrange("(a p) t -> p a t", p=128), ["yrTd"], [("yrTb", b)], "yrTb%d" % b)
                    for jb in range(8):
                        for dc in range(8):
                            MM(pf[0][:, :], wg[:, jb, dc, :], xTb[b][:, dc, :], dc == 0, dc == 7, ["wg", ("xTb", b)], ["pf0"])
                        for dc in range(8):
                            MM(pf[1][:, :], wg[:, 8 + jb, dc, :], xTb[b][:, dc, :], dc == 0, dc == 7, ["wg", ("xTb", b)], ["pf1"])
                        for kc in range(8):
                            MM(pf[2][:, :], wba[:, kc, jb * 128:(jb + 1) * 128], yaTb[b][:, kc, :], kc == 0, kc == 7, ["wba", ("yaTb", b)], ["pf2"])
                        for kc in range(8):
                            MM(pf[3][:, :], wbl[:, kc, jb * 128:(jb + 1) * 128], yrTb[b][:, kc, :], kc == 0, kc == 7, ["wbl", ("yrTb", b)], ["pf3"])
                        sb2 = jb % 2
                        ACT(sga[sb2], pf[0][:, :], AF.Sigmoid, ["pf0"], [("sga", sb2)])
                        ACT(sgr[sb2], pf[1][:, :], AF.Sigmoid, ["pf1"], [("sgr", sb2)])
                        TT("dve", sga[sb2], sga[sb2], pf[2][:, :], ALU.mult, [("sga", sb2), "pf2"], [("sga", sb2)])
                        TT("dve", sgr[sb2], sgr[sb2], pf[3][:, :], ALU.mult, [("sgr", sb2), "pf3"], [("sgr", sb2)])
                        TT("pool", mT[:, jb, :], sga[sb2], sgr[sb2], ALU.add, [("sga", sb2), ("sgr", sb2)], ["mT"])
                    for ts_ in range(4):
                        tb = tq * 4 + ts_
                        xb_ = ti % 2
                        ti += 1
                        DMA(xres[xb_], x_src[tb * 128:(tb + 1) * 128, :], [xsrc_key], [("xres", xb_)], "xres%d" % xb_)
                        for nh in range(2):
                            for jb in range(8):
                                MM(pf[4 + nh][:, :], mT[:, jb, ts_ * 128:(ts_ + 1) * 128], wo[:, jb, nh * 512:(nh + 1) * 512], jb == 0, jb == 7,
                                   ["mT", "wo"], [("pf", 4 + nh)])
                            STT(yln[xb_][:, nh * 512:(nh + 1) * 512], xres[xb_][:, nh * 512:(nh + 1) * 512], ALPHA, pf[4 + nh][:, :],
                                ALU.mult, ALU.add, [("xres", xb_), ("pf", 4 + nh)], [("yln", xb_)])
                        layernorm(yln[xb_], lng, lnb, yln[xb_], wk, [("yln", xb_)], [("yln", xb_)], "ln")
                        DMA(x1_d[tb * 128:(tb + 1) * 128, :], yln[xb_], [("yln", xb_)], ["x1d"], "x1st%d" % xb_)

            if "B" in phases:
                SC.barrier()
                AR.reset()
                wq = AR.alloc([16, 8, 128], BF16)
                skT = AR.alloc([16, 128], BF16)
                lng = AR.alloc([D], F32)
                lnb = AR.alloc([D], F32)
                bmark = AR.off
                stg = [AR.alloc([2048], F32) for _ in range(2)]
                DMA(lng, ln2g_d[l].partition_broadcast(128), [], ["lnp"], "p4")
                DMA(lnb, ln2b_d[l].partition_broadcast(128), [], ["lnp"], "p4")
                for cb in range(16):
                    b = cb % 2
                    DMA(stg[b][:, 0:1024], wq_d[l, cb], [], [("stg", b)], "stg%d" % b)
                    CP("dve" if cb % 2 else "pool", wq[:, cb, :, :], stg[b][:, 0:1024].rearrange("p (a b) -> p a b", a=8), [("stg", b)], ["wq"])
                DMA(stg[0], sk_d[l], [], [("stg", 0)], "stg0")
                CP("dve", skT, stg[0].rearrange("p (a b) -> p a b", a=16), [("stg", 0)], ["skT"])
                SC.barrier()
                AR.off = bmark
                x1 = [AR.alloc([D], F32) for _ in range(2)]
                x1b = [AR.alloc([D], BF16) for _ in range(2)]
                x1T = AR.alloc([8, 128], BF16)
                qTs = AR.alloc([16, 128], BF16)
                scs = AR.alloc([16, 128], F32)
                top = AR.alloc([16, 16], F32)
                tix = AR.alloc([16, 16], U32)
                tixf = AR.alloc([16, 16], F32)
                work = AR.alloc([256], F32)
                cand = AR.alloc([8, 256], F32)
                eid = AR.alloc([8, 256], F32)
                tsv = AR.alloc([8, 16], F32)
                pos = AR.alloc([8, 16], U32)
                posf = AR.alloc([8, 16], F32)
                ef = AR.alloc([128], F32)
                eidx = [AR.alloc([128], I32) for _ in range(2)]
                gt = [AR.alloc([8, 16], F32) for _ in range(2)]
                dsm = AR.alloc([8, 16], F32)
                zs = AR.alloc([8], F32)
                actv = AR.alloc([128], F32)
                wgt = AR.alloc([128], F32)
                acc = AR.alloc([D], F32)
                junkb = AR.alloc([D], BF16)
                ND = 8
                diag = [AR.alloc([128], BF16) for _ in range(ND)]
                wk = {"st": AR.alloc([12], F32), "mv": AR.alloc([2], F32), "lnv": AR.alloc([1], F32),
                      "rstd": AR.alloc([1], F32), "eps": AR.alloc([1], F32)}
                NG = 4
                LOOK = 4
                NRB = (LOOK + 1) * NG
                Rb = [AR.alloc([2 * D], BF16) for _ in range(NRB)]
                MEMSET("pool", wk["eps"], LN_EPS, ["eps"])
                tbl = tb16[l]
                tkey = ("tb16", l)
                pacc = [pf[5], pf[6]]

                def routing(tb):
                    b = tb % 2
                    DMA(x1[b], x1_d[tb * 128:(tb + 1) * 128, :], ["x1d"], [("x1", b)], "x1ld%d" % b)
                    CP("act", x1b[b], x1[b], [("x1", b)], [("x1b", b)])
                    for dc in range(8):
                        TR(pb[:, dc * 128:(dc + 1) * 128], x1b[b][:, dc * 128:(dc + 1) * 128], [("x1b", b)], ["pb"])
                    CP("act", x1T, pb[:, :].rearrange("p (a b) -> p a b", a=8), ["pb"], ["x1T"])
                    for c4 in range(4):
                        pp = pf[0]
                        pk = ("pf", 0)
                        for ci in range(4):
                            cb = c4 * 4 + ci
                            for dc in range(8):
                                MM(pp[:, ci * 128:(ci + 1) * 128], wq[:, cb, dc, :], x1T[:, dc, :], dc == 0, dc == 7, ["wq", "x1T"], [pk])
                        CP("act", qTs[:, c4 * 4:(c4 + 1) * 4, :], pp[:, :].rearrange("p (a b) -> p a b", a=4), [pk], ["qTs"])
                    for c4 in range(4):
                        pp = pf[1 + c4]
                        pk = ("pf", 1 + c4)
                        for ci in range(4):
                            cb = c4 * 4 + ci
                            MM(pp[:, ci * 128:(ci + 1) * 128], qTs[:, cb, :], skT[:, cb, :], True, True, ["qTs", "skT"], [pk])
                        CP("act", scs[:, c4 * 4:(c4 + 1) * 4, :], pp[:, :].rearrange("p (a b) -> p a b", a=4), [pk], ["scs"])
                    for g in range(16):
                        SC.add("dve", lambda e, g=g: e.max(out=top[:, g, 0:8], in_=scs[:, g, :]), ["scs"], ["top"])
                        SC.add("dve", lambda e, g=g: e.max_index(out=tix[:, g, 0:8], in_max=top[:, g, 0:8], in_values=scs[:, g, :]), ["scs", "top"], ["tix"])
                        SC.add("dve", lambda e, g=g: e.match_replace(out=work[:, 0:128], in_to_replace=top[:, g, 0:8], in_values=scs[:, g, :],
                                                                     imm_value=-1e30), ["scs", "top"], ["work"])
                        SC.add("dve", lambda e, g=g: e.max(out=top[:, g, 8:16], in_=work[:, 0:128]), ["work"], ["top"])
                        SC.add("dve", lambda e, g=g: e.max_index(out=tix[:, g, 8:16], in_max=top[:, g, 8:16], in_values=work[:, 0:128]), ["work", "top"], ["tix"])
                    CP("dve", tixf, tix, ["tix"], ["tixf"])
                    top4 = top.rearrange("p (h c) k -> p h c k", c=2)
                    tix4 = tixf.rearrange("p (h c) k -> p h c k", c=2)
                    cand4 = cand.rearrange("p h (a b) -> p h a b", a=16)
                    eid4 = eid.rearrange("p h (a b) -> p h a b", a=16)
                    TT("dve", cand4, top4[:, :, 0, :].unsqueeze(3).broadcast_to([128, 8, 16, 16]),
                       top4[:, :, 1, :].unsqueeze(2).broadcast_to([128, 8, 16, 16]), ALU.add, ["top"], ["cand"])
                    TS("dve", tix4[:, :, 0, :], tix4[:, :, 0, :], 128.0, None, ALU.mult, None, ["tixf"], ["tixf"])
                    TT("dve", eid4, tix4[:, :, 0, :].unsqueeze(3).broadcast_to([128, 8, 16, 16]),
                       tix4[:, :, 1, :].unsqueeze(2).broadcast_to([128, 8, 16, 16]), ALU.add, ["tixf"], ["eid"])
                    for h in range(8):
                        SC.add("dve", lambda e, h=h: e.max(out=tsv[:, h, 0:8], in_=cand[:, h, :]), ["cand"], ["tsv"])
                        SC.add("dve", lambda e, h=h: e.max_index(out=pos[:, h, 0:8], in_max=tsv[:, h, 0:8], in_values=cand[:, h, :]), ["cand", "tsv"], ["pos"])
                        SC.add("dve", lambda e, h=h: e.match_replace(out=work, in_to_replace=tsv[:, h, 0:8], in_values=cand[:, h, :],
                                                                     imm_value=-1e30), ["cand", "tsv"], ["work"])
                        SC.add("dve", lambda e, h=h: e.max(out=tsv[:, h, 8:16], in_=work), ["work"], ["tsv"])
                        SC.add("dve", lambda e, h=h: e.max_index(out=pos[:, h, 8:16], in_max=tsv[:, h, 8:16], in_values=work), ["work", "tsv"], ["pos"])
                    CP("dve", posf, pos, ["pos"], ["posf"])
                    for h in range(8):
                        for k in range(16):
                            STT(work, iota, posf[:, h, k:k + 1], eid[:, h, :], ALU.is_equal, ALU.mult,
                                ["iota", "posf", "eid"], ["work", "ef"], accum=ef[:, h * 16 + k:h * 16 + k + 1])
                    CP("dve", eidx[b], ef, ["ef"], [("eidx", b)])
                    TT("dve", dsm, tsv, tsv[:, :, 0:1].broadcast_to([128, 8, 16]), ALU.subtract, ["tsv"], ["dsm"])
                    ACT(dsm, dsm, AF.Exp, ["dsm"], ["dsm"])
                    SC.add("dve", lambda e: e.tensor_reduce(out=zs, in_=dsm, axis=AX.X, op=ALU.add), ["dsm"], ["zs"])
                    SC.add("dve", lambda e: e.reciprocal(out=zs, in_=zs), ["zs"], ["zs"])
                    TT("dve", gt[b], dsm, zs.unsqueeze(2).broadcast_to([128, 8, 16]), ALU.mult, ["dsm", "zs"], [("gt", b)])

                NGR = 128 // NG
                gi = [0]
                di = [0]
                slotbuf = {}
                issued = set()

                def gathers(tb, g):
                    if (tb, g) in issued:
                        return
                    issued.add((tb, g))
                    b = tb % 2
                    for i in range(NG):
                        s_ = g * NG + i
                        rb = gi[0] % NRB
                        gi[0] += 1
                        slotbuf[(tb, s_)] = rb
                        SC.add("pool", lambda e, s_=s_, rb=rb, b=b, tbl=tbl: e.indirect_dma_start(
                            out=Rb[rb], out_offset=None, in_=tbl, in_offset=bass.IndirectOffsetOnAxis(ap=eidx[b][:, s_:s_ + 1], axis=0)),
                            [("eidx", b), tkey], [("R", rb)], dma="g%d" % rb)

                def evaluate(tb):
                    b = tb % 2
                    gtf = gt[b].rearrange("p h k -> p (h k)")
                    for g in range(LOOK):
                        gathers(tb, g)
                    for g in range(NGR):
                        if g + LOOK < NGR:
                            gathers(tb, g + LOOK)
                        elif tb + 1 < NTB:
                            gathers(tb + 1, g + LOOK - NGR)
                        sl = slice(g * NG, (g + 1) * NG)
                        for i in range(NG):
                            s_ = g * NG + i
                            rb = slotbuf[(tb, s_)]
                            STT(junkb, Rb[rb][:, 0:D], 1.0, x1b[b], ALU.mult, ALU.mult, [("R", rb), ("x1b", b)], ["junkb", ("actv", g)],
                                accum=actv[:, s_:s_ + 1])
                        ACT(wgt[:, sl], actv[:, sl], AF.Gelu_apprx_tanh, [("actv", g)], [("wgt", g)])
                        TT("dve", wgt[:, sl], wgt[:, sl], gtf[:, sl], ALU.mult, [("wgt", g), ("gt", b)], [("wgt", g)])
                        for i in range(NG):
                            s_ = g * NG + i
                            rb = slotbuf.pop((tb, s_))
                            dk = di[0] % ND
                            di[0] += 1
                            TS("pool", diag[dk], ident, wgt[:, s_:s_ + 1], 1.0, ALU.mult, ALU.mult, [("wgt", g), "ident"], [("diag", dk)])
                            for half in range(2):
                                MM(pacc[half][:, :], diag[dk], Rb[rb][:, D + half * 512:D + (half + 1) * 512], s_ == 0, s_ == 127,
                                   [("diag", dk), ("R", rb)], [("pacc", half)])
                    for half in range(2):
                        STT(acc[:, half * 512:(half + 1) * 512], x1[b][:, half * 512:(half + 1) * 512], ALPHA, pacc[half][:, :],
                            ALU.mult, ALU.add, [("x1", b), ("pacc", half)], ["acc"])
                    layernorm(acc, lng, lnb, acc, wk, ["acc"], ["acc"], "ln")
                    DMA(x_dst[tb * 128:(tb + 1) * 128, :], acc, ["acc"], ["x2d"], "x2st")

                routing(0)
                for tb in range(NTB):
                    if tb + 1 < NTB:
                        routing(tb + 1)
                    evaluate(tb)

        n = SC.finalize(block)
    return nc, n


def prep_shared(inp):
    f = lambda a: np.ascontiguousarray(np.asarray(a, dtype=np.float32))
    w_in = np.asarray(inp["w_in"], np.float32)
    out = {}
    out["w_in"] = f(w_in.reshape(L, 8, 128, 56, 128).transpose(0, 3, 2, 1, 4).reshape(L, 56, 128, 1024))
    for k in ("w_br_attn", "w_br_lru", "w_out"):
        out[k] = f(np.asarray(inp[k], np.float32).reshape(L, 8, 128, 1024).transpose(0, 2, 1, 3))
    wq = np.asarray(inp["peer_wq"], np.float32)
    out["peer_wq"] = f(wq.reshape(L, 8, 128, 16, 128).transpose(0, 3, 2, 1, 4).reshape(L, 16, 128, 1024))
    sk = np.asarray(inp["peer_subkeys"], np.float32)
    out["peer_skT"] = f(sk.transpose(0, 4, 1, 2, 3).reshape(L, 128, 16 * 128))
    for i in range(L):
        out["peer_u%d" % i] = f(np.asarray(inp["peer_u"][i], np.float32))
        out["peer_v%d" % i] = f(np.asarray(inp["peer_v"][i], np.float32))
    out["iota256"] = np.arange(256, dtype=np.float32)
    out["gate_a_w"] = f(np.asarray(inp["gate_a_w"], np.float32).transpose(0, 2, 1, 3).reshape(L, 128, 1024))
    out["gate_x_w"] = f(np.asarray(inp["gate_x_w"], np.float32).transpose(0, 2, 1, 3).reshape(L, 128, 1024))
    chp = np.zeros((L, 8, 128, 8), np.float32)
    cw = np.asarray(inp["conv_w"], np.float32)
    for k in range(4):
        chp[:, :, :, k] = cw[:, k, :].reshape(L, 8, 128)
    chp[:, :, :, 4] = np.asarray(inp["conv_b"], np.float32).reshape(L, 8, 128)
    chp[:, :, :, 5] = np.asarray(inp["gate_a_b"], np.float32).reshape(L, 8, 128)
    chp[:, :, :, 6] = np.asarray(inp["gate_x_b"], np.float32).reshape(L, 8, 128)
    chp[:, :, :, 7] = np.asarray(inp["lru_lambda"], np.float32).reshape(L, 8, 128)
    out["chp"] = f(chp.transpose(0, 2, 1, 3).reshape(L, 128, 64))
    out["lambda_qk"] = f(np.asarray(inp["lambda_qk"], np.float32).reshape(L, 256))
    for k in ("subln_g", "ln1_g", "ln1_b", "ln2_g", "ln2_b"):
        out[k] = f(inp[k])
    augk = np.zeros((3, 128), np.float32)
    augk[0] = np.arange(128)
    augk[1] = 1.0
    augk[2] = 1.0
    augq = np.zeros((3, 8, 512), np.float32)
    qq = np.arange(512)
    for h in range(8):
        ch = 8.0 * 2.0 ** (-(h + 1))
        augq[0, h] = ch
        augq[1, h] = -ch * 128.0 * (qq // 128)
        augq[2, h] = -ch * (qq % 128)
    out["aug_k"] = augk
    out["aug_q"] = f(augq.reshape(3, 8 * 512))
    return out


_CACHE = {}


def kernel(**inputs):
    shared = prep_shared(inputs)
    x = np.asarray(inputs["x"], np.float32)
    nb = x.shape[0]
    if "nc" not in _CACHE:
        _CACHE["nc"] = build_program()[0]
    nc = _CACHE["nc"]
    in_maps = []
    for b in range(nb):
        m = dict(shared)
        m["x"] = np.ascontiguousarray(x[b])
        in_maps.append(m)
    res = run_bass_kernel_spmd(nc, in_maps, core_ids=list(range(nb)))
    return np.stack([np.asarray(r["y"], np.float32) for r in res.results], axis=0)
```

```python
import math
import numpy as np
import concourse.bass as bass
import concourse.mybir as mybir
from concourse.bass_utils import run_bass_kernel_spmd
from contextlib import ExitStack

F32 = mybir.dt.float32
BF16 = mybir.dt.bfloat16
U32 = mybir.dt.uint32
I32 = mybir.dt.int32
AF = mybir.ActivationFunctionType
ALU = mybir.AluOpType
AX = mybir.AxisListType

D = 1024
S = 4096
L = 2
NTB = S // 128
ALPHA = (2.0 * L) ** 0.25
LN_EPS = 1e-5
RMS_EPS = 1e-6
NEXP = 16384


class Op:
    __slots__ = ("eng", "fn", "deps", "is_dma", "sem", "count", "signals", "waits", "idx", "barrier")


class Sched:
    def __init__(self, nc, stack, same_engine_sync=True):
        self.nc = nc
        self.stack = stack
        self.ops = []
        self.last_w = {}
        self.readers = {}
        self.same = same_engine_sync
        self.semh = {}

    def sem(self, key):
        if key not in self.semh:
            self.semh[key] = self.stack.enter_context(self.nc.semaphore("s_%d" % len(self.semh)))
        return self.semh[key]

    def add(self, eng, fn, reads=(), writes=(), dma=None):
        op = Op()
        op.eng = eng
        op.fn = fn
        op.barrier = False
        op.is_dma = dma is not None
        op.sem = ("dma", dma) if dma is not None else ("eng", eng)
        op.signals = op.is_dma
        op.count = 0
        op.idx = len(self.ops)
        deps = set()
        for r in reads:
            w = self.last_w.get(r)
            if w is not None:
                deps.add(w)
        for w_ in writes:
            w = self.last_w.get(w_)
            if w is not None:
                deps.add(w)
            for rd in self.readers.get(w_, ()):
                deps.add(rd)
        op.deps = deps
        for r in reads:
            self.readers.setdefault(r, []).append(op.idx)
        for w_ in writes:
            self.last_w[w_] = op.idx
            self.readers[w_] = []
        self.ops.append(op)
        return op

    def barrier(self, exclude=("cv",)):
        last = {}
        for op in self.ops:
            if not op.barrier and not op.is_dma:
                last[op.eng] = op
        for op in last.values():
            op.signals = True
        for eng in ("sp", "act", "dve", "pool", "pe"):
            op = Op()
            op.eng = eng
            op.fn = None
            op.barrier = True
            op.is_dma = False
            op.sem = None
            op.signals = False
            op.count = 0
            op.idx = len(self.ops)
            op.deps = set()
            op.waits = tuple(("dma", k) for k in exclude)
            self.ops.append(op)
        keep = {k: v for k, v in self.last_w.items() if self.ops[v].is_dma and self.ops[v].sem[1] in exclude}
        self.last_w = keep
        self.readers = {}

    def _skip(self, dop, op):
        return (not dop.is_dma) and dop.eng == op.eng and (not op.is_dma) and (dop.eng == "pe" or not self.same)

    def finalize(self, block):
        ops = self.ops
        for op in ops:
            for d in op.deps:
                dop = ops[d]
                if dop.is_dma or self._skip(dop, op):
                    continue
                dop.signals = True
        cnt = {}
        for op in ops:
            if op.barrier:
                op.count = dict(cnt)
                continue
            if op.signals:
                inc = 16 if op.is_dma else 1
                cnt[op.sem] = cnt.get(op.sem, 0) + inc
                op.count = cnt[op.sem]
        waited = {}
        for op in ops:
            w = waited.setdefault(op.eng, {})
            if op.barrier:
                excl = op.waits
                need = {s: v for s, v in op.count.items() if s != ("eng", op.eng) and s not in excl}
                op.waits = []
            else:
                op.waits = []
                need = {}
                for d in op.deps:
                    dop = ops[d]
                    if self._skip(dop, op):
                        continue
                    if need.get(dop.sem, 0) < dop.count:
                        need[dop.sem] = dop.count
            for s, v in need.items():
                if w.get(s, 0) < v:
                    w[s] = v
                    op.waits.append((s, v))
        final_waits = [(s, v) for s, v in cnt.items() if s[0] == "dma"]
        for s in cnt:
            self.sem(s)
        per_eng = {}
        for op in ops:
            per_eng.setdefault(op.eng, []).append(op)

        def emit(engname, eng_obj, final=False):
            for op in per_eng.get(engname, []):
                for s, v in op.waits:
                    eng_obj.wait_ge(self.sem(s), v)
                if op.fn is None:
                    continue
                ins = op.fn(eng_obj)
                if op.signals:
                    ins.then_inc(self.sem(op.sem), 16 if op.is_dma else 1)
            if final:
                for s, v in final_waits:
                    eng_obj.wait_ge(self.sem(s), v)

        @block.sync
        def _(e):
            emit("sp", e, final=True)

        @block.scalar
        def _(e):
            emit("act", e)

        @block.vector
        def _(e):
            emit("dve", e)

        @block.gpsimd
        def _(e):
            emit("pool", e)

        @block.tensor
        def _(e):
            emit("pe", e)
        return len(ops)


class Arena:
    def __init__(self, tensor, ncols):
        self.t = tensor
        self.n = ncols
        self.off = 0

    def reset(self):
        self.off = 0

    def alloc(self, shape, dt):
        n = 1
        for s_ in shape:
            n *= s_
        if dt == BF16:
            ncol = (n + 1) // 2
        else:
            ncol = n
        ncol = (ncol + 15) // 16 * 16
        assert self.off + ncol <= self.n, ("arena overflow", self.off, ncol, self.n)
        v = self.t[:, self.off:self.off + ncol]
        self.off += ncol
        if dt != F32:
            v = v.bitcast(dt)
        v = v[:, 0:n]
        if len(shape) == 2:
            v = v.rearrange("p (a b) -> p a b", a=shape[0])
        elif len(shape) == 3:
            v = v.rearrange("p (a b c) -> p a b c", a=shape[0], b=shape[1])
        return v


def build_program(n_layers=L, debug=False, phases=("A0", "A1", "A2", "A3", "B")):
    nc = bass.Bass("TRN2", target_bir_lowering=False)

    def din(name, shape, dt=F32):
        return nc.dram_tensor(name, list(shape), dt, kind="ExternalInput").ap()

    def dscr(name, shape, dt):
        return nc.dram_tensor(name, list(shape), dt, kind="ExternalOutput" if debug else "Internal").ap()

    x_d = din("x", [S, D])
    win_d = din("w_in", [L, 56, 128, 1024])
    wba_d = din("w_br_attn", [L, 128, 8, 1024])
    wbl_d = din("w_br_lru", [L, 128, 8, 1024])
    wo_d = din("w_out", [L, 128, 8, 1024])
    wq_d = din("peer_wq", [L, 16, 128, 1024])
    sk_d = din("peer_skT", [L, 128, 16 * 128])
    pu_ds = [din("peer_u%d" % i, [NEXP, D]) for i in range(L)]
    pv_ds = [din("peer_v%d" % i, [NEXP, D]) for i in range(L)]
    iota_d = din("iota256", [256])
    gaw_d = din("gate_a_w", [L, 128, 8 * 128])
    gxw_d = din("gate_x_w", [L, 128, 8 * 128])
    chp_d = din("chp", [L, 128, 64])
    lq_d = din("lambda_qk", [L, 256])
    sg_d = din("subln_g", [L, 128])
    ln1g_d = din("ln1_g", [L, D])
    ln1b_d = din("ln1_b", [L, D])
    ln2g_d = din("ln2_g", [L, D])
    ln2b_d = din("ln2_b", [L, D])
    augk_d = din("aug_k", [3, 128])
    augq_d = din("aug_q", [3, 8 * 512])
    y_d = nc.dram_tensor("y", [S, D], F32, kind="ExternalOutput").ap()

    xT_d = dscr("xT_s", [D, S], BF16)
    yaT_d = dscr("yaT_s", [D, S], BF16)
    yrT_d = dscr("yrT_s", [D, S], BF16)
    x1_d = dscr("x1_s", [S, D], F32)
    x2_d = dscr("x2_s", [S, D], F32)
    tb16 = [nc.dram_tensor("tb16_%d" % i, [NEXP, 2 * D], BF16, kind="Internal").ap() for i in range(L)]

    with ExitStack() as st:
        ARN = 49000
        arena_t = st.enter_context(nc.sbuf_tensor("arena", [128, ARN], F32))
        cst_t = st.enter_context(nc.sbuf_tensor("cst", [128, 3200], F32))
        pf = [st.enter_context(nc.psum_tensor("pf%d" % i, [128, 512], F32)) for i in range(7)]
        pb = st.enter_context(nc.psum_tensor("pbb", [128, 1024], BF16))
        block = st.enter_context(nc.Block())
        SC = Sched(nc, st)
        AR = Arena(arena_t, ARN)
        CA = Arena(cst_t, 3200)

        def DMA(out, in_, reads, writes, key, q="sp"):
            SC.add(q, lambda e: e.dma_start(out=out, in_=in_), reads, writes, dma=key)

        def MM(out, lhsT, rhs, start, stop, reads, writes):
            SC.add("pe", lambda e: e.matmul(out, lhsT=lhsT, rhs=rhs, start=start, stop=stop), reads, writes)

        def TR(out, in_, reads, writes):
            SC.add("pe", lambda e: e.transpose(out=out, in_=in_, identity=ident), list(reads) + ["ident"], writes)

        def ACT(out, in_, func, reads, writes, bias=None, scale=None, accum=None):
            kw = {}
            if bias is not None:
                kw["bias"] = bias
            if scale is not None:
                kw["scale"] = scale
            if accum is not None:
                kw["accum_out"] = accum
            SC.add("act", lambda e: e.activation(out=out, in_=in_, func=func, **kw), reads, writes)

        def CP(eng, out, in_, reads, writes):
            if eng == "act":
                SC.add("act", lambda e: e.activation(out=out, in_=in_, func=AF.Copy), reads, writes)
            else:
                SC.add(eng, lambda e: e.tensor_copy(out=out, in_=in_), reads, writes)

        def TT(eng, out, in0, in1, op, reads, writes):
            SC.add(eng, lambda e: e.tensor_tensor(out=out, in0=in0, in1=in1, op=op), reads, writes)

        def TS(eng, out, in0, s1, s2, op0, op1, reads, writes, accum=None):
            if accum is None:
                if s2 is None:
                    SC.add(eng, lambda e: e.tensor_scalar(out=out, in0=in0, scalar1=s1, scalar2=None, op0=op0), reads, writes)
                else:
                    SC.add(eng, lambda e: e.tensor_scalar(out=out, in0=in0, scalar1=s1, scalar2=s2, op0=op0, op1=op1), reads, writes)
            else:
                SC.add(eng, lambda e: e.tensor_scalar(out=out, in0=in0, scalar1=s1, scalar2=s2, op0=op0, op1=op1, accum_out=accum), reads, writes)

        def STT(out, in0, scalar, in1, op0, op1, reads, writes, accum=None):
            if accum is None:
                SC.add("dve", lambda e: e.scalar_tensor_tensor(out=out, in0=in0, scalar=scalar, in1=in1, op0=op0, op1=op1), reads, writes)
            else:
                SC.add("dve", lambda e: e.scalar_tensor_tensor(out=out, in0=in0, scalar=scalar, in1=in1, op0=op0, op1=op1, accum_out=accum), reads, writes)

        def MEMSET(eng, out, val, writes):
            SC.add(eng, lambda e: e.memset(out, val), (), writes)

        identf = CA.alloc([128], F32)
        ident = CA.alloc([128], BF16)
        trif = CA.alloc([128], F32)
        tri = CA.alloc([128], BF16)
        augk_f = CA.alloc([128], F32)
        augq_f = AR.alloc([8 * 512], F32)
        augk = CA.alloc([128], BF16)
        augq = CA.alloc([8, 512], BF16)
        MEMSET("pool", identf, 1.0, ["identf"])
        SC.add("pool", lambda e: e.affine_select(out=identf, in_=identf, pattern=[[-1, 128]], compare_op=ALU.is_equal,
                                                 fill=0.0, base=0, channel_multiplier=1), ["identf"], ["identf"])
        CP("dve", ident, identf, ["identf"], ["ident"])
        MEMSET("pool", trif, 1.0, ["trif"])
        SC.add("pool", lambda e: e.affine_select(out=trif, in_=trif, pattern=[[1, 128]], compare_op=ALU.is_ge,
                                                 fill=0.0, base=0, channel_multiplier=-1), ["trif"], ["trif"])
        CP("dve", tri, trif, ["trif"], ["tri"])
        zrow = CA.alloc([512], BF16)
        MEMSET("pool", zrow, 0.0, ["zrow"])
        iota = CA.alloc([256], F32)
        DMA(iota, iota_d.partition_broadcast(128), [], ["iota"], "c2")
        DMA(augk_f[64:67, :], augk_d, [], ["augk_f"], "c0")
        DMA(augq_f[64:67, :], augq_d, [], ["augq_f"], "c1")
        CP("dve", augk[64:67, :], augk_f[64:67, :], ["augk_f"], ["augk"])
        CP("dve", augq[64:67, :, :], augq_f[64:67, :].rearrange("p (a b) -> p a b", a=8), ["augq_f"], ["augq"])

        def conv_dma(dst, src, key):
            SC.add("pool", lambda e: e.dma_start(out=dst, in_=src), ["xT"], [key], dma="cv")

        def emit_conversion():
            for l_ in range(n_layers):
                for (src, c0) in ((pu_ds[l_], 0), (pv_ds[l_], D)):
                    for ch in range(4):
                        conv_dma(tb16[l_][ch * 4096:(ch + 1) * 4096, c0:c0 + D], src[ch * 4096:(ch + 1) * 4096, :], ("tb16", l_))

        def layernorm(y, g_bc, b_bc, out, wk, rkeys, wkeys, tagk):
            stt, mv, lnv, rstd = wk["st"], wk["mv"], wk["lnv"], wk["rstd"]
            SC.add("dve", lambda e: e.bn_stats(out=stt[:, 0:6], in_=y[:, 0:512]), rkeys, [tagk + "st"])
            SC.add("dve", lambda e: e.bn_stats(out=stt[:, 6:12], in_=y[:, 512:1024]), rkeys, [tagk + "st"])
            SC.add("dve", lambda e: e.bn_aggr(out=mv, in_=stt), [tagk + "st"], [tagk + "mv"])
            ACT(lnv, mv[:, 1:2], AF.Ln, [tagk + "mv", "eps"], [tagk + "lnv"], bias=wk["eps"])
            ACT(rstd, lnv, AF.Exp, [tagk + "lnv"], [tagk + "rstd"], scale=-0.5)
            TS("dve", y, y, mv[:, 0:1], rstd, ALU.subtract, ALU.mult, list(rkeys) + [tagk + "mv", tagk + "rstd"], wkeys_y(rkeys))
            TT("pool", y, y, g_bc, ALU.mult, list(rkeys) + ["lnp"], wkeys_y(rkeys))
            TT("pool", out, y, b_bc, ALU.add, list(rkeys) + ["lnp"], wkeys)

        def wkeys_y(rkeys):
            return list(rkeys)

        for l in range(n_layers):
            lam_init = 0.8 - 0.6 * math.exp(-0.3 * l)
            x_src = x_d if l == 0 else x2_d
            x_dst = y_d if l == n_layers - 1 else x2_d
            xsrc_key = "x2d"
            SC.barrier()
            AR.reset()
            xT = AR.alloc([8, S], BF16)
            a1_mark = AR.off
            if "A0" in phases:
                xs = [AR.alloc([D], F32) for _ in range(2)]
                xb = [AR.alloc([D], BF16) for _ in range(2)]
                for tb in range(NTB):
                    b = tb % 2
                    DMA(xs[b], x_src[tb * 128:(tb + 1) * 128, :], [xsrc_key], [("xs", b)], "xs%d" % b)
                    CP("act", xb[b], xs[b], [("xs", b)], [("xb", b)])
                    for dc in range(8):
                        TR(pb[:, dc * 128:(dc + 1) * 128], xb[b][:, dc * 128:(dc + 1) * 128], [("xb", b)], ["pb"])
                    CP("dve", xT[:, :, tb * 128:(tb + 1) * 128], pb[:, :].rearrange("p (a b) -> p a b", a=8), ["pb"], ["xT"])
                for dc in range(8):
                    DMA(xT_d[dc * 128:(dc + 1) * 128, :], xT[:, dc, :], ["xT"], ["xTd"], "xTd")
                if l == 0 and "B" in phases:
                    emit_conversion()

            if "A1" in phases:
                AR.off = a1_mark
                wst = [AR.alloc([3, 1024], F32) for _ in range(2)]
                wbf = [AR.alloc([3, 1024], BF16) for _ in range(2)]
                qT2 = AR.alloc([2, S], BF16)
                kT2 = AR.alloc([2, S], BF16)
                Va = AR.alloc([NTB, 130], BF16)
                NE = 6
                Eb = [AR.alloc([512], BF16) for _ in range(NE)]
                Osb = AR.alloc([4, 512], F32)
                obuf = AR.alloc([4, 128], F32)
                junk = AR.alloc([128], F32)
                yab = AR.alloc([4, 128], BF16)
                yst = [AR.alloc([512], BF16) for _ in range(2)]
                lq = AR.alloc([256], F32)
                sgb = AR.alloc([128], F32)
                gsc = AR.alloc([128], F32)
                sm = AR.alloc([32], F32)
                neglam = sm[:, 0:1]
                s12 = sm[:, 1:3]
                e12 = sm[:, 3:5]
                rz = sm[:, 8:16]
                rz2l = sm[:, 16:20]
                ss = sm[:, 20:24]
                lnv4 = sm[:, 24:28]
                rstd4 = sm[:, 28:32]
                epsr = AR.alloc([1], F32)
                MEMSET("pool", epsr, RMS_EPS, ["epsr"])
                MEMSET("pool", Va[:, :, 128:130], 1.0, ["Va1"])
                for c in range(2):
                    CP("pool", kT2[64:67, c, :].rearrange("p (a b) -> p a b", a=NTB),
                       augk[64:67, :].unsqueeze(1).broadcast_to([3, NTB, 128]), ["augk"], ["kT"])
                DMA(lq, lq_d[l].partition_broadcast(128), [], ["lq"], "p0")
                DMA(sgb, sg_d[l].partition_broadcast(128), [], ["sgb"], "p1")
                STT(junk[:, 0:64], lq[:, 0:64], 1.0, lq[:, 64:128], ALU.mult, ALU.mult, ["lq"], ["junk", "s1"], accum=s12[:, 0:1])
                STT(junk[:, 0:64], lq[:, 128:192], 1.0, lq[:, 192:256], ALU.mult, ALU.mult, ["lq"], ["junk", "s2"], accum=s12[:, 1:2])
                ACT(e12, s12, AF.Exp, ["s1", "s2"], ["e12"])
                TT("dve", neglam, e12[:, 1:2], e12[:, 0:1], ALU.subtract, ["e12"], ["neglam"])
                TS("dve", neglam, neglam, -lam_init, None, ALU.add, None, ["neglam"], ["neglam"])
                TS("dve", gsc, sgb, 1.0 - lam_init, None, ALU.mult, None, ["sgb"], ["gsc"])

                def load_w(h, slot):
                    for i, cb in enumerate((h, 8 + h, 16 + h)):
                        DMA(wst[slot][:, i, :], win_d[l, cb], [], [("wst", slot)], "wst%d" % slot)
                    CP("dve", wbf[slot], wst[slot], [("wst", slot)], [("wbf", slot)])

                def epilogue2(h, j):
                    SC.add("dve", lambda e: e.reciprocal(out=rz.rearrange("p (a b) -> p a b", a=4),
                                                         in_=Osb[:, :, 128:512:256]), ["Osb"], ["rz"])
                    TS("dve", rz2l, rz[:, 4:8], neglam, None, ALU.mult, None, ["rz", "neglam"], ["rz2l"])
                    for qs in range(4):
                        o1 = Osb[:, qs // 2, (qs % 2) * 256:(qs % 2) * 256 + 128]
                        o2 = Osb[:, 2 + qs // 2, (qs % 2) * 256:(qs % 2) * 256 + 128]
                        TS("dve", obuf[:, qs, :], o1, rz[:, qs:qs + 1], None, ALU.mult, None, ["Osb", "rz"], ["obuf"])
                        STT(obuf[:, qs, :], o2, rz2l[:, qs:qs + 1], obuf[:, qs, :], ALU.mult, ALU.add, ["Osb", "rz2l", "obuf"], ["obuf"])
                        STT(junk, obuf[:, qs, :], 1.0, obuf[:, qs, :], ALU.mult, ALU.mult, ["obuf"], ["junk", "ss"], accum=ss[:, qs:qs + 1])
                    ACT(lnv4, ss, AF.Ln, ["ss"], ["lnv4"], scale=1.0 / 128.0, bias=epsr)
                    ACT(rstd4, lnv4, AF.Exp, ["lnv4"], ["rstd4"], scale=-0.5)
                    for qs in range(4):
                        STT(yab[:, qs, :], obuf[:, qs, :], rstd4[:, qs:qs + 1], gsc, ALU.mult, ALU.mult, ["obuf", "rstd4", "gsc"], ["yab"])
                    for qs in range(4):
                        TR(pb[:, qs * 128:(qs + 1) * 128], yab[:, qs, :], ["yab"], ["pb"])
                    ys = yst[j % 2]
                    CP("dve", ys, pb[:, 0:512], ["pb"], [("yst", j % 2)])
                    DMA(yaT_d[h * 128:(h + 1) * 128, j * 512:(j + 1) * 512], ys, [("yst", j % 2)], ["yaTd"], "yst%d" % (j % 2))

                load_w(0, 0)
                ei = 0
                si = 0
                for h in range(8):
                    slot = h % 2
                    if h + 1 < 8:
                        load_w(h + 1, 1 - slot)
                    slope = 2.0 ** (-(h + 1))
                    for c in range(2):
                        CP("dve", qT2[64:67, c, :].rearrange("p (a b) -> p a b", a=8),
                           augq[64:67, h, :].unsqueeze(1).broadcast_to([3, 8, 512]), ["augq"], ["qT"])
                    for tq in range(8):
                        for (wi, dst, key) in ((0, qT2, "qT"), (1, kT2, "kT")):
                            for dc in range(8):
                                MM(pf[6][:, :], wbf[slot][:, wi, dc * 128:(dc + 1) * 128], xT[:, dc, tq * 512:(tq + 1) * 512],
                                   dc == 0, dc == 7, [("wbf", slot), "xT"], ["pf6"])
                            CP("dve", dst[0:64, 0, tq * 512:(tq + 1) * 512], pf[6][0:64, :], ["pf6"], [key])
                            CP("dve", dst[0:64, 1, tq * 512:(tq + 1) * 512], pf[6][64:128, :], ["pf6"], [key])
                    for tb4 in range(8):
                        for t in range(4):
                            tb = tb4 * 4 + t
                            for dc in range(8):
                                MM(pf[6][:, t * 128:(t + 1) * 128], xT[:, dc, tb * 128:(tb + 1) * 128],
                                   wbf[slot][:, 2, dc * 128:(dc + 1) * 128], dc == 0, dc == 7, [("wbf", slot), "xT"], ["pf6"])
                        CP("dve", Va[:, tb4 * 4:(tb4 + 1) * 4, 0:128], pf[6][:, :].rearrange("p (a b) -> p a b", a=4), ["pf6"], ["Va"])
                    tiles = [(j, c, kb) for j in range(8) for c in range(2) for kb in range(4 * j + 4)]
                    sinfo = {}

                    def emit_S(i):
                        nonlocal si
                        j, c, kb = tiles[i]
                        r = kb - 4 * j
                        nq0 = max(0, r) * 128
                        sb_ = pf[4 + si % 2]
                        skey = ("S", si % 2)
                        si += 1
                        MM(sb_[:, nq0:512], kT2[0:67, c, kb * 128:(kb + 1) * 128],
                           qT2[0:67, c, j * 512 + nq0:(j + 1) * 512], True, True, ["qT", "kT"], [skey])
                        sinfo[i] = (sb_, skey)

                    pending = []

                    def emit_rest(i):
                        nonlocal ei
                        j, c, kb = tiles[i]
                        r = kb - 4 * j
                        nq0 = max(0, r) * 128
                        sb_, skey = sinfo.pop(i)
                        if c == 0 and kb == 0:
                            for bnk in range(4):
                                MM(pf[bnk][:, :], zrow[0:1, 0:128], zrow[0:1, 0:512], True, False, ["zrow"], [("O", bnk)])
                        E = Eb[ei % NE]
                        ekey = ("E", ei % NE)
                        ei += 1
                        ACT(E[:, nq0:512], sb_[:, nq0:512], AF.Exp, [skey], [ekey], scale=0.125,
                            bias=float(slope * (kb * 128 - j * 512)))
                        if r >= 0:
                            TT("dve", E[:, r * 128:(r + 1) * 128], E[:, r * 128:(r + 1) * 128], tri, ALU.mult, [ekey, "tri"], [ekey])
                        for qs in range(max(0, r), 4):
                            ob = pf[c * 2 + qs // 2]
                            MM(ob[:, (qs % 2) * 256:(qs % 2) * 256 + 129], E[:, qs * 128:(qs + 1) * 128], Va[:, kb, 0:129],
                               False, kb == 4 * j + qs, [ekey, "Va", "Va1"], [("O", c * 2 + qs // 2)])
                        if c == 1 and kb == 4 * j + 3:
                            for bnk in range(4):
                                CP("dve", Osb[:, bnk, :], pf[bnk][:, :], [("O", bnk)], ["Osb"])
                            pending.append((i + 5, h, j))
                        while pending and (pending[0][0] <= i or i == len(tiles) - 1):
                            _, hh, jj = pending.pop(0)
                            epilogue2(hh, jj)

                    emit_S(0)
                    for i in range(len(tiles)):
                        if i + 1 < len(tiles):
                            emit_S(i + 1)
                        emit_rest(i)

            if "A2" in phases:
                SC.barrier()
                AR.off = a1_mark
                B0 = AR.alloc([S + 16], F32)
                B1 = AR.alloc([S], F32)
                B2 = AR.alloc([S], F32)
                B3 = AR.alloc([S], F32)
                xcb = AR.alloc([S], BF16)
                Yb = AR.alloc([S], BF16)
                wst2 = [AR.alloc([2, 1024], F32) for _ in range(2)]
                wbf2 = [AR.alloc([2, 1024], BF16) for _ in range(2)]
                gwf = AR.alloc([2, 1024], F32)
                gwb = AR.alloc([2, 8, 128], BF16)
                chp = AR.alloc([8, 8], F32)
                cc = AR.alloc([8, 4], F32)
                DMA(gwf[:, 0, :], gaw_d[l], [], ["gwf"], "p2")
                DMA(gwf[:, 1, :], gxw_d[l], [], ["gwf"], "p2")
                CP("dve", gwb, gwf.rearrange("p a (g j) -> p a g j", g=8), ["gwf"], ["gwb"])
                DMA(chp, chp_d[l].rearrange("p (g f) -> p g f", g=8), [], ["chp"], "p3")
                ACT(cc[:, :, 2], chp[:, :, 7], AF.Exp, ["chp"], ["cc"], scale=-1.0)
                ACT(cc[:, :, 3], cc[:, :, 2], AF.Ln, ["cc"], ["cc"], bias=1.0)
                TS("dve", cc[:, :, 0], cc[:, :, 3], -8.0, None, ALU.mult, None, ["cc"], ["cc"])
                TS("dve", cc[:, :, 1], cc[:, :, 3], -16.0, None, ALU.mult, None, ["cc"], ["cc"])

                def load_w2(g, slot):
                    DMA(wst2[slot][:, 0, :], win_d[l, 24 + g], [], [("wst2", slot)], "wst2%d" % slot)
                    DMA(wst2[slot][:, 1, :], win_d[l, 32 + g], [], [("wst2", slot)], "wst2%d" % slot)
                    CP("pool", wbf2[slot], wst2[slot], [("wst2", slot)], [("wbf2", slot)])

                load_w2(0, 0)
                pi = 0
                for g in range(8):
                    slot = g % 2
                    if g + 1 < 8:
                        load_w2(g + 1, 1 - slot)
                    MEMSET("pool", B0[:, 0:3], 0.0, ["B0"])
                    for tq in range(8):
                        pp = pf[pi % 7]
                        pk = ("pf", pi % 7)
                        pi += 1
                        for dc in range(8):
                            MM(pp[:, :], wbf2[slot][:, 0, dc * 128:(dc + 1) * 128], xT[:, dc, tq * 512:(tq + 1) * 512], dc == 0, dc == 7,
                               [("wbf2", slot), "xT"], [pk])
                        CP("act", B0[:, 3 + tq * 512:3 + (tq + 1) * 512], pp[:, :], [pk], ["B0"])
                    TS("dve", B1, B0[:, 0:S], chp[:, g, 0:1], chp[:, g, 4:5], ALU.mult, ALU.add, ["B0", "chp"], ["B1"])
                    for k in range(1, 4):
                        STT(B1, B0[:, k:k + S], chp[:, g, k:k + 1], B1, ALU.mult, ALU.add, ["B0", "chp", "B1"], ["B1"])
                    CP("pool", xcb, B1, ["B1"], ["xcb"])
                    for (wi, dst, off, key, bcol) in ((0, B2, 0, "B2", 5), (1, B0, 3, "B0", 6)):
                        for tq in range(8):
                            pp = pf[pi % 7]
                            pk = ("pf", pi % 7)
                            pi += 1
                            MM(pp[:, :], gwb[:, wi, g, :], xcb[:, tq * 512:(tq + 1) * 512], True, True, ["gwb", "xcb"], [pk])
                            ACT(dst[:, off + tq * 512:off + (tq + 1) * 512], pp[:, :], AF.Sigmoid, [pk, "chp", "B1"], [key],
                                bias=chp[:, g, bcol:bcol + 1])
                    ACT(B3, B2, AF.Exp, ["B2", "cc"], ["B3"], scale=cc[:, g, 1:2])
                    ACT(B3, B3, AF.Sqrt, ["B3"], ["B3"], scale=-1.0, bias=1.0)
                    MEMSET("pool", B3[:, 0:1], 1.0, ["B3"])
                    ACT(B2, B2, AF.Exp, ["B2", "cc"], ["B2"], scale=cc[:, g, 0:1])
                    TT("dve", B0[:, 3:3 + S], B0[:, 3:3 + S], B1, ALU.mult, ["B0", "B1"], ["B0"])
                    TT("dve", B0[:, 3:3 + S], B0[:, 3:3 + S], B3, ALU.mult, ["B0", "B3"], ["B0"])
                    SC.add("dve", lambda e: e.tensor_tensor_scan(out=B3, data0=B2, data1=B0[:, 3:3 + S], initial=0.0,
                                                                 op0=ALU.mult, op1=ALU.add), ["B2", "B0", "B3"], ["B3"])
                    for tq in range(8):
                        pp = pf[pi % 7]
                        pk = ("pf", pi % 7)
                        pi += 1
                        for dc in range(8):
                            MM(pp[:, :], wbf2[slot][:, 1, dc * 128:(dc + 1) * 128], xT[:, dc, tq * 512:(tq + 1) * 512], dc == 0, dc == 7,
                               [("wbf2", slot), "xT"], [pk])
                        ACT(B1[:, tq * 512:(tq + 1) * 512], pp[:, :], AF.Gelu_apprx_tanh, [pk, "B0"], ["B1"])
                    TT("dve", Yb, B3, B1, ALU.mult, ["B3", "B1", "yrTd"], ["Yb"])
                    DMA(yrT_d[g * 128:(g + 1) * 128, :], Yb, ["Yb"], ["yrTd"], "yrTd")

            if "A3" in phases:
                SC.barrier()
                AR.reset()
                wg = AR.alloc([16, 8, 128], BF16)
                wba = AR.alloc([8, 1024], BF16)
                wbl = AR.alloc([8, 1024], BF16)
                wo = AR.alloc([8, 1024], BF16)
                stg = [AR.alloc([1024], F32) for _ in range(2)]
                lng = AR.alloc([D], F32)
                lnb = AR.alloc([D], F32)
                DMA(lng, ln1g_d[l].partition_broadcast(128), [], ["lnp"], "p4")
                DMA(lnb, ln1b_d[l].partition_broadcast(128), [], ["lnp"], "p4")
                si_ = 0
                for cb in range(16):
                    b = si_ % 2
                    si_ += 1
                    DMA(stg[b], win_d[l, 40 + cb], [], [("stg", b)], "stg%d" % b)
                    CP("dve" if cb % 2 else "pool", wg[:, cb, :, :], stg[b].rearrange("p (a b) -> p a b", a=8), [("stg", b)], ["wg"])
                for (src, dst, key) in ((wba_d, wba, "wba"), (wbl_d, wbl, "wbl"), (wo_d, wo, "wo")):
                    for kc in range(8):
                        b = si_ % 2
                        si_ += 1
                        DMA(stg[b], src[l, :, kc, :], [], [("stg", b)], "stg%d" % b)
                        CP("dve" if kc % 2 else "pool", dst[:, kc, :], stg[b], [("stg", b)], [key])
                xTb = [AR.alloc([8, 512], BF16) for _ in range(2)]
                yaTb = [AR.alloc([8, 512], BF16) for _ in range(2)]
                yrTb = [AR.alloc([8, 512], BF16) for _ in range(2)]
                mT = AR.alloc([8, 512], BF16)
                sga = [AR.alloc([512], F32) for _ in range(2)]
                sgr = [AR.alloc([512], F32) for _ in range(2)]
                xres = [AR.alloc([D], F32) for _ in range(2)]
                yln = [AR.alloc([D], F32) for _ in range(2)]
                wk = {"st": AR.alloc([12], F32), "mv": AR.alloc([2], F32), "lnv": AR.alloc([1], F32),
                      "rstd": AR.alloc([1], F32), "eps": AR.alloc([1], F32)}
                MEMSET("pool", wk["eps"], LN_EPS, ["eps"])
                ti = 0
                for tq in range(8):
                    b = tq % 2
                    DMA(xTb[b], xT_d[:, tq * 512:(tq + 1) * 512].rearrange("(a p) t -> p a t", p=128), ["xTd"], [("xTb", b)], "xTb%d" % b)
                    DMA(yaTb[b], yaT_d[:, tq * 512:(tq + 1) * 512].rearrange("(a p) t -> p a t", p=128), ["yaTd"], [("yaTb", b)], "yaTb%d" % b)
                    DMA(yrTb[b], yrT_d[:, tq * 512:(tq + 1) * 512].rearrange("(a p) t -> p a t", p=128), ["yrTd"], [("yrTb", b)], "yrTb%d" % b)
                    for jb in range(8):
                        for dc in range(8):
                            MM(pf[0][:, :], wg[:, jb, dc, :], xTb[b][:, dc, :], dc == 0, dc == 7, ["wg", ("xTb", b)], ["pf0"])
                        for dc in range(8):
                            MM(pf[1][:, :], wg[:, 8 + jb, dc, :], xTb[b][:, dc, :], dc == 0, dc == 7, ["wg", ("xTb", b)], ["pf1"])
                        for kc in range(8):
                            MM(pf[2][:, :], wba[:, kc, jb * 128:(jb + 1) * 128], yaTb[b][:, kc, :], kc == 0, kc == 7, ["wba", ("yaTb", b)], ["pf2"])
                        for kc in range(8):
                            MM(pf[3][:, :], wbl[:, kc, jb * 128:(jb + 1) * 128], yrTb[b][:, kc, :], kc == 0, kc == 7, ["wbl", ("yrTb", b)], ["pf3"])
                        sb2 = jb % 2
                        ACT(sga[sb2], pf[0][:, :], AF.Sigmoid, ["pf0"], [("sga", sb2)])
                        ACT(sgr[sb2], pf[1][:, :], AF.Sigmoid, ["pf1"], [("sgr", sb2)])
                        TT("dve", sga[sb2], sga[sb2], pf[2][:, :], ALU.mult, [("sga", sb2), "pf2"], [("sga", sb2)])
                        TT("dve", sgr[sb2], sgr[sb2], pf[3][:, :], ALU.mult, [("sgr", sb2), "pf3"], [("sgr", sb2)])
                        TT("pool", mT[:, jb, :], sga[sb2], sgr[sb2], ALU.add, [("sga", sb2), ("sgr", sb2)], ["mT"])
                    for ts_ in range(4):
                        tb = tq * 4 + ts_
                        xb_ = ti % 2
                        ti += 1
                        DMA(xres[xb_], x_src[tb * 128:(tb + 1) * 128, :], [xsrc_key], [("xres", xb_)], "xres%d" % xb_)
                        for nh in range(2):
                            for jb in range(8):
                                MM(pf[4 + nh][:, :], mT[:, jb, ts_ * 128:(ts_ + 1) * 128], wo[:, jb, nh * 512:(nh + 1) * 512], jb == 0, jb == 7,
                                   ["mT", "wo"], [("pf", 4 + nh)])
                            STT(yln[xb_][:, nh * 512:(nh + 1) * 512], xres[xb_][:, nh * 512:(nh + 1) * 512], ALPHA, pf[4 + nh][:, :],
                                ALU.mult, ALU.add, [("xres", xb_), ("pf", 4 + nh)], [("yln", xb_)])
                        layernorm(yln[xb_], lng, lnb, yln[xb_], wk, [("yln", xb_)], [("yln", xb_)], "ln")
                        DMA(x1_d[tb * 128:(tb + 1) * 128, :], yln[xb_], [("yln", xb_)], ["x1d"], "x1st%d" % xb_)

            if "B" in phases:
                SC.barrier()
                AR.reset()
                wq = AR.alloc([16, 8, 128], BF16)
                skT = AR.alloc([16, 128], BF16)
                lng = AR.alloc([D], F32)
                lnb = AR.alloc([D], F32)
                bmark = AR.off
                stg = [AR.alloc([2048], F32) for _ in range(2)]
                DMA(lng, ln2g_d[l].partition_broadcast(128), [], ["lnp"], "p4")
                DMA(lnb, ln2b_d[l].partition_broadcast(128), [], ["lnp"], "p4")
                for cb in range(16):
                    b = cb % 2
                    DMA(stg[b][:, 0:1024], wq_d[l, cb], [], [("stg", b)], "stg%d" % b)
                    CP("dve" if cb % 2 else "pool", wq[:, cb, :, :], stg[b][:, 0:1024].rearrange("p (a b) -> p a b", a=8), [("stg", b)], ["wq"])
                DMA(stg[0], sk_d[l], [], [("stg", 0)], "stg0")
                CP("dve", skT, stg[0].rearrange("p (a b) -> p a b", a=16), [("stg", 0)], ["skT"])
                SC.barrier()
                AR.off = bmark
                x1 = [AR.alloc([D], F32) for _ in range(2)]
                x1b = [AR.alloc([D], BF16) for _ in range(2)]
                x1T = AR.alloc([8, 128], BF16)
                qTs = AR.alloc([16, 128], BF16)
                scs = AR.alloc([16, 128], F32)
                top = AR.alloc([16, 16], F32)
                tix = AR.alloc([16, 16], U32)
                tixf = AR.alloc([16, 16], F32)
                work = AR.alloc([256], F32)
                cand = AR.alloc([8, 256], F32)
                eid = AR.alloc([8, 256], F32)
                tsv = AR.alloc([8, 16], F32)
                pos = AR.alloc([8, 16], U32)
                posf = AR.alloc([8, 16], F32)
                ef = AR.alloc([128], F32)
                af_ = AR.alloc([128], F32)
                bf_ = AR.alloc([128], F32)
                e1_ = AR.alloc([128], F32)
                e2_ = AR.alloc([128], F32)
                eidx = [AR.alloc([128], I32) for _ in range(2)]
                gt = [AR.alloc([8, 16], F32) for _ in range(2)]
                dsm = AR.alloc([8, 16], F32)
                zs = AR.alloc([8], F32)
                actv = AR.alloc([128], F32)
                wgt = AR.alloc([128], F32)
                acc = AR.alloc([D], F32)
                junkb = AR.alloc([D], BF16)
                ND = 8
                diag = [AR.alloc([128], BF16) for _ in range(ND)]
                wk = {"st": AR.alloc([12], F32), "mv": AR.alloc([2], F32), "lnv": AR.alloc([1], F32),
                      "rstd": AR.alloc([1], F32), "eps": AR.alloc([1], F32)}
                NG = 4
                LOOK = 4
                NRB = (LOOK + 1) * NG
                Rb = [AR.alloc([2 * D], BF16) for _ in range(NRB)]
                MEMSET("pool", wk["eps"], LN_EPS, ["eps"])
                tbl = tb16[l]
                tkey = ("tb16", l)
                pacc = [pf[5], pf[6]]

                def routing(tb):
                    b = tb % 2
                    DMA(x1[b], x1_d[tb * 128:(tb + 1) * 128, :], ["x1d"], [("x1", b)], "x1ld%d" % b)
                    CP("act", x1b[b], x1[b], [("x1", b)], [("x1b", b)])
                    for dc in range(8):
                        TR(pb[:, dc * 128:(dc + 1) * 128], x1b[b][:, dc * 128:(dc + 1) * 128], [("x1b", b)], ["pb"])
                    CP("act", x1T, pb[:, :].rearrange("p (a b) -> p a b", a=8), ["pb"], ["x1T"])
                    for c4 in range(4):
                        pp = pf[0]
                        pk = ("pf", 0)
                        for ci in range(4):
                            cb = c4 * 4 + ci
                            for dc in range(8):
                                MM(pp[:, ci * 128:(ci + 1) * 128], wq[:, cb, dc, :], x1T[:, dc, :], dc == 0, dc == 7, ["wq", "x1T"], [pk])
                        CP("act", qTs[:, c4 * 4:(c4 + 1) * 4, :], pp[:, :].rearrange("p (a b) -> p a b", a=4), [pk], ["qTs"])
                    for c4 in range(4):
                        pp = pf[1 + c4]
                        pk = ("pf", 1 + c4)
                        for ci in range(4):
                            cb = c4 * 4 + ci
                            MM(pp[:, ci * 128:(ci + 1) * 128], qTs[:, cb, :], skT[:, cb, :], True, True, ["qTs", "skT"], [pk])
                        CP("act", scs[:, c4 * 4:(c4 + 1) * 4, :], pp[:, :].rearrange("p (a b) -> p a b", a=4), [pk], ["scs"])
                    for g in range(16):
                        SC.add("dve", lambda e, g=g: e.max(out=top[:, g, 0:8], in_=scs[:, g, :]), ["scs"], ["top"])
                        SC.add("dve", lambda e, g=g: e.max_index(out=tix[:, g, 0:8], in_max=top[:, g, 0:8], in_values=scs[:, g, :]), ["scs", "top"], ["tix"])
                        SC.add("dve", lambda e, g=g: e.match_replace(out=work[:, 0:128], in_to_replace=top[:, g, 0:8], in_values=scs[:, g, :],
                                                                     imm_value=-1e30), ["scs", "top"], ["work"])
                        SC.add("dve", lambda e, g=g: e.max(out=top[:, g, 8:16], in_=work[:, 0:128]), ["work"], ["top"])
                        SC.add("dve", lambda e, g=g: e.max_index(out=tix[:, g, 8:16], in_max=top[:, g, 8:16], in_values=work[:, 0:128]), ["work", "top"], ["tix"])
                    CP("dve", tixf, tix, ["tix"], ["tixf"])
                    top4 = top.rearrange("p (h c) k -> p h c k", c=2)
                    tix4 = tixf.rearrange("p (h c) k -> p h c k", c=2)
                    cand4 = cand.rearrange("p h (a b) -> p h a b", a=16)
                    eid4 = eid.rearrange("p h (a b) -> p h a b", a=16)
                    TT("dve", cand4, top4[:, :, 0, :].unsqueeze(3).broadcast_to([128, 8, 16, 16]),
                       top4[:, :, 1, :].unsqueeze(2).broadcast_to([128, 8, 16, 16]), ALU.add, ["top"], ["cand"])
                    TS("dve", tix4[:, :, 0, :], tix4[:, :, 0, :], 128.0, None, ALU.mult, None, ["tixf"], ["tixf"])
                    for h in range(8):
                        SC.add("dve", lambda e, h=h: e.max(out=tsv[:, h, 0:8], in_=cand[:, h, :]), ["cand"], ["tsv"])
                        SC.add("dve", lambda e, h=h: e.max_index(out=pos[:, h, 0:8], in_max=tsv[:, h, 0:8], in_values=cand[:, h, :]), ["cand", "tsv"], ["pos"])
                        SC.add("dve", lambda e, h=h: e.match_replace(out=work, in_to_replace=tsv[:, h, 0:8], in_values=cand[:, h, :],
                                                                     imm_value=-1e30), ["cand", "tsv"], ["work"])
                        SC.add("dve", lambda e, h=h: e.max(out=tsv[:, h, 8:16], in_=work), ["work"], ["tsv"])
                        SC.add("dve", lambda e, h=h: e.max_index(out=pos[:, h, 8:16], in_max=tsv[:, h, 8:16], in_values=work), ["work", "tsv"], ["pos"])
                    CP("dve", posf, pos, ["pos"], ["posf"])
                    posflat = posf.rearrange("p h k -> p (h k)")
                    ge3 = cand.rearrange("p h x -> p (h x)")[:, 0:128 * 15].rearrange("p (s m) -> p s m", m=15)
                    TT("dve", ge3, posflat.unsqueeze(2).broadcast_to([128, 128, 15]),
                       iota[:, 16:256:16].unsqueeze(1).broadcast_to([128, 128, 15]), ALU.is_ge, ["posf", "iota", "cand"], ["cand"])
                    SC.add("dve", lambda e: e.tensor_reduce(out=af_, in_=ge3, axis=AX.X, op=ALU.add), ["cand"], ["af"])
                    STT(bf_, af_, -16.0, posflat, ALU.mult, ALU.add, ["af", "posf"], ["bf"])
                    io16 = iota[:, 0:16].unsqueeze(1).unsqueeze(1).broadcast_to([128, 8, 16, 16])
                    for (src_, lst, dst_, key) in ((af_, 0, e1_, "e1"), (bf_, 1, e2_, "e2")):
                        TT("dve", eid4, src_.rearrange("p (h k) -> p h k", h=8).unsqueeze(3).broadcast_to([128, 8, 16, 16]), io16,
                           ALU.is_equal, ["af", "bf", "iota", "eid"], ["eid"])
                        TT("dve", eid4, eid4, tix4[:, :, lst, :].unsqueeze(2).broadcast_to([128, 8, 16, 16]), ALU.mult, ["eid", "tixf"], ["eid"])
                        SC.add("dve", lambda e, dst_=dst_: e.tensor_reduce(out=dst_, in_=eid.rearrange("p h (k a) -> p (h k) a", a=16),
                                                                           axis=AX.X, op=ALU.add), ["eid"], [key])
                    TT("dve", ef, e1_, e2_, ALU.add, ["e1", "e2"], ["ef"])
                    CP("dve", eidx[b], ef, ["ef"], [("eidx", b)])
                    TT("dve", dsm, tsv, tsv[:, :, 0:1].broadcast_to([128, 8, 16]), ALU.subtract, ["tsv"], ["dsm"])
                    ACT(dsm, dsm, AF.Exp, ["dsm"], ["dsm"])
                    SC.add("dve", lambda e: e.tensor_reduce(out=zs, in_=dsm, axis=AX.X, op=ALU.add), ["dsm"], ["zs"])
                    SC.add("dve", lambda e: e.reciprocal(out=zs, in_=zs), ["zs"], ["zs"])
                    TT("dve", gt[b], dsm, zs.unsqueeze(2).broadcast_to([128, 8, 16]), ALU.mult, ["dsm", "zs"], [("gt", b)])

                NGR = 128 // NG
                gi = [0]
                di = [0]
                slotbuf = {}
                issued = set()

                def gathers(tb, g):
                    if (tb, g) in issued:
                        return
                    issued.add((tb, g))
                    b = tb % 2
                    for i in range(NG):
                        s_ = g * NG + i
                        rb = gi[0] % NRB
                        gi[0] += 1
                        slotbuf[(tb, s_)] = rb
                        SC.add("pool", lambda e, s_=s_, rb=rb, b=b, tbl=tbl: e.indirect_dma_start(
                            out=Rb[rb], out_offset=None, in_=tbl, in_offset=bass.IndirectOffsetOnAxis(ap=eidx[b][:, s_:s_ + 1], axis=0)),
                            [("eidx", b), tkey], [("R", rb)], dma="g%d" % rb)

                def evaluate(tb):
                    b = tb % 2
                    gtf = gt[b].rearrange("p h k -> p (h k)")
                    for g in range(LOOK):
                        gathers(tb, g)
                    for g in range(NGR):
                        if g + LOOK < NGR:
                            gathers(tb, g + LOOK)
                        elif tb + 1 < NTB:
                            gathers(tb + 1, g + LOOK - NGR)
                        sl = slice(g * NG, (g + 1) * NG)
                        for i in range(NG):
                            s_ = g * NG + i
                            rb = slotbuf[(tb, s_)]
                            STT(junkb, Rb[rb][:, 0:D], 1.0, x1b[b], ALU.mult, ALU.mult, [("R", rb), ("x1b", b)], [("actv", s_)],
                                accum=actv[:, s_:s_ + 1])
                        ACT(wgt[:, sl], actv[:, sl], AF.Gelu_apprx_tanh, [("actv", g * NG + i) for i in range(NG)], [("wgt", g)])
                        TT("dve", wgt[:, sl], wgt[:, sl], gtf[:, sl], ALU.mult, [("wgt", g), ("gt", b)], [("wgt", g)])
                        for i in range(NG):
                            s_ = g * NG + i
                            rb = slotbuf.pop((tb, s_))
                            dk = di[0] % ND
                            di[0] += 1
                            ACT(diag[dk], ident, AF.Copy, [("wgt", g), "ident"], [("diag", dk)], scale=wgt[:, s_:s_ + 1])
                            for half in range(2):
                                MM(pacc[half][:, :], diag[dk], Rb[rb][:, D + half * 512:D + (half + 1) * 512], s_ == 0, s_ == 127,
                                   [("diag", dk), ("R", rb)], [("pacc", half)])
                    for half in range(2):
                        STT(acc[:, half * 512:(half + 1) * 512], x1[b][:, half * 512:(half + 1) * 512], ALPHA, pacc[half][:, :],
                            ALU.mult, ALU.add, [("x1", b), ("pacc", half)], ["acc"])
                    layernorm(acc, lng, lnb, acc, wk, ["acc"], ["acc"], "ln")
                    DMA(x_dst[tb * 128:(tb + 1) * 128, :], acc, ["acc"], ["x2d"], "x2st")

                routing(0)
                for tb in range(NTB):
                    if tb + 1 < NTB:
                        routing(tb + 1)
                    evaluate(tb)

        n = SC.finalize(block)
    return nc, n


def prep_shared(inp):
    f = lambda a: np.ascontiguousarray(np.asarray(a, dtype=np.float32))
    w_in = np.asarray(inp["w_in"], np.float32)
    out = {}
    out["w_in"] = f(w_in.reshape(L, 8, 128, 56, 128).transpose(0, 3, 2, 1, 4).reshape(L, 56, 128, 1024))
    for k in ("w_br_attn", "w_br_lru", "w_out"):
        out[k] = f(np.asarray(inp[k], np.float32).reshape(L, 8, 128, 1024).transpose(0, 2, 1, 3))
    wq = np.asarray(inp["peer_wq"], np.float32)
    out["peer_wq"] = f(wq.reshape(L, 8, 128, 16, 128).transpose(0, 3, 2, 1, 4).reshape(L, 16, 128, 1024))
    sk = np.asarray(inp["peer_subkeys"], np.float32)
    out["peer_skT"] = f(sk.transpose(0, 4, 1, 2, 3).reshape(L, 128, 16 * 128))
    for i in range(L):
        out["peer_u%d" % i] = f(np.asarray(inp["peer_u"][i], np.float32))
        out["peer_v%d" % i] = f(np.asarray(inp["peer_v"][i], np.float32))
    out["iota256"] = np.arange(256, dtype=np.float32)
    out["gate_a_w"] = f(np.asarray(inp["gate_a_w"], np.float32).transpose(0, 2, 1, 3).reshape(L, 128, 1024))
    out["gate_x_w"] = f(np.asarray(inp["gate_x_w"], np.float32).transpose(0, 2, 1, 3).reshape(L, 128, 1024))
    chp = np.zeros((L, 8, 128, 8), np.float32)
    cw = np.asarray(inp["conv_w"], np.float32)
    for k in range(4):
        chp[:, :, :, k] = cw[:, k, :].reshape(L, 8, 128)
    chp[:, :, :, 4] = np.asarray(inp["conv_b"], np.float32).reshape(L, 8, 128)
    chp[:, :, :, 5] = np.asarray(inp["gate_a_b"], np.float32).reshape(L, 8, 128)
    chp[:, :, :, 6] = np.asarray(inp["gate_x_b"], np.float32).reshape(L, 8, 128)
    chp[:, :, :, 7] = np.asarray(inp["lru_lambda"], np.float32).reshape(L, 8, 128)
    out["chp"] = f(chp.transpose(0, 2, 1, 3).reshape(L, 128, 64))
    out["lambda_qk"] = f(np.asarray(inp["lambda_qk"], np.float32).reshape(L, 256))
    for k in ("subln_g", "ln1_g", "ln1_b", "ln2_g", "ln2_b"):
        out[k] = f(inp[k])
    augk = np.zeros((3, 128), np.float32)
    augk[0] = np.arange(128)
    augk[1] = 1.0
    augk[2] = 1.0
    augq = np.zeros((3, 8, 512), np.float32)
    qq = np.arange(512)
    for h in range(8):
        ch = 8.0 * 2.0 ** (-(h + 1))
        augq[0, h] = ch
        augq[1, h] = -ch * 128.0 * (qq // 128)
        augq[2, h] = -ch * (qq % 128)
    out["aug_k"] = augk
    out["aug_q"] = f(augq.reshape(3, 8 * 512))
    return out


_CACHE = {}


def kernel(**inputs):
    shared = prep_shared(inputs)
    x = np.asarray(inputs["x"], np.float32)
    nb = x.shape[0]
    if "nc" not in _CACHE:
        _CACHE["nc"] = build_program()[0]
    nc = _CACHE["nc"]
    in_maps = []
    for b in range(nb):
        m = dict(shared)
        m["x"] = np.ascontiguousarray(x[b])
        in_maps.append(m)
    res = run_bass_kernel_spmd(nc, in_maps, core_ids=list(range(nb)))
    return np.stack([np.asarray(r["y"], np.float32) for r in res.results], axis=0)
```

```python
import math
import numpy as np
import concourse.bass as bass
import concourse.mybir as mybir
from concourse.bass_utils import run_bass_kernel_spmd
from contextlib import ExitStack

F32 = mybir.dt.float32
BF16 = mybir.dt.bfloat16
U32 = mybir.dt.uint32
I32 = mybir.dt.int32
AF = mybir.ActivationFunctionType
ALU = mybir.AluOpType
AX = mybir.AxisListType

D = 1024
S = 4096
L = 2
NTB = S // 128
ALPHA = (2.0 * L) ** 0.25
LN_EPS = 1e-5
RMS_EPS = 1e-6
NEXP = 16384
SAME_SYNC = True
SKIP_ARG = 130.0


class Op:
    __slots__ = ("eng", "fn", "deps", "is_dma", "sem", "count", "signals", "waits", "idx", "barrier")


class Sched:
    def __init__(self, nc, stack, same_engine_sync=True):
        self.nc = nc
        self.stack = stack
        self.ops = []
        self.last_w = {}
        self.readers = {}
        self.same = same_engine_sync
        self.semh = {}

    def sem(self, key):
        if key not in self.semh:
            self.semh[key] = self.stack.enter_context(self.nc.semaphore("s_%d" % len(self.semh)))
        return self.semh[key]

    def add(self, eng, fn, reads=(), writes=(), dma=None):
        op = Op()
        op.eng = eng
        op.fn = fn
        op.barrier = False
        op.is_dma = dma is not None
        op.sem = ("dma", dma) if dma is not None else ("eng", eng)
        op.signals = op.is_dma
        op.count = 0
        op.idx = len(self.ops)
        deps = set()
        for r in reads:
            w = self.last_w.get(r)
            if w is not None:
                deps.add(w)
        for w_ in writes:
            w = self.last_w.get(w_)
            if w is not None:
                deps.add(w)
            for rd in self.readers.get(w_, ()):
                deps.add(rd)
        op.deps = deps
        for r in reads:
            self.readers.setdefault(r, []).append(op.idx)
        for w_ in writes:
            self.last_w[w_] = op.idx
            self.readers[w_] = []
        self.ops.append(op)
        return op

    def barrier(self, exclude=("cv",)):
        last = {}
        for op in self.ops:
            if not op.barrier and not op.is_dma:
                last[op.eng] = op
        for op in last.values():
            op.signals = True
        for eng in ("sp", "act", "dve", "pool", "pe"):
            op = Op()
            op.eng = eng
            op.fn = None
            op.barrier = True
            op.is_dma = False
            op.sem = None
            op.signals = False
            op.count = 0
            op.idx = len(self.ops)
            op.deps = set()
            op.waits = tuple(("dma", k) for k in exclude)
            self.ops.append(op)
        keep = {k: v for k, v in self.last_w.items() if self.ops[v].is_dma and self.ops[v].sem[1] in exclude}
        self.last_w = keep
        self.readers = {}

    def _skip(self, dop, op):
        return (not dop.is_dma) and dop.eng == op.eng and (not op.is_dma) and (dop.eng == "pe" or not self.same)

    def finalize(self, block):
        ops = self.ops
        for op in ops:
            for d in op.deps:
                dop = ops[d]
                if dop.is_dma or self._skip(dop, op):
                    continue
                dop.signals = True
        cnt = {}
        for op in ops:
            if op.barrier:
                op.count = dict(cnt)
                continue
            if op.signals:
                inc = 16 if op.is_dma else 1
                cnt[op.sem] = cnt.get(op.sem, 0) + inc
                op.count = cnt[op.sem]
        waited = {}
        for op in ops:
            w = waited.setdefault(op.eng, {})
            if op.barrier:
                excl = op.waits
                need = {s: v for s, v in op.count.items() if s != ("eng", op.eng) and s not in excl}
                op.waits = []
            else:
                op.waits = []
                need = {}
                for d in op.deps:
                    dop = ops[d]
                    if self._skip(dop, op):
                        continue
                    if need.get(dop.sem, 0) < dop.count:
                        need[dop.sem] = dop.count
            for s, v in need.items():
                if w.get(s, 0) < v:
                    w[s] = v
                    op.waits.append((s, v))
        final_waits = [(s, v) for s, v in cnt.items() if s[0] == "dma"]
        for s in cnt:
            self.sem(s)
        per_eng = {}
        for op in ops:
            per_eng.setdefault(op.eng, []).append(op)

        def emit(engname, eng_obj, final=False):
            for op in per_eng.get(engname, []):
                for s, v in op.waits:
                    eng_obj.wait_ge(self.sem(s), v)
                if op.fn is None:
                    continue
                ins = op.fn(eng_obj)
                if op.signals:
                    ins.then_inc(self.sem(op.sem), 16 if op.is_dma else 1)
            if final:
                for s, v in final_waits:
                    eng_obj.wait_ge(self.sem(s), v)

        @block.sync
        def _(e):
            emit("sp", e, final=True)

        @block.scalar
        def _(e):
            emit("act", e)

        @block.vector
        def _(e):
            emit("dve", e)

        @block.gpsimd
        def _(e):
            emit("pool", e)

        @block.tensor
        def _(e):
            emit("pe", e)
        return len(ops)


class Arena:
    def __init__(self, tensor, ncols):
        self.t = tensor
        self.n = ncols
        self.off = 0

    def reset(self):
        self.off = 0

    def alloc(self, shape, dt):
        n = 1
        for s_ in shape:
            n *= s_
        if dt == BF16:
            ncol = (n + 1) // 2
        else:
            ncol = n
        ncol = (ncol + 15) // 16 * 16
        assert self.off + ncol <= self.n, ("arena overflow", self.off, ncol, self.n)
        v = self.t[:, self.off:self.off + ncol]
        self.off += ncol
        if dt != F32:
            v = v.bitcast(dt)
        v = v[:, 0:n]
        if len(shape) == 2:
            v = v.rearrange("p (a b) -> p a b", a=shape[0])
        elif len(shape) == 3:
            v = v.rearrange("p (a b c) -> p a b c", a=shape[0], b=shape[1])
        return v


def build_program(n_layers=L, debug=False, phases=("A0", "A1", "A2", "A3", "B")):
    nc = bass.Bass("TRN2", target_bir_lowering=False)

    def din(name, shape, dt=F32):
        return nc.dram_tensor(name, list(shape), dt, kind="ExternalInput").ap()

    def dscr(name, shape, dt):
        return nc.dram_tensor(name, list(shape), dt, kind="ExternalOutput" if debug else "Internal").ap()

    x_d = din("x", [S, D])
    win_d = din("w_in", [L, 56, 128, 1024])
    wba_d = din("w_br_attn", [L, 128, 8, 1024])
    wbl_d = din("w_br_lru", [L, 128, 8, 1024])
    wo_d = din("w_out", [L, 128, 8, 1024])
    wq_d = din("peer_wq", [L, 16, 128, 1024])
    sk_d = din("peer_skT", [L, 128, 16 * 128])
    pu_ds = [din("peer_u%d" % i, [NEXP, D]) for i in range(L)]
    pv_ds = [din("peer_v%d" % i, [NEXP, D]) for i in range(L)]
    iota_d = din("iota256", [256])
    gaw_d = din("gate_a_w", [L, 128, 8 * 128])
    gxw_d = din("gate_x_w", [L, 128, 8 * 128])
    chp_d = din("chp", [L, 128, 64])
    lq_d = din("lambda_qk", [L, 256])
    sg_d = din("subln_g", [L, 128])
    ln1g_d = din("ln1_g", [L, D])
    ln1b_d = din("ln1_b", [L, D])
    ln2g_d = din("ln2_g", [L, D])
    ln2b_d = din("ln2_b", [L, D])
    augk_d = din("aug_k", [3, 128])
    augq_d = din("aug_q", [3, 8 * 512])
    y_d = nc.dram_tensor("y", [S, D], F32, kind="ExternalOutput").ap()

    xT_d = dscr("xT_s", [D, S], BF16)
    yaT_d = dscr("yaT_s", [D, S], BF16)
    yrT_d = dscr("yrT_s", [D, S], BF16)
    x1_d = dscr("x1_s", [S, D], F32)
    x2_d = dscr("x2_s", [S, D], F32)
    tb16 = [nc.dram_tensor("tb16_%d" % i, [NEXP, 2 * D], BF16, kind="Internal").ap() for i in range(L)]

    with ExitStack() as st:
        ARN = 49000
        arena_t = st.enter_context(nc.sbuf_tensor("arena", [128, ARN], F32))
        cst_t = st.enter_context(nc.sbuf_tensor("cst", [128, 3200], F32))
        pf = [st.enter_context(nc.psum_tensor("pf%d" % i, [128, 512], F32)) for i in range(7)]
        pb = st.enter_context(nc.psum_tensor("pbb", [128, 1024], BF16))
        block = st.enter_context(nc.Block())
        SC = Sched(nc, st, same_engine_sync=SAME_SYNC)
        AR = Arena(arena_t, ARN)
        CA = Arena(cst_t, 3200)

        def DMA(out, in_, reads, writes, key, q="sp"):
            SC.add(q, lambda e: e.dma_start(out=out, in_=in_), reads, writes, dma=key)

        def MM(out, lhsT, rhs, start, stop, reads, writes):
            SC.add("pe", lambda e: e.matmul(out, lhsT=lhsT, rhs=rhs, start=start, stop=stop), reads, writes)

        def TR(out, in_, reads, writes):
            SC.add("pe", lambda e: e.transpose(out=out, in_=in_, identity=ident), list(reads) + ["ident"], writes)

        def ACT(out, in_, func, reads, writes, bias=None, scale=None, accum=None):
            kw = {}
            if bias is not None:
                kw["bias"] = bias
            if scale is not None:
                kw["scale"] = scale
            if accum is not None:
                kw["accum_out"] = accum
            SC.add("act", lambda e: e.activation(out=out, in_=in_, func=func, **kw), reads, writes)

        def CP(eng, out, in_, reads, writes):
            if eng == "act":
                SC.add("act", lambda e: e.activation(out=out, in_=in_, func=AF.Copy), reads, writes)
            else:
                SC.add(eng, lambda e: e.tensor_copy(out=out, in_=in_), reads, writes)

        def TT(eng, out, in0, in1, op, reads, writes):
            SC.add(eng, lambda e: e.tensor_tensor(out=out, in0=in0, in1=in1, op=op), reads, writes)

        def TS(eng, out, in0, s1, s2, op0, op1, reads, writes, accum=None):
            if accum is None:
                if s2 is None:
                    SC.add(eng, lambda e: e.tensor_scalar(out=out, in0=in0, scalar1=s1, scalar2=None, op0=op0), reads, writes)
                else:
                    SC.add(eng, lambda e: e.tensor_scalar(out=out, in0=in0, scalar1=s1, scalar2=s2, op0=op0, op1=op1), reads, writes)
            else:
                SC.add(eng, lambda e: e.tensor_scalar(out=out, in0=in0, scalar1=s1, scalar2=s2, op0=op0, op1=op1, accum_out=accum), reads, writes)

        def STT(out, in0, scalar, in1, op0, op1, reads, writes, accum=None):
            if accum is None:
                SC.add("dve", lambda e: e.scalar_tensor_tensor(out=out, in0=in0, scalar=scalar, in1=in1, op0=op0, op1=op1), reads, writes)
            else:
                SC.add("dve", lambda e: e.scalar_tensor_tensor(out=out, in0=in0, scalar=scalar, in1=in1, op0=op0, op1=op1, accum_out=accum), reads, writes)

        def MEMSET(eng, out, val, writes):
            SC.add(eng, lambda e: e.memset(out, val), (), writes)

        identf = CA.alloc([128], F32)
        ident = CA.alloc([128], BF16)
        trif = CA.alloc([128], F32)
        tri = CA.alloc([128], BF16)
        augk_f = CA.alloc([128], F32)
        augq_f = AR.alloc([8 * 512], F32)
        augk = CA.alloc([128], BF16)
        augq = CA.alloc([8, 512], BF16)
        MEMSET("pool", identf, 1.0, ["identf"])
        SC.add("pool", lambda e: e.affine_select(out=identf, in_=identf, pattern=[[-1, 128]], compare_op=ALU.is_equal,
                                                 fill=0.0, base=0, channel_multiplier=1), ["identf"], ["identf"])
        CP("dve", ident, identf, ["identf"], ["ident"])
        MEMSET("pool", trif, 1.0, ["trif"])
        SC.add("pool", lambda e: e.affine_select(out=trif, in_=trif, pattern=[[1, 128]], compare_op=ALU.is_ge,
                                                 fill=0.0, base=0, channel_multiplier=-1), ["trif"], ["trif"])
        CP("dve", tri, trif, ["trif"], ["tri"])
        zrow = CA.alloc([512], BF16)
        MEMSET("pool", zrow, 0.0, ["zrow"])
        iota = CA.alloc([256], F32)
        DMA(iota, iota_d.partition_broadcast(128), [], ["iota"], "c2")
        DMA(augk_f[64:67, :], augk_d, [], ["augk_f"], "c0")
        DMA(augq_f[64:67, :], augq_d, [], ["augq_f"], "c1")
        CP("dve", augk[64:67, :], augk_f[64:67, :], ["augk_f"], ["augk"])
        CP("dve", augq[64:67, :, :], augq_f[64:67, :].rearrange("p (a b) -> p a b", a=8), ["augq_f"], ["augq"])

        def conv_dma(dst, src, key):
            SC.add("pool", lambda e: e.dma_start(out=dst, in_=src), ["xT"], [key], dma="cv")

        conv_chunks = []
        for l_ in range(n_layers):
            for (src, c0) in ((pu_ds[l_], 0), (pv_ds[l_], D)):
                for ch in range(4):
                    conv_chunks.append((tb16[l_][ch * 4096:(ch + 1) * 4096, c0:c0 + D], src[ch * 4096:(ch + 1) * 4096, :], ("tb16", l_)))

        def emit_conversion(n):
            for _ in range(n):
                if conv_chunks and "B" in phases:
                    conv_dma(*conv_chunks.pop(0))

        def layernorm(y, g_bc, b_bc, out, wk, rkeys, wkeys, tagk):
            stt, mv, lnv, rstd = wk["st"], wk["mv"], wk["lnv"], wk["rstd"]
            SC.add("dve", lambda e: e.bn_stats(out=stt[:, 0:6], in_=y[:, 0:512]), rkeys, [tagk + "st"])
            SC.add("dve", lambda e: e.bn_stats(out=stt[:, 6:12], in_=y[:, 512:1024]), rkeys, [tagk + "st"])
            SC.add("dve", lambda e: e.bn_aggr(out=mv, in_=stt), [tagk + "st"], [tagk + "mv"])
            ACT(lnv, mv[:, 1:2], AF.Ln, [tagk + "mv", "eps"], [tagk + "lnv"], bias=wk["eps"])
            ACT(rstd, lnv, AF.Exp, [tagk + "lnv"], [tagk + "rstd"], scale=-0.5)
            TS("dve", y, y, mv[:, 0:1], rstd, ALU.subtract, ALU.mult, list(rkeys) + [tagk + "mv", tagk + "rstd"], wkeys_y(rkeys))
            TT("pool", y, y, g_bc, ALU.mult, list(rkeys) + ["lnp"], wkeys_y(rkeys))
            TT("pool", out, y, b_bc, ALU.add, list(rkeys) + ["lnp"], wkeys)

        def wkeys_y(rkeys):
            return list(rkeys)

        for l in range(n_layers):
            lam_init = 0.8 - 0.6 * math.exp(-0.3 * l)
            x_src = x_d if l == 0 else x2_d
            x_dst = y_d if l == n_layers - 1 else x2_d
            xsrc_key = "x2d"
            SC.barrier()
            AR.reset()
            xT = AR.alloc([8, S], BF16)
            a1_mark = AR.off
            if "A0" in phases:
                xs = [AR.alloc([D], F32) for _ in range(2)]
                xb = [AR.alloc([D], BF16) for _ in range(2)]
                for tb in range(NTB):
                    b = tb % 2
                    DMA(xs[b], x_src[tb * 128:(tb + 1) * 128, :], [xsrc_key], [("xs", b)], "xs%d" % b)
                    CP("act", xb[b], xs[b], [("xs", b)], [("xb", b)])
                    for dc in range(8):
                        TR(pb[:, dc * 128:(dc + 1) * 128], xb[b][:, dc * 128:(dc + 1) * 128], [("xb", b)], ["pb"])
                    CP("dve", xT[:, :, tb * 128:(tb + 1) * 128], pb[:, :].rearrange("p (a b) -> p a b", a=8), ["pb"], ["xT"])
                for dc in range(8):
                    DMA(xT_d[dc * 128:(dc + 1) * 128, :], xT[:, dc, :], ["xT"], ["xTd"], "xTd")
                if "A1" not in phases:
                    emit_conversion(len(conv_chunks))

            if "A1" in phases:
                AR.off = a1_mark
                wst = [AR.alloc([3, 1024], F32) for _ in range(2)]
                wbf = [AR.alloc([3, 1024], BF16) for _ in range(2)]
                qT2 = AR.alloc([2, S], BF16)
                kT2 = AR.alloc([2, S], BF16)
                Va = AR.alloc([NTB, 130], BF16)
                NE = 6
                Eb = [AR.alloc([512], BF16) for _ in range(NE)]
                Osb = AR.alloc([4, 512], F32)
                obuf = AR.alloc([4, 128], F32)
                junk = AR.alloc([128], F32)
                yab = AR.alloc([4, 128], BF16)
                yst = [AR.alloc([512], BF16) for _ in range(2)]
                lq = AR.alloc([256], F32)
                sgb = AR.alloc([128], F32)
                gsc = AR.alloc([128], F32)
                sm = AR.alloc([32], F32)
                neglam = sm[:, 0:1]
                s12 = sm[:, 1:3]
                e12 = sm[:, 3:5]
                rz = sm[:, 8:16]
                rz2l = sm[:, 16:20]
                ss = sm[:, 20:24]
                lnv4 = sm[:, 24:28]
                rstd4 = sm[:, 28:32]
                epsr = AR.alloc([1], F32)
                MEMSET("pool", epsr, RMS_EPS, ["epsr"])
                MEMSET("pool", Va[:, :, 128:130], 1.0, ["Va1"])
                for c in range(2):
                    CP("pool", kT2[64:67, c, :].rearrange("p (a b) -> p a b", a=NTB),
                       augk[64:67, :].unsqueeze(1).broadcast_to([3, NTB, 128]), ["augk"], ["kT"])
                DMA(lq, lq_d[l].partition_broadcast(128), [], ["lq"], "p0")
                DMA(sgb, sg_d[l].partition_broadcast(128), [], ["sgb"], "p1")
                STT(junk[:, 0:64], lq[:, 0:64], 1.0, lq[:, 64:128], ALU.mult, ALU.mult, ["lq"], ["junk", "s1"], accum=s12[:, 0:1])
                STT(junk[:, 0:64], lq[:, 128:192], 1.0, lq[:, 192:256], ALU.mult, ALU.mult, ["lq"], ["junk", "s2"], accum=s12[:, 1:2])
                ACT(e12, s12, AF.Exp, ["s1", "s2"], ["e12"])
                TT("dve", neglam, e12[:, 1:2], e12[:, 0:1], ALU.subtract, ["e12"], ["neglam"])
                TS("dve", neglam, neglam, -lam_init, None, ALU.add, None, ["neglam"], ["neglam"])
                TS("dve", gsc, sgb, 1.0 - lam_init, None, ALU.mult, None, ["sgb"], ["gsc"])

                def load_w(h, slot):
                    for i, cb in enumerate((h, 8 + h, 16 + h)):
                        DMA(wst[slot][:, i, :], win_d[l, cb], [], [("wst", slot)], "wst%d" % slot)
                    CP("dve", wbf[slot], wst[slot], [("wst", slot)], [("wbf", slot)])

                def epilogue2(h, j):
                    SC.add("dve", lambda e: e.reciprocal(out=rz.rearrange("p (a b) -> p a b", a=4),
                                                         in_=Osb[:, :, 128:512:256]), ["Osb"], ["rz"])
                    TS("dve", rz2l, rz[:, 4:8], neglam, None, ALU.mult, None, ["rz", "neglam"], ["rz2l"])
                    for qs in range(4):
                        o1 = Osb[:, qs // 2, (qs % 2) * 256:(qs % 2) * 256 + 128]
                        o2 = Osb[:, 2 + qs // 2, (qs % 2) * 256:(qs % 2) * 256 + 128]
                        TS("dve", obuf[:, qs, :], o1, rz[:, qs:qs + 1], None, ALU.mult, None, ["Osb", "rz"], ["obuf"])
                        STT(obuf[:, qs, :], o2, rz2l[:, qs:qs + 1], obuf[:, qs, :], ALU.mult, ALU.add, ["Osb", "rz2l", "obuf"], ["obuf"])
                        STT(junk, obuf[:, qs, :], 1.0, obuf[:, qs, :], ALU.mult, ALU.mult, ["obuf"], ["junk", "ss"], accum=ss[:, qs:qs + 1])
                    ACT(lnv4, ss, AF.Ln, ["ss"], ["lnv4"], scale=1.0 / 128.0, bias=epsr)
                    ACT(rstd4, lnv4, AF.Exp, ["lnv4"], ["rstd4"], scale=-0.5)
                    for qs in range(4):
                        STT(yab[:, qs, :], obuf[:, qs, :], rstd4[:, qs:qs + 1], gsc, ALU.mult, ALU.mult, ["obuf", "rstd4", "gsc"], ["yab"])
                    for qs in range(4):
                        TR(pb[:, qs * 128:(qs + 1) * 128], yab[:, qs, :], ["yab"], ["pb"])
                    ys = yst[j % 2]
                    CP("dve", ys, pb[:, 0:512], ["pb"], [("yst", j % 2)])
                    DMA(yaT_d[h * 128:(h + 1) * 128, j * 512:(j + 1) * 512], ys, [("yst", j % 2)], ["yaTd"], "yst%d" % (j % 2))

                load_w(0, 0)
                ei = 0
                si = 0
                for h in range(8):
                    slot = h % 2
                    if h + 1 < 8:
                        load_w(h + 1, 1 - slot)
                    emit_conversion(2)
                    slope = 2.0 ** (-(h + 1))
                    for c in range(2):
                        CP("dve", qT2[64:67, c, :].rearrange("p (a b) -> p a b", a=8),
                           augq[64:67, h, :].unsqueeze(1).broadcast_to([3, 8, 512]), ["augq"], ["qT"])
                    for tq in range(8):
                        for (wi, dst, key) in ((0, qT2, "qT"), (1, kT2, "kT")):
                            for dc in range(8):
                                MM(pf[6][:, :], wbf[slot][:, wi, dc * 128:(dc + 1) * 128], xT[:, dc, tq * 512:(tq + 1) * 512],
                                   dc == 0, dc == 7, [("wbf", slot), "xT"], ["pf6"])
                            CP("dve", dst[0:64, 0, tq * 512:(tq + 1) * 512], pf[6][0:64, :], ["pf6"], [key])
                            CP("dve", dst[0:64, 1, tq * 512:(tq + 1) * 512], pf[6][64:128, :], ["pf6"], [key])
                    for tb4 in range(8):
                        for t in range(4):
                            tb = tb4 * 4 + t
                            for dc in range(8):
                                MM(pf[6][:, t * 128:(t + 1) * 128], xT[:, dc, tb * 128:(tb + 1) * 128],
                                   wbf[slot][:, 2, dc * 128:(dc + 1) * 128], dc == 0, dc == 7, [("wbf", slot), "xT"], ["pf6"])
                        CP("dve", Va[:, tb4 * 4:(tb4 + 1) * 4, 0:128], pf[6][:, :].rearrange("p (a b) -> p a b", a=4), ["pf6"], ["Va"])
                    def live(j, kb):
                        return slope * (j * 512 - kb * 128 - 127) <= SKIP_ARG
                    tiles = [(j, c, kb) for j in range(8) for c in range(2) for kb in range(4 * j + 4) if live(j, kb)]
                    first_of_j = {}
                    for (j_, c_, kb_) in tiles:
                        first_of_j.setdefault(j_, (c_, kb_))
                    sinfo = {}

                    def emit_S(i):
                        nonlocal si
                        j, c, kb = tiles[i]
                        r = kb - 4 * j
                        nq0 = max(0, r) * 128
                        sb_ = pf[4 + si % 2]
                        skey = ("S", si % 2)
                        si += 1
                        MM(sb_[:, nq0:512], kT2[0:67, c, kb * 128:(kb + 1) * 128],
                           qT2[0:67, c, j * 512 + nq0:(j + 1) * 512], True, True, ["qT", "kT"], [skey])
                        sinfo[i] = (sb_, skey)

                    pending = []

                    def emit_rest(i):
                        nonlocal ei
                        j, c, kb = tiles[i]
                        r = kb - 4 * j
                        nq0 = max(0, r) * 128
                        sb_, skey = sinfo.pop(i)
                        if (c, kb) == first_of_j[j]:
                            for bnk in range(4):
                                MM(pf[bnk][:, :], zrow[0:1, 0:128], zrow[0:1, 0:512], True, False, ["zrow"], [("O", bnk)])
                        E = Eb[ei % NE]
                        ekey = ("E", ei % NE)
                        ei += 1
                        ACT(E[:, nq0:512], sb_[:, nq0:512], AF.Exp, [skey], [ekey], scale=0.125,
                            bias=float(slope * (kb * 128 - j * 512)))
                        if r >= 0:
                            TT("dve", E[:, r * 128:(r + 1) * 128], E[:, r * 128:(r + 1) * 128], tri, ALU.mult, [ekey, "tri"], [ekey])
                        for qs in range(max(0, r), 4):
                            ob = pf[c * 2 + qs // 2]
                            MM(ob[:, (qs % 2) * 256:(qs % 2) * 256 + 129], E[:, qs * 128:(qs + 1) * 128], Va[:, kb, 0:129],
                               False, kb == 4 * j + qs, [ekey, "Va", "Va1"], [("O", c * 2 + qs // 2)])
                        if c == 1 and kb == 4 * j + 3:
                            for bnk in range(4):
                                CP("dve", Osb[:, bnk, :], pf[bnk][:, :], [("O", bnk)], ["Osb"])
                            pending.append((i + 5, h, j))
                        while pending and (pending[0][0] <= i or i == len(tiles) - 1):
                            _, hh, jj = pending.pop(0)
                            epilogue2(hh, jj)

                    emit_S(0)
                    for i in range(len(tiles)):
                        if i + 1 < len(tiles):
                            emit_S(i + 1)
                        emit_rest(i)

            if "A2" in phases:
                SC.barrier()
                AR.off = a1_mark
                B0 = AR.alloc([S + 16], F32)
                B1 = AR.alloc([S], F32)
                B2 = AR.alloc([S], F32)
                B3 = AR.alloc([S], F32)
                xcb = AR.alloc([S], BF16)
                Yb = AR.alloc([S], BF16)
                wst2 = [AR.alloc([2, 1024], F32) for _ in range(2)]
                wbf2 = [AR.alloc([2, 1024], BF16) for _ in range(2)]
                gwf = AR.alloc([2, 1024], F32)
                gwb = AR.alloc([2, 8, 128], BF16)
                chp = AR.alloc([8, 8], F32)
                cc = AR.alloc([8, 4], F32)
                DMA(gwf[:, 0, :], gaw_d[l], [], ["gwf"], "p2")
                DMA(gwf[:, 1, :], gxw_d[l], [], ["gwf"], "p2")
                CP("dve", gwb, gwf.rearrange("p a (g j) -> p a g j", g=8), ["gwf"], ["gwb"])
                DMA(chp, chp_d[l].rearrange("p (g f) -> p g f", g=8), [], ["chp"], "p3")
                ACT(cc[:, :, 2], chp[:, :, 7], AF.Exp, ["chp"], ["cc"], scale=-1.0)
                ACT(cc[:, :, 3], cc[:, :, 2], AF.Ln, ["cc"], ["cc"], bias=1.0)
                TS("dve", cc[:, :, 0], cc[:, :, 3], -8.0, None, ALU.mult, None, ["cc"], ["cc"])
                TS("dve", cc[:, :, 1], cc[:, :, 3], -16.0, None, ALU.mult, None, ["cc"], ["cc"])

                def load_w2(g, slot):
                    DMA(wst2[slot][:, 0, :], win_d[l, 24 + g], [], [("wst2", slot)], "wst2%d" % slot)
                    DMA(wst2[slot][:, 1, :], win_d[l, 32 + g], [], [("wst2", slot)], "wst2%d" % slot)
                    CP("pool", wbf2[slot], wst2[slot], [("wst2", slot)], [("wbf2", slot)])

                load_w2(0, 0)
                pi = 0
                for g in range(8):
                    slot = g % 2
                    if g + 1 < 8:
                        load_w2(g + 1, 1 - slot)
                    MEMSET("pool", B0[:, 0:3], 0.0, ["B0"])
                    for tq in range(8):
                        pp = pf[pi % 7]
                        pk = ("pf", pi % 7)
                        pi += 1
                        for dc in range(8):
                            MM(pp[:, :], wbf2[slot][:, 0, dc * 128:(dc + 1) * 128], xT[:, dc, tq * 512:(tq + 1) * 512], dc == 0, dc == 7,
                               [("wbf2", slot), "xT"], [pk])
                        CP("act", B0[:, 3 + tq * 512:3 + (tq + 1) * 512], pp[:, :], [pk], ["B0"])
                    TS("dve", B1, B0[:, 0:S], chp[:, g, 0:1], chp[:, g, 4:5], ALU.mult, ALU.add, ["B0", "chp"], ["B1"])
                    for k in range(1, 4):
                        STT(B1, B0[:, k:k + S], chp[:, g, k:k + 1], B1, ALU.mult, ALU.add, ["B0", "chp", "B1"], ["B1"])
                    CP("pool", xcb, B1, ["B1"], ["xcb"])
                    for (wi, dst, off, key, bcol) in ((0, B2, 0, "B2", 5), (1, B0, 3, "B0", 6)):
                        for tq in range(8):
                            pp = pf[pi % 7]
                            pk = ("pf", pi % 7)
                            pi += 1
                            MM(pp[:, :], gwb[:, wi, g, :], xcb[:, tq * 512:(tq + 1) * 512], True, True, ["gwb", "xcb"], [pk])
                            ACT(dst[:, off + tq * 512:off + (tq + 1) * 512], pp[:, :], AF.Sigmoid, [pk, "chp", "B1"], [key],
                                bias=chp[:, g, bcol:bcol + 1])
                    ACT(B3, B2, AF.Exp, ["B2", "cc"], ["B3"], scale=cc[:, g, 1:2])
                    ACT(B3, B3, AF.Sqrt, ["B3"], ["B3"], scale=-1.0, bias=1.0)
                    MEMSET("pool", B3[:, 0:1], 1.0, ["B3"])
                    ACT(B2, B2, AF.Exp, ["B2", "cc"], ["B2"], scale=cc[:, g, 0:1])
                    TT("dve", B0[:, 3:3 + S], B0[:, 3:3 + S], B1, ALU.mult, ["B0", "B1"], ["B0"])
                    TT("dve", B0[:, 3:3 + S], B0[:, 3:3 + S], B3, ALU.mult, ["B0", "B3"], ["B0"])
                    SC.add("dve", lambda e: e.tensor_tensor_scan(out=B3, data0=B2, data1=B0[:, 3:3 + S], initial=0.0,
                                                                 op0=ALU.mult, op1=ALU.add), ["B2", "B0", "B3"], ["B3"])
                    for tq in range(8):
                        pp = pf[pi % 7]
                        pk = ("pf", pi % 7)
                        pi += 1
                        for dc in range(8):
                            MM(pp[:, :], wbf2[slot][:, 1, dc * 128:(dc + 1) * 128], xT[:, dc, tq * 512:(tq + 1) * 512], dc == 0, dc == 7,
                               [("wbf2", slot), "xT"], [pk])
                        ACT(B1[:, tq * 512:(tq + 1) * 512], pp[:, :], AF.Gelu_apprx_tanh, [pk, "B0"], ["B1"])
                    TT("dve", Yb, B3, B1, ALU.mult, ["B3", "B1", "yrTd"], ["Yb"])
                    DMA(yrT_d[g * 128:(g + 1) * 128, :], Yb, ["Yb"], ["yrTd"], "yrTd")

            if "A3" in phases:
                SC.barrier()
                AR.reset()
                wg = AR.alloc([16, 8, 128], BF16)
                wba = AR.alloc([8, 1024], BF16)
                wbl = AR.alloc([8, 1024], BF16)
                wo = AR.alloc([8, 1024], BF16)
                stg = [AR.alloc([1024], F32) for _ in range(2)]
                lng = AR.alloc([D], F32)
                lnb = AR.alloc([D], F32)
                DMA(lng, ln1g_d[l].partition_broadcast(128), [], ["lnp"], "p4")
                DMA(lnb, ln1b_d[l].partition_broadcast(128), [], ["lnp"], "p4")
                si_ = 0
                for cb in range(16):
                    b = si_ % 2
                    si_ += 1
                    DMA(stg[b], win_d[l, 40 + cb], [], [("stg", b)], "stg%d" % b)
                    CP("dve" if cb % 2 else "pool", wg[:, cb, :, :], stg[b].rearrange("p (a b) -> p a b", a=8), [("stg", b)], ["wg"])
                for (src, dst, key) in ((wba_d, wba, "wba"), (wbl_d, wbl, "wbl"), (wo_d, wo, "wo")):
                    for kc in range(8):
                        b = si_ % 2
                        si_ += 1
                        DMA(stg[b], src[l, :, kc, :], [], [("stg", b)], "stg%d" % b)
                        CP("dve" if kc % 2 else "pool", dst[:, kc, :], stg[b], [("stg", b)], [key])
                xTb = [AR.alloc([8, 512], BF16) for _ in range(2)]
                yaTb = [AR.alloc([8, 512], BF16) for _ in range(2)]
                yrTb = [AR.alloc([8, 512], BF16) for _ in range(2)]
                mT = AR.alloc([8, 512], BF16)
                sga = [AR.alloc([512], F32) for _ in range(2)]
                sgr = [AR.alloc([512], F32) for _ in range(2)]
                xres = [AR.alloc([D], F32) for _ in range(2)]
                yln = [AR.alloc([D], F32) for _ in range(2)]
                wk = {"st": AR.alloc([12], F32), "mv": AR.alloc([2], F32), "lnv": AR.alloc([1], F32),
                      "rstd": AR.alloc([1], F32), "eps": AR.alloc([1], F32)}
                MEMSET("pool", wk["eps"], LN_EPS, ["eps"])
                ti = 0
                for tq in range(8):
                    b = tq % 2
                    DMA(xTb[b], xT_d[:, tq * 512:(tq + 1) * 512].rearrange("(a p) t -> p a t", p=128), ["xTd"], [("xTb", b)], "xTb%d" % b)
                    DMA(yaTb[b], yaT_d[:, tq * 512:(tq + 1) * 512].rearrange("(a p) t -> p a t", p=128), ["yaTd"], [("yaTb", b)], "yaTb%d" % b)
                    DMA(yrTb[b], yrT_d[:, tq * 512:(tq + 1) * 512].rearrange("(a p) t -> p a t", p=128), ["yrTd"], [("yrTb", b)], "yrTb%d" % b)
                    for jb in range(8):
                        for dc in range(8):
                            MM(pf[0][:, :], wg[:, jb, dc, :], xTb[b][:, dc, :], dc == 0, dc == 7, ["wg", ("xTb", b)], ["pf0"])
                        for dc in range(8):
                            MM(pf[1][:, :], wg[:, 8 + jb, dc, :], xTb[b][:, dc, :], dc == 0, dc == 7, ["wg", ("xTb", b)], ["pf1"])
                        for kc in range(8):
                            MM(pf[2][:, :], wba[:, kc, jb * 128:(jb + 1) * 128], yaTb[b][:, kc, :], kc == 0, kc == 7, ["wba", ("yaTb", b)], ["pf2"])
                        for kc in range(8):
                            MM(pf[3][:, :], wbl[:, kc, jb * 128:(jb + 1) * 128], yrTb[b][:, kc, :], kc == 0, kc == 7, ["wbl", ("yrTb", b)], ["pf3"])
                        sb2 = jb % 2
                        ACT(sga[sb2], pf[0][:, :], AF.Sigmoid, ["pf0"], [("sga", sb2)])
                        ACT(sgr[sb2], pf[1][:, :], AF.Sigmoid, ["pf1"], [("sgr", sb2)])
                        TT("dve", sga[sb2], sga[sb2], pf[2][:, :], ALU.mult, [("sga", sb2), "pf2"], [("sga", sb2)])
                        TT("dve", sgr[sb2], sgr[sb2], pf[3][:, :], ALU.mult, [("sgr", sb2), "pf3"], [("sgr", sb2)])
                        TT("pool", mT[:, jb, :], sga[sb2], sgr[sb2], ALU.add, [("sga", sb2), ("sgr", sb2)], ["mT"])
                    for ts_ in range(4):
                        tb = tq * 4 + ts_
                        xb_ = ti % 2
                        ti += 1
                        DMA(xres[xb_], x_src[tb * 128:(tb + 1) * 128, :], [xsrc_key], [("xres", xb_)], "xres%d" % xb_)
                        for nh in range(2):
                            for jb in range(8):
                                MM(pf[4 + nh][:, :], mT[:, jb, ts_ * 128:(ts_ + 1) * 128], wo[:, jb, nh * 512:(nh + 1) * 512], jb == 0, jb == 7,
                                   ["mT", "wo"], [("pf", 4 + nh)])
                            STT(yln[xb_][:, nh * 512:(nh + 1) * 512], xres[xb_][:, nh * 512:(nh + 1) * 512], ALPHA, pf[4 + nh][:, :],
                                ALU.mult, ALU.add, [("xres", xb_), ("pf", 4 + nh)], [("yln", xb_)])
                        layernorm(yln[xb_], lng, lnb, yln[xb_], wk, [("yln", xb_)], [("yln", xb_)], "ln")
                        DMA(x1_d[tb * 128:(tb + 1) * 128, :], yln[xb_], [("yln", xb_)], ["x1d"], "x1st%d" % xb_)

            if "B" in phases:
                SC.barrier()
                AR.reset()
                wq = AR.alloc([16, 8, 128], BF16)
                skT = AR.alloc([16, 128], BF16)
                lng = AR.alloc([D], F32)
                lnb = AR.alloc([D], F32)
                bmark = AR.off
                stg = [AR.alloc([2048], F32) for _ in range(2)]
                DMA(lng, ln2g_d[l].partition_broadcast(128), [], ["lnp"], "p4")
                DMA(lnb, ln2b_d[l].partition_broadcast(128), [], ["lnp"], "p4")
                for cb in range(16):
                    b = cb % 2
                    DMA(stg[b][:, 0:1024], wq_d[l, cb], [], [("stg", b)], "stg%d" % b)
                    CP("dve" if cb % 2 else "pool", wq[:, cb, :, :], stg[b][:, 0:1024].rearrange("p (a b) -> p a b", a=8), [("stg", b)], ["wq"])
                DMA(stg[0], sk_d[l], [], [("stg", 0)], "stg0")
                CP("dve", skT, stg[0].rearrange("p (a b) -> p a b", a=16), [("stg", 0)], ["skT"])
                SC.barrier()
                AR.off = bmark
                x1 = [AR.alloc([D], F32) for _ in range(2)]
                x1b = [AR.alloc([D], BF16) for _ in range(2)]
                x1T = AR.alloc([8, 128], BF16)
                qTs = AR.alloc([16, 128], BF16)
                scs = AR.alloc([16, 128], F32)
                top = AR.alloc([16, 16], F32)
                tix = AR.alloc([16, 16], U32)
                tixf = AR.alloc([16, 16], F32)
                work = AR.alloc([256], F32)
                cand = AR.alloc([8, 256], F32)
                eid = AR.alloc([8, 256], F32)
                tsv = AR.alloc([8, 16], F32)
                pos = AR.alloc([8, 16], U32)
                posf = AR.alloc([8, 16], F32)
                ef = AR.alloc([128], F32)
                af_ = AR.alloc([128], F32)
                bf_ = AR.alloc([128], F32)
                e1_ = AR.alloc([128], F32)
                e2_ = AR.alloc([128], F32)
                eidx = [AR.alloc([128], I32) for _ in range(2)]
                gt = [AR.alloc([8, 16], F32) for _ in range(2)]
                dsm = AR.alloc([8, 16], F32)
                zs = AR.alloc([8], F32)
                actv = AR.alloc([128], F32)
                wgt = AR.alloc([128], F32)
                acc = AR.alloc([D], F32)
                junkb = AR.alloc([D], BF16)
                ND = 8
                diag = [AR.alloc([128], BF16) for _ in range(ND)]
                wk = {"st": AR.alloc([12], F32), "mv": AR.alloc([2], F32), "lnv": AR.alloc([1], F32),
                      "rstd": AR.alloc([1], F32), "eps": AR.alloc([1], F32)}
                NG = 4
                LOOK = 4
                NRB = (LOOK + 1) * NG
                Rb = [AR.alloc([2 * D], BF16) for _ in range(NRB)]
                MEMSET("pool", wk["eps"], LN_EPS, ["eps"])
                tbl = tb16[l]
                tkey = ("tb16", l)
                pacc = [pf[5], pf[6]]

                def routing(tb):
                    b = tb % 2
                    DMA(x1[b], x1_d[tb * 128:(tb + 1) * 128, :], ["x1d"], [("x1", b)], "x1ld%d" % b)
                    CP("act", x1b[b], x1[b], [("x1", b)], [("x1b", b)])
                    for dc in range(8):
                        TR(pb[:, dc * 128:(dc + 1) * 128], x1b[b][:, dc * 128:(dc + 1) * 128], [("x1b", b)], ["pb"])
                    CP("act", x1T, pb[:, :].rearrange("p (a b) -> p a b", a=8), ["pb"], ["x1T"])
                    for c4 in range(4):
                        pp = pf[0]
                        pk = ("pf", 0)
                        for ci in range(4):
                            cb = c4 * 4 + ci
                            for dc in range(8):
                                MM(pp[:, ci * 128:(ci + 1) * 128], wq[:, cb, dc, :], x1T[:, dc, :], dc == 0, dc == 7, ["wq", "x1T"], [pk])
                        CP("act", qTs[:, c4 * 4:(c4 + 1) * 4, :], pp[:, :].rearrange("p (a b) -> p a b", a=4), [pk], ["qTs"])
                    for c4 in range(4):
                        pp = pf[1 + c4]
                        pk = ("pf", 1 + c4)
                        for ci in range(4):
                            cb = c4 * 4 + ci
                            MM(pp[:, ci * 128:(ci + 1) * 128], qTs[:, cb, :], skT[:, cb, :], True, True, ["qTs", "skT"], [pk])
                        CP("act", scs[:, c4 * 4:(c4 + 1) * 4, :], pp[:, :].rearrange("p (a b) -> p a b", a=4), [pk], ["scs"])
                    for g in range(16):
                        SC.add("dve", lambda e, g=g: e.max(out=top[:, g, 0:8], in_=scs[:, g, :]), ["scs"], ["top"])
                        SC.add("dve", lambda e, g=g: e.max_index(out=tix[:, g, 0:8], in_max=top[:, g, 0:8], in_values=scs[:, g, :]), ["scs", "top"], ["tix"])
                        SC.add("dve", lambda e, g=g: e.match_replace(out=work[:, 0:128], in_to_replace=top[:, g, 0:8], in_values=scs[:, g, :],
                                                                     imm_value=-1e30), ["scs", "top"], ["work"])
                        SC.add("dve", lambda e, g=g: e.max(out=top[:, g, 8:16], in_=work[:, 0:128]), ["work"], ["top"])
                        SC.add("dve", lambda e, g=g: e.max_index(out=tix[:, g, 8:16], in_max=top[:, g, 8:16], in_values=work[:, 0:128]), ["work", "top"], ["tix"])
                    CP("dve", tixf, tix, ["tix"], ["tixf"])
                    top4 = top.rearrange("p (h c) k -> p h c k", c=2)
                    tix4 = tixf.rearrange("p (h c) k -> p h c k", c=2)
                    cand4 = cand.rearrange("p h (a b) -> p h a b", a=16)
                    eid4 = eid.rearrange("p h (a b) -> p h a b", a=16)
                    TT("dve", cand4, top4[:, :, 0, :].unsqueeze(3).broadcast_to([128, 8, 16, 16]),
                       top4[:, :, 1, :].unsqueeze(2).broadcast_to([128, 8, 16, 16]), ALU.add, ["top"], ["cand"])
                    TS("dve", tix4[:, :, 0, :], tix4[:, :, 0, :], 128.0, None, ALU.mult, None, ["tixf"], ["tixf"])
                    for h in range(8):
                        SC.add("dve", lambda e, h=h: e.max(out=tsv[:, h, 0:8], in_=cand[:, h, :]), ["cand"], ["tsv"])
                        SC.add("dve", lambda e, h=h: e.max_index(out=pos[:, h, 0:8], in_max=tsv[:, h, 0:8], in_values=cand[:, h, :]), ["cand", "tsv"], ["pos"])
                        SC.add("dve", lambda e, h=h: e.match_replace(out=work, in_to_replace=tsv[:, h, 0:8], in_values=cand[:, h, :],
                                                                     imm_value=-1e30), ["cand", "tsv"], ["work"])
                        SC.add("dve", lambda e, h=h: e.max(out=tsv[:, h, 8:16], in_=work), ["work"], ["tsv"])
                        SC.add("dve", lambda e, h=h: e.max_index(out=pos[:, h, 8:16], in_max=tsv[:, h, 8:16], in_values=work), ["work", "tsv"], ["pos"])
                    CP("dve", posf, pos, ["pos"], ["posf"])
                    posflat = posf.rearrange("p h k -> p (h k)")
                    ge3 = cand.rearrange("p h x -> p (h x)")[:, 0:128 * 15].rearrange("p (s m) -> p s m", m=15)
                    TT("dve", ge3, posflat.unsqueeze(2).broadcast_to([128, 128, 15]),
                       iota[:, 16:256:16].unsqueeze(1).broadcast_to([128, 128, 15]), ALU.is_ge, ["posf", "iota", "cand"], ["cand"])
                    SC.add("dve", lambda e: e.tensor_reduce(out=af_, in_=ge3, axis=AX.X, op=ALU.add), ["cand"], ["af"])
                    STT(bf_, af_, -16.0, posflat, ALU.mult, ALU.add, ["af", "posf"], ["bf"])
                    io16 = iota[:, 0:16].unsqueeze(1).unsqueeze(1).broadcast_to([128, 8, 16, 16])
                    for (src_, lst, dst_, key) in ((af_, 0, e1_, "e1"), (bf_, 1, e2_, "e2")):
                        TT("dve", eid4, src_.rearrange("p (h k) -> p h k", h=8).unsqueeze(3).broadcast_to([128, 8, 16, 16]), io16,
                           ALU.is_equal, ["af", "bf", "iota", "eid"], ["eid"])
                        TT("dve", eid4, eid4, tix4[:, :, lst, :].unsqueeze(2).broadcast_to([128, 8, 16, 16]), ALU.mult, ["eid", "tixf"], ["eid"])
                        SC.add("dve", lambda e, dst_=dst_: e.tensor_reduce(out=dst_, in_=eid.rearrange("p h (k a) -> p (h k) a", a=16),
                                                                           axis=AX.X, op=ALU.add), ["eid"], [key])
                    TT("dve", ef, e1_, e2_, ALU.add, ["e1", "e2"], ["ef"])
                    CP("dve", eidx[b], ef, ["ef"], [("eidx", b)])
                    TT("dve", dsm, tsv, tsv[:, :, 0:1].broadcast_to([128, 8, 16]), ALU.subtract, ["tsv"], ["dsm"])
                    ACT(dsm, dsm, AF.Exp, ["dsm"], ["dsm"])
                    SC.add("dve", lambda e: e.tensor_reduce(out=zs, in_=dsm, axis=AX.X, op=ALU.add), ["dsm"], ["zs"])
                    SC.add("dve", lambda e: e.reciprocal(out=zs, in_=zs), ["zs"], ["zs"])
                    TT("dve", gt[b], dsm, zs.unsqueeze(2).broadcast_to([128, 8, 16]), ALU.mult, ["dsm", "zs"], [("gt", b)])

                NGR = 128 // NG
                gi = [0]
                di = [0]
                slotbuf = {}
                issued = set()

                def gathers(tb, g):
                    if (tb, g) in issued:
                        return
                    issued.add((tb, g))
                    b = tb % 2
                    for i in range(NG):
                        s_ = g * NG + i
                        rb = gi[0] % NRB
                        gi[0] += 1
                        slotbuf[(tb, s_)] = rb
                        SC.add("pool", lambda e, s_=s_, rb=rb, b=b, tbl=tbl: e.indirect_dma_start(
                            out=Rb[rb], out_offset=None, in_=tbl, in_offset=bass.IndirectOffsetOnAxis(ap=eidx[b][:, s_:s_ + 1], axis=0)),
                            [("eidx", b), tkey], [("R", rb)], dma="g%d" % rb)

                def evaluate(tb):
                    b = tb % 2
                    gtf = gt[b].rearrange("p h k -> p (h k)")
                    for g in range(LOOK):
                        gathers(tb, g)
                    for g in range(NGR):
                        if g + LOOK < NGR:
                            gathers(tb, g + LOOK)
                        elif tb + 1 < NTB:
                            gathers(tb + 1, g + LOOK - NGR)
                        sl = slice(g * NG, (g + 1) * NG)
                        for i in range(NG):
                            s_ = g * NG + i
                            rb = slotbuf[(tb, s_)]
                            STT(junkb, Rb[rb][:, 0:D], 1.0, x1b[b], ALU.mult, ALU.mult, [("R", rb), ("x1b", b)], [("actv", s_)],
                                accum=actv[:, s_:s_ + 1])
                        ACT(wgt[:, sl], actv[:, sl], AF.Gelu_apprx_tanh, [("actv", g * NG + i) for i in range(NG)], [("wgt", g)])
                        TT("dve", wgt[:, sl], wgt[:, sl], gtf[:, sl], ALU.mult, [("wgt", g), ("gt", b)], [("wgt", g)])
                        for i in range(NG):
                            s_ = g * NG + i
                            rb = slotbuf.pop((tb, s_))
                            dk = di[0] % ND
                            di[0] += 1
                            ACT(diag[dk], ident, AF.Copy, [("wgt", g), "ident"], [("diag", dk)], scale=wgt[:, s_:s_ + 1])
                            for half in range(2):
                                MM(pacc[half][:, :], diag[dk], Rb[rb][:, D + half * 512:D + (half + 1) * 512], s_ == 0, s_ == 127,
                                   [("diag", dk), ("R", rb)], [("pacc", half)])
                    for half in range(2):
                        STT(acc[:, half * 512:(half + 1) * 512], x1[b][:, half * 512:(half + 1) * 512], ALPHA, pacc[half][:, :],
                            ALU.mult, ALU.add, [("x1", b), ("pacc", half)], ["acc"])
                    layernorm(acc, lng, lnb, acc, wk, ["acc"], ["acc"], "ln")
                    DMA(x_dst[tb * 128:(tb + 1) * 128, :], acc, ["acc"], ["x2d"], "x2st")

                routing(0)
                for tb in range(NTB):
                    if tb + 1 < NTB:
                        routing(tb + 1)
                    evaluate(tb)

        n = SC.finalize(block)
    return nc, n


def prep_shared(inp):
    f = lambda a: np.ascontiguousarray(np.asarray(a, dtype=np.float32))
    w_in = np.asarray(inp["w_in"], np.float32)
    out = {}
    out["w_in"] = f(w_in.reshape(L, 8, 128, 56, 128).transpose(0, 3, 2, 1, 4).reshape(L, 56, 128, 1024))
    for k in ("w_br_attn", "w_br_lru", "w_out"):
        out[k] = f(np.asarray(inp[k], np.float32).reshape(L, 8, 128, 1024).transpose(0, 2, 1, 3))
    wq = np.asarray(inp["peer_wq"], np.float32)
    out["peer_wq"] = f(wq.reshape(L, 8, 128, 16, 128).transpose(0, 3, 2, 1, 4).reshape(L, 16, 128, 1024))
    sk = np.asarray(inp["peer_subkeys"], np.float32)
    out["peer_skT"] = f(sk.transpose(0, 4, 1, 2, 3).reshape(L, 128, 16 * 128))
    for i in range(L):
        out["peer_u%d" % i] = f(np.asarray(inp["peer_u"][i], np.float32))
        out["peer_v%d" % i] = f(np.asarray(inp["peer_v"][i], np.float32))
    out["iota256"] = np.arange(256, dtype=np.float32)
    out["gate_a_w"] = f(np.asarray(inp["gate_a_w"], np.float32).transpose(0, 2, 1, 3).reshape(L, 128, 1024))
    out["gate_x_w"] = f(np.asarray(inp["gate_x_w"], np.float32).transpose(0, 2, 1, 3).reshape(L, 128, 1024))
    chp = np.zeros((L, 8, 128, 8), np.float32)
    cw = np.asarray(inp["conv_w"], np.float32)
    for k in range(4):
        chp[:, :, :, k] = cw[:, k, :].reshape(L, 8, 128)
    chp[:, :, :, 4] = np.asarray(inp["conv_b"], np.float32).reshape(L, 8, 128)
    chp[:, :, :, 5] = np.asarray(inp["gate_a_b"], np.float32).reshape(L, 8, 128)
    chp[:, :, :, 6] = np.asarray(inp["gate_x_b"], np.float32).reshape(L, 8, 128)
    chp[:, :, :, 7] = np.asarray(inp["lru_lambda"], np.float32).reshape(L, 8, 128)
    out["chp"] = f(chp.transpose(0, 2, 1, 3).reshape(L, 128, 64))
    out["lambda_qk"] = f(np.asarray(inp["lambda_qk"], np.float32).reshape(L, 256))
    for k in ("subln_g", "ln1_g", "ln1_b", "ln2_g", "ln2_b"):
        out[k] = f(inp[k])
    augk = np.zeros((3, 128), np.float32)
    augk[0] = np.arange(128)
    augk[1] = 1.0
    augk[2] = 1.0
    augq = np.zeros((3, 8, 512), np.float32)
    qq = np.arange(512)
    for h in range(8):
        ch = 8.0 * 2.0 ** (-(h + 1))
        augq[0, h] = ch
        augq[1, h] = -ch * 128.0 * (qq // 128)
        augq[2, h] = -ch * (qq % 128)
    out["aug_k"] = augk
    out["aug_q"] = f(augq.reshape(3, 8 * 512))
    return out


_CACHE = {}


def kernel(**inputs):
    shared = prep_shared(inputs)
    x = np.asarray(inputs["x"], np.float32)
    nb = x.shape[0]
    if "nc" not in _CACHE:
        _CACHE["nc"] = build_program()[0]
    nc = _CACHE["nc"]
    in_maps = []
    for b in range(nb):
        m = dict(shared)
        m["x"] = np.ascontiguousarray(x[b])
        in_maps.append(m)
    res = run_bass_kernel_spmd(nc, in_maps, core_ids=list(range(nb)))
    return np.stack([np.asarray(r["y"], np.float32) for r in res.results], axis=0)
```

```python
import math
import numpy as np
import concourse.bass as bass
import concourse.mybir as mybir
from concourse.bass_utils import run_bass_kernel_spmd
from contextlib import ExitStack

F32 = mybir.dt.float32
BF16 = mybir.dt.bfloat16
U32 = mybir.dt.uint32
I32 = mybir.dt.int32
AF = mybir.ActivationFunctionType
ALU = mybir.AluOpType
AX = mybir.AxisListType

D = 1024
S = 4096
L = 2
NTB = S // 128
ALPHA = (2.0 * L) ** 0.25
LN_EPS = 1e-5
RMS_EPS = 1e-6
NEXP = 16384
SAME_SYNC = True
SKIP_ARG = 130.0


class Op:
    __slots__ = ("eng", "fn", "deps", "is_dma", "sem", "count", "signals", "waits", "idx", "barrier")


class Sched:
    def __init__(self, nc, stack, same_engine_sync=True):
        self.nc = nc
        self.stack = stack
        self.ops = []
        self.last_w = {}
        self.readers = {}
        self.same = same_engine_sync
        self.semh = {}

    def sem(self, key):
        if key not in self.semh:
            self.semh[key] = self.stack.enter_context(self.nc.semaphore("s_%d" % len(self.semh)))
        return self.semh[key]

    def add(self, eng, fn, reads=(), writes=(), dma=None):
        op = Op()
        op.eng = eng
        op.fn = fn
        op.barrier = False
        op.is_dma = dma is not None
        op.sem = ("dma", dma) if dma is not None else ("eng", eng)
        op.signals = op.is_dma
        op.count = 0
        op.idx = len(self.ops)
        deps = set()
        for r in reads:
            w = self.last_w.get(r)
            if w is not None:
                deps.add(w)
        for w_ in writes:
            w = self.last_w.get(w_)
            if w is not None:
                deps.add(w)
            for rd in self.readers.get(w_, ()):
                deps.add(rd)
        op.deps = deps
        for r in reads:
            self.readers.setdefault(r, []).append(op.idx)
        for w_ in writes:
            self.last_w[w_] = op.idx
            self.readers[w_] = []
        self.ops.append(op)
        return op

    def barrier(self, exclude=("cv",)):
        last = {}
        for op in self.ops:
            if not op.barrier and not op.is_dma:
                last[op.eng] = op
        for op in last.values():
            op.signals = True
        for eng in ("sp", "act", "dve", "pool", "pe"):
            op = Op()
            op.eng = eng
            op.fn = None
            op.barrier = True
            op.is_dma = False
            op.sem = None
            op.signals = False
            op.count = 0
            op.idx = len(self.ops)
            op.deps = set()
            op.waits = tuple(("dma", k) for k in exclude)
            self.ops.append(op)
        keep = {k: v for k, v in self.last_w.items() if self.ops[v].is_dma and self.ops[v].sem[1] in exclude}
        self.last_w = keep
        self.readers = {}

    def _skip(self, dop, op):
        return (not dop.is_dma) and dop.eng == op.eng and (not op.is_dma) and (dop.eng == "pe" or not self.same)

    def finalize(self, block):
        ops = self.ops
        for op in ops:
            for d in op.deps:
                dop = ops[d]
                if dop.is_dma or self._skip(dop, op):
                    continue
                dop.signals = True
        cnt = {}
        for op in ops:
            if op.barrier:
                op.count = dict(cnt)
                continue
            if op.signals:
                inc = 16 if op.is_dma else 1
                cnt[op.sem] = cnt.get(op.sem, 0) + inc
                op.count = cnt[op.sem]
        waited = {}
        for op in ops:
            w = waited.setdefault(op.eng, {})
            if op.barrier:
                excl = op.waits
                need = {s: v for s, v in op.count.items() if s != ("eng", op.eng) and s not in excl}
                op.waits = []
            else:
                op.waits = []
                need = {}
                for d in op.deps:
                    dop = ops[d]
                    if self._skip(dop, op):
                        continue
                    if need.get(dop.sem, 0) < dop.count:
                        need[dop.sem] = dop.count
            for s, v in need.items():
                if w.get(s, 0) < v:
                    w[s] = v
                    op.waits.append((s, v))
        final_waits = [(s, v) for s, v in cnt.items() if s[0] == "dma"]
        for s in cnt:
            self.sem(s)
        per_eng = {}
        for op in ops:
            per_eng.setdefault(op.eng, []).append(op)

        def emit(engname, eng_obj, final=False):
            for op in per_eng.get(engname, []):
                for s, v in op.waits:
                    eng_obj.wait_ge(self.sem(s), v)
                if op.fn is None:
                    continue
                ins = op.fn(eng_obj)
                if op.signals:
                    ins.then_inc(self.sem(op.sem), 16 if op.is_dma else 1)
            if final:
                for s, v in final_waits:
                    eng_obj.wait_ge(self.sem(s), v)

        @block.sync
        def _(e):
            emit("sp", e, final=True)

        @block.scalar
        def _(e):
            emit("act", e)

        @block.vector
        def _(e):
            emit("dve", e)

        @block.gpsimd
        def _(e):
            emit("pool", e)

        @block.tensor
        def _(e):
            emit("pe", e)
        return len(ops)


class Arena:
    def __init__(self, tensor, ncols):
        self.t = tensor
        self.n = ncols
        self.off = 0

    def reset(self):
        self.off = 0

    def alloc(self, shape, dt):
        n = 1
        for s_ in shape:
            n *= s_
        if dt == BF16:
            ncol = (n + 1) // 2
        else:
            ncol = n
        ncol = (ncol + 15) // 16 * 16
        assert self.off + ncol <= self.n, ("arena overflow", self.off, ncol, self.n)
        v = self.t[:, self.off:self.off + ncol]
        self.off += ncol
        if dt != F32:
            v = v.bitcast(dt)
        v = v[:, 0:n]
        if len(shape) == 2:
            v = v.rearrange("p (a b) -> p a b", a=shape[0])
        elif len(shape) == 3:
            v = v.rearrange("p (a b c) -> p a b c", a=shape[0], b=shape[1])
        return v


def build_program(n_layers=L, debug=False, phases=("A0", "A1", "A2", "A3", "B")):
    nc = bass.Bass("TRN2", target_bir_lowering=False)

    def din(name, shape, dt=F32):
        return nc.dram_tensor(name, list(shape), dt, kind="ExternalInput").ap()

    def dscr(name, shape, dt):
        return nc.dram_tensor(name, list(shape), dt, kind="ExternalOutput" if debug else "Internal").ap()

    x_d = din("x", [S, D])
    win_d = din("w_in", [L, 56, 128, 1024])
    wba_d = din("w_br_attn", [L, 128, 8, 1024])
    wbl_d = din("w_br_lru", [L, 128, 8, 1024])
    wo_d = din("w_out", [L, 128, 8, 1024])
    wq_d = din("peer_wq", [L, 16, 128, 1024])
    sk_d = din("peer_skT", [L, 128, 16 * 128])
    pu_ds = [din("peer_u%d" % i, [NEXP, D]) for i in range(L)]
    pv_ds = [din("peer_v%d" % i, [NEXP, D]) for i in range(L)]
    iota_d = din("iota256", [256])
    gaw_d = din("gate_a_w", [L, 128, 8 * 128])
    gxw_d = din("gate_x_w", [L, 128, 8 * 128])
    chp_d = din("chp", [L, 128, 64])
    lq_d = din("lambda_qk", [L, 256])
    sg_d = din("subln_g", [L, 128])
    ln1g_d = din("ln1_g", [L, D])
    ln1b_d = din("ln1_b", [L, D])
    ln2g_d = din("ln2_g", [L, D])
    ln2b_d = din("ln2_b", [L, D])
    augk_d = din("aug_k", [3, 128])
    augq_d = din("aug_q", [3, 8 * 512])
    y_d = nc.dram_tensor("y", [S, D], F32, kind="ExternalOutput").ap()

    xT_d = dscr("xT_s", [D, S], BF16)
    yaT_d = dscr("yaT_s", [D, S], BF16)
    yrT_d = dscr("yrT_s", [D, S], BF16)
    x1_d = dscr("x1_s", [S, D], F32)
    x2_d = dscr("x2_s", [S, D], F32)
    tb16 = [nc.dram_tensor("tb16_%d" % i, [NEXP, 2 * D], BF16, kind="Internal").ap() for i in range(L)]

    with ExitStack() as st:
        ARN = 50000
        arena_t = st.enter_context(nc.sbuf_tensor("arena", [128, ARN], F32))
        cst_t = st.enter_context(nc.sbuf_tensor("cst", [128, 3200], F32))
        pf = [st.enter_context(nc.psum_tensor("pf%d" % i, [128, 512], F32)) for i in range(7)]
        pb = st.enter_context(nc.psum_tensor("pbb", [128, 1024], BF16))
        block = st.enter_context(nc.Block())
        SC = Sched(nc, st, same_engine_sync=SAME_SYNC)
        AR = Arena(arena_t, ARN)
        CA = Arena(cst_t, 3200)

        def DMA(out, in_, reads, writes, key, q="sp"):
            SC.add(q, lambda e: e.dma_start(out=out, in_=in_), reads, writes, dma=key)

        def MM(out, lhsT, rhs, start, stop, reads, writes):
            SC.add("pe", lambda e: e.matmul(out, lhsT=lhsT, rhs=rhs, start=start, stop=stop), reads, writes)

        def TR(out, in_, reads, writes):
            SC.add("pe", lambda e: e.transpose(out=out, in_=in_, identity=ident), list(reads) + ["ident"], writes)

        def ACT(out, in_, func, reads, writes, bias=None, scale=None, accum=None):
            kw = {}
            if bias is not None:
                kw["bias"] = bias
            if scale is not None:
                kw["scale"] = scale
            if accum is not None:
                kw["accum_out"] = accum
            SC.add("act", lambda e: e.activation(out=out, in_=in_, func=func, **kw), reads, writes)

        def CP(eng, out, in_, reads, writes):
            if eng == "act":
                SC.add("act", lambda e: e.activation(out=out, in_=in_, func=AF.Copy), reads, writes)
            else:
                SC.add(eng, lambda e: e.tensor_copy(out=out, in_=in_), reads, writes)

        def TT(eng, out, in0, in1, op, reads, writes):
            SC.add(eng, lambda e: e.tensor_tensor(out=out, in0=in0, in1=in1, op=op), reads, writes)

        def TS(eng, out, in0, s1, s2, op0, op1, reads, writes, accum=None):
            if accum is None:
                if s2 is None:
                    SC.add(eng, lambda e: e.tensor_scalar(out=out, in0=in0, scalar1=s1, scalar2=None, op0=op0), reads, writes)
                else:
                    SC.add(eng, lambda e: e.tensor_scalar(out=out, in0=in0, scalar1=s1, scalar2=s2, op0=op0, op1=op1), reads, writes)
            else:
                SC.add(eng, lambda e: e.tensor_scalar(out=out, in0=in0, scalar1=s1, scalar2=s2, op0=op0, op1=op1, accum_out=accum), reads, writes)

        def STT(out, in0, scalar, in1, op0, op1, reads, writes, accum=None):
            if accum is None:
                SC.add("dve", lambda e: e.scalar_tensor_tensor(out=out, in0=in0, scalar=scalar, in1=in1, op0=op0, op1=op1), reads, writes)
            else:
                SC.add("dve", lambda e: e.scalar_tensor_tensor(out=out, in0=in0, scalar=scalar, in1=in1, op0=op0, op1=op1, accum_out=accum), reads, writes)

        def MEMSET(eng, out, val, writes):
            SC.add(eng, lambda e: e.memset(out, val), (), writes)

        identf = CA.alloc([128], F32)
        ident = CA.alloc([128], BF16)
        trif = CA.alloc([128], F32)
        tri = CA.alloc([128], BF16)
        augk_f = CA.alloc([128], F32)
        augq_f = AR.alloc([8 * 512], F32)
        augk = CA.alloc([128], BF16)
        augq = CA.alloc([8, 512], BF16)
        MEMSET("pool", identf, 1.0, ["identf"])
        SC.add("pool", lambda e: e.affine_select(out=identf, in_=identf, pattern=[[-1, 128]], compare_op=ALU.is_equal,
                                                 fill=0.0, base=0, channel_multiplier=1), ["identf"], ["identf"])
        CP("dve", ident, identf, ["identf"], ["ident"])
        MEMSET("pool", trif, 1.0, ["trif"])
        SC.add("pool", lambda e: e.affine_select(out=trif, in_=trif, pattern=[[1, 128]], compare_op=ALU.is_ge,
                                                 fill=0.0, base=0, channel_multiplier=-1), ["trif"], ["trif"])
        CP("dve", tri, trif, ["trif"], ["tri"])
        zrow = CA.alloc([512], BF16)
        MEMSET("pool", zrow, 0.0, ["zrow"])
        iota = CA.alloc([256], F32)
        DMA(iota, iota_d.partition_broadcast(128), [], ["iota"], "c2")
        DMA(augk_f[64:67, :], augk_d, [], ["augk_f"], "c0")
        DMA(augq_f[64:67, :], augq_d, [], ["augq_f"], "c1")
        CP("dve", augk[64:67, :], augk_f[64:67, :], ["augk_f"], ["augk"])
        CP("dve", augq[64:67, :, :], augq_f[64:67, :].rearrange("p (a b) -> p a b", a=8), ["augq_f"], ["augq"])

        def conv_dma(dst, src, key):
            SC.add("pool", lambda e: e.dma_start(out=dst, in_=src), ["xT"], [key], dma="cv")

        conv_chunks = []
        for l_ in range(n_layers):
            for (src, c0) in ((pu_ds[l_], 0), (pv_ds[l_], D)):
                for ch in range(4):
                    conv_chunks.append((tb16[l_][ch * 4096:(ch + 1) * 4096, c0:c0 + D], src[ch * 4096:(ch + 1) * 4096, :], ("tb16", l_)))

        def emit_conversion(n):
            for _ in range(n):
                if conv_chunks and "B" in phases:
                    conv_dma(*conv_chunks.pop(0))

        def layernorm(y, g_bc, b_bc, out, wk, rkeys, wkeys, tagk):
            stt, mv, lnv, rstd = wk["st"], wk["mv"], wk["lnv"], wk["rstd"]
            SC.add("dve", lambda e: e.bn_stats(out=stt[:, 0:6], in_=y[:, 0:512]), rkeys, [tagk + "st"])
            SC.add("dve", lambda e: e.bn_stats(out=stt[:, 6:12], in_=y[:, 512:1024]), rkeys, [tagk + "st"])
            SC.add("dve", lambda e: e.bn_aggr(out=mv, in_=stt), [tagk + "st"], [tagk + "mv"])
            ACT(lnv, mv[:, 1:2], AF.Ln, [tagk + "mv", "eps"], [tagk + "lnv"], bias=wk["eps"])
            ACT(rstd, lnv, AF.Exp, [tagk + "lnv"], [tagk + "rstd"], scale=-0.5)
            TS("dve", y, y, mv[:, 0:1], rstd, ALU.subtract, ALU.mult, list(rkeys) + [tagk + "mv", tagk + "rstd"], wkeys_y(rkeys))
            TT("pool", y, y, g_bc, ALU.mult, list(rkeys) + ["lnp"], wkeys_y(rkeys))
            TT("pool", out, y, b_bc, ALU.add, list(rkeys) + ["lnp"], wkeys)

        def wkeys_y(rkeys):
            return list(rkeys)

        for l in range(n_layers):
            lam_init = 0.8 - 0.6 * math.exp(-0.3 * l)
            x_src = x_d if l == 0 else x2_d
            x_dst = y_d if l == n_layers - 1 else x2_d
            xsrc_key = "x2d"
            SC.barrier()
            AR.reset()
            xT = AR.alloc([8, S], BF16)
            a1_mark = AR.off
            if "A0" in phases:
                xs = [AR.alloc([D], F32) for _ in range(2)]
                xb = [AR.alloc([D], BF16) for _ in range(2)]
                for tb in range(NTB):
                    b = tb % 2
                    DMA(xs[b], x_src[tb * 128:(tb + 1) * 128, :], [xsrc_key], [("xs", b)], "xs%d" % b)
                    CP("act", xb[b], xs[b], [("xs", b)], [("xb", b)])
                    for dc in range(8):
                        TR(pb[:, dc * 128:(dc + 1) * 128], xb[b][:, dc * 128:(dc + 1) * 128], [("xb", b)], ["pb"])
                    CP("dve", xT[:, :, tb * 128:(tb + 1) * 128], pb[:, :].rearrange("p (a b) -> p a b", a=8), ["pb"], ["xT"])
                for dc in range(8):
                    DMA(xT_d[dc * 128:(dc + 1) * 128, :], xT[:, dc, :], ["xT"], ["xTd"], "xTd")
                if "A1" not in phases:
                    emit_conversion(len(conv_chunks))

            if "A1" in phases:
                AR.off = a1_mark
                wst = [AR.alloc([3, 1024], F32) for _ in range(2)]
                wbf = [AR.alloc([3, 1024], BF16) for _ in range(2)]
                qT2 = AR.alloc([2, S], BF16)
                kT2 = AR.alloc([2, S], BF16)
                Va = AR.alloc([NTB, 130], BF16)
                NE = 6
                Eb = [AR.alloc([512], BF16) for _ in range(NE)]
                Osb = AR.alloc([4, 512], F32)
                obuf = AR.alloc([4, 128], F32)
                junk = AR.alloc([128], F32)
                yab = AR.alloc([4, 128], BF16)
                yst = [AR.alloc([512], BF16) for _ in range(2)]
                lq = AR.alloc([256], F32)
                sgb = AR.alloc([128], F32)
                gsc = AR.alloc([128], F32)
                sm = AR.alloc([32], F32)
                neglam = sm[:, 0:1]
                s12 = sm[:, 1:3]
                e12 = sm[:, 3:5]
                rz = sm[:, 8:16]
                rz2l = sm[:, 16:20]
                ss = sm[:, 20:24]
                lnv4 = sm[:, 24:28]
                rstd4 = sm[:, 28:32]
                epsr = AR.alloc([1], F32)
                MEMSET("pool", epsr, RMS_EPS, ["epsr"])
                MEMSET("pool", Va[:, :, 128:130], 1.0, ["Va1"])
                for c in range(2):
                    CP("pool", kT2[64:67, c, :].rearrange("p (a b) -> p a b", a=NTB),
                       augk[64:67, :].unsqueeze(1).broadcast_to([3, NTB, 128]), ["augk"], ["kT"])
                DMA(lq, lq_d[l].partition_broadcast(128), [], ["lq"], "p0")
                DMA(sgb, sg_d[l].partition_broadcast(128), [], ["sgb"], "p1")
                STT(junk[:, 0:64], lq[:, 0:64], 1.0, lq[:, 64:128], ALU.mult, ALU.mult, ["lq"], ["junk", "s1"], accum=s12[:, 0:1])
                STT(junk[:, 0:64], lq[:, 128:192], 1.0, lq[:, 192:256], ALU.mult, ALU.mult, ["lq"], ["junk", "s2"], accum=s12[:, 1:2])
                ACT(e12, s12, AF.Exp, ["s1", "s2"], ["e12"])
                TT("dve", neglam, e12[:, 1:2], e12[:, 0:1], ALU.subtract, ["e12"], ["neglam"])
                TS("dve", neglam, neglam, -lam_init, None, ALU.add, None, ["neglam"], ["neglam"])
                TS("dve", gsc, sgb, 1.0 - lam_init, None, ALU.mult, None, ["sgb"], ["gsc"])

                def load_w(h, slot):
                    for i, cb in enumerate((h, 8 + h, 16 + h)):
                        DMA(wst[slot][:, i, :], win_d[l, cb], [], [("wst", slot)], "wst%d" % slot)
                    CP("dve", wbf[slot], wst[slot], [("wst", slot)], [("wbf", slot)])

                def epilogue2(h, j):
                    SC.add("dve", lambda e: e.reciprocal(out=rz.rearrange("p (a b) -> p a b", a=4),
                                                         in_=Osb[:, :, 128:512:256]), ["Osb"], ["rz"])
                    TS("dve", rz2l, rz[:, 4:8], neglam, None, ALU.mult, None, ["rz", "neglam"], ["rz2l"])
                    for qs in range(4):
                        o1 = Osb[:, qs // 2, (qs % 2) * 256:(qs % 2) * 256 + 128]
                        o2 = Osb[:, 2 + qs // 2, (qs % 2) * 256:(qs % 2) * 256 + 128]
                        TS("dve", obuf[:, qs, :], o1, rz[:, qs:qs + 1], None, ALU.mult, None, ["Osb", "rz"], ["obuf"])
                        STT(obuf[:, qs, :], o2, rz2l[:, qs:qs + 1], obuf[:, qs, :], ALU.mult, ALU.add, ["Osb", "rz2l", "obuf"], ["obuf"])
                        STT(junk, obuf[:, qs, :], 1.0, obuf[:, qs, :], ALU.mult, ALU.mult, ["obuf"], ["junk", "ss"], accum=ss[:, qs:qs + 1])
                    ACT(lnv4, ss, AF.Ln, ["ss"], ["lnv4"], scale=1.0 / 128.0, bias=epsr)
                    ACT(rstd4, lnv4, AF.Exp, ["lnv4"], ["rstd4"], scale=-0.5)
                    for qs in range(4):
                        STT(yab[:, qs, :], obuf[:, qs, :], rstd4[:, qs:qs + 1], gsc, ALU.mult, ALU.mult, ["obuf", "rstd4", "gsc"], ["yab"])
                    for qs in range(4):
                        TR(pb[:, qs * 128:(qs + 1) * 128], yab[:, qs, :], ["yab"], ["pb"])
                    ys = yst[j % 2]
                    CP("dve", ys, pb[:, 0:512], ["pb"], [("yst", j % 2)])
                    DMA(yaT_d[h * 128:(h + 1) * 128, j * 512:(j + 1) * 512], ys, [("yst", j % 2)], ["yaTd"], "yst%d" % (j % 2))

                load_w(0, 0)
                ei = 0
                si = 0
                for h in range(8):
                    slot = h % 2
                    if h + 1 < 8:
                        load_w(h + 1, 1 - slot)
                    emit_conversion(2)
                    slope = 2.0 ** (-(h + 1))
                    for c in range(2):
                        CP("dve", qT2[64:67, c, :].rearrange("p (a b) -> p a b", a=8),
                           augq[64:67, h, :].unsqueeze(1).broadcast_to([3, 8, 512]), ["augq"], ["qT"])
                    for tq in range(8):
                        for (wi, dst, key) in ((0, qT2, "qT"), (1, kT2, "kT")):
                            for dc in range(8):
                                MM(pf[6][:, :], wbf[slot][:, wi, dc * 128:(dc + 1) * 128], xT[:, dc, tq * 512:(tq + 1) * 512],
                                   dc == 0, dc == 7, [("wbf", slot), "xT"], ["pf6"])
                            CP("dve", dst[0:64, 0, tq * 512:(tq + 1) * 512], pf[6][0:64, :], ["pf6"], [key])
                            CP("dve", dst[0:64, 1, tq * 512:(tq + 1) * 512], pf[6][64:128, :], ["pf6"], [key])
                    for tb4 in range(8):
                        for t in range(4):
                            tb = tb4 * 4 + t
                            for dc in range(8):
                                MM(pf[6][:, t * 128:(t + 1) * 128], xT[:, dc, tb * 128:(tb + 1) * 128],
                                   wbf[slot][:, 2, dc * 128:(dc + 1) * 128], dc == 0, dc == 7, [("wbf", slot), "xT"], ["pf6"])
                        CP("dve", Va[:, tb4 * 4:(tb4 + 1) * 4, 0:128], pf[6][:, :].rearrange("p (a b) -> p a b", a=4), ["pf6"], ["Va"])
                    def live(j, kb):
                        return slope * (j * 512 - kb * 128 - 127) <= SKIP_ARG
                    tiles = [(j, c, kb) for j in range(8) for c in range(2) for kb in range(4 * j + 4) if live(j, kb)]
                    first_of_j = {}
                    for (j_, c_, kb_) in tiles:
                        first_of_j.setdefault(j_, (c_, kb_))
                    sinfo = {}

                    def emit_S(i):
                        nonlocal si
                        j, c, kb = tiles[i]
                        r = kb - 4 * j
                        nq0 = max(0, r) * 128
                        sb_ = pf[4 + si % 2]
                        skey = ("S", si % 2)
                        si += 1
                        MM(sb_[:, nq0:512], kT2[0:67, c, kb * 128:(kb + 1) * 128],
                           qT2[0:67, c, j * 512 + nq0:(j + 1) * 512], True, True, ["qT", "kT"], [skey])
                        sinfo[i] = (sb_, skey)

                    pending = []

                    def emit_rest(i):
                        nonlocal ei
                        j, c, kb = tiles[i]
                        r = kb - 4 * j
                        nq0 = max(0, r) * 128
                        sb_, skey = sinfo.pop(i)
                        if (c, kb) == first_of_j[j]:
                            for bnk in range(4):
                                MM(pf[bnk][:, :], zrow[0:1, 0:128], zrow[0:1, 0:512], True, False, ["zrow"], [("O", bnk)])
                        E = Eb[ei % NE]
                        ekey = ("E", ei % NE)
                        ei += 1
                        ACT(E[:, nq0:512], sb_[:, nq0:512], AF.Exp, [skey], [ekey], scale=0.125,
                            bias=float(slope * (kb * 128 - j * 512)))
                        if r >= 0:
                            TT("dve", E[:, r * 128:(r + 1) * 128], E[:, r * 128:(r + 1) * 128], tri, ALU.mult, [ekey, "tri"], [ekey])
                        for qs in range(max(0, r), 4):
                            ob = pf[c * 2 + qs // 2]
                            MM(ob[:, (qs % 2) * 256:(qs % 2) * 256 + 129], E[:, qs * 128:(qs + 1) * 128], Va[:, kb, 0:129],
                               False, kb == 4 * j + qs, [ekey, "Va", "Va1"], [("O", c * 2 + qs // 2)])
                        if c == 1 and kb == 4 * j + 3:
                            for bnk in range(4):
                                CP("dve", Osb[:, bnk, :], pf[bnk][:, :], [("O", bnk)], ["Osb"])
                            pending.append((i + 5, h, j))
                        while pending and (pending[0][0] <= i or i == len(tiles) - 1):
                            _, hh, jj = pending.pop(0)
                            epilogue2(hh, jj)

                    emit_S(0)
                    for i in range(len(tiles)):
                        if i + 1 < len(tiles):
                            emit_S(i + 1)
                        emit_rest(i)

            if "A2" in phases:
                SC.barrier()
                AR.off = a1_mark
                B0 = AR.alloc([S + 16], F32)
                B1 = AR.alloc([S], F32)
                B2 = AR.alloc([S], F32)
                B3 = AR.alloc([S], F32)
                xcb = AR.alloc([S], BF16)
                Yb = AR.alloc([S], BF16)
                wst2 = [AR.alloc([2, 1024], F32) for _ in range(2)]
                wbf2 = [AR.alloc([2, 1024], BF16) for _ in range(2)]
                gwf = AR.alloc([2, 1024], F32)
                gwb = AR.alloc([2, 8, 128], BF16)
                chp = AR.alloc([8, 8], F32)
                cc = AR.alloc([8, 4], F32)
                DMA(gwf[:, 0, :], gaw_d[l], [], ["gwf"], "p2")
                DMA(gwf[:, 1, :], gxw_d[l], [], ["gwf"], "p2")
                CP("dve", gwb, gwf.rearrange("p a (g j) -> p a g j", g=8), ["gwf"], ["gwb"])
                DMA(chp, chp_d[l].rearrange("p (g f) -> p g f", g=8), [], ["chp"], "p3")
                ACT(cc[:, :, 2], chp[:, :, 7], AF.Exp, ["chp"], ["cc"], scale=-1.0)
                ACT(cc[:, :, 3], cc[:, :, 2], AF.Ln, ["cc"], ["cc"], bias=1.0)
                TS("dve", cc[:, :, 0], cc[:, :, 3], -8.0, None, ALU.mult, None, ["cc"], ["cc"])
                TS("dve", cc[:, :, 1], cc[:, :, 3], -16.0, None, ALU.mult, None, ["cc"], ["cc"])

                def load_w2(g, slot):
                    DMA(wst2[slot][:, 0, :], win_d[l, 24 + g], [], [("wst2", slot)], "wst2%d" % slot)
                    DMA(wst2[slot][:, 1, :], win_d[l, 32 + g], [], [("wst2", slot)], "wst2%d" % slot)
                    CP("pool", wbf2[slot], wst2[slot], [("wst2", slot)], [("wbf2", slot)])

                load_w2(0, 0)
                pi = 0
                for g in range(8):
                    slot = g % 2
                    if g + 1 < 8:
                        load_w2(g + 1, 1 - slot)
                    MEMSET("pool", B0[:, 0:3], 0.0, ["B0"])
                    for tq in range(8):
                        pp = pf[pi % 7]
                        pk = ("pf", pi % 7)
                        pi += 1
                        for dc in range(8):
                            MM(pp[:, :], wbf2[slot][:, 0, dc * 128:(dc + 1) * 128], xT[:, dc, tq * 512:(tq + 1) * 512], dc == 0, dc == 7,
                               [("wbf2", slot), "xT"], [pk])
                        CP("act", B0[:, 3 + tq * 512:3 + (tq + 1) * 512], pp[:, :], [pk], ["B0"])
                    TS("dve", B1, B0[:, 0:S], chp[:, g, 0:1], chp[:, g, 4:5], ALU.mult, ALU.add, ["B0", "chp"], ["B1"])
                    for k in range(1, 4):
                        STT(B1, B0[:, k:k + S], chp[:, g, k:k + 1], B1, ALU.mult, ALU.add, ["B0", "chp", "B1"], ["B1"])
                    CP("pool", xcb, B1, ["B1"], ["xcb"])
                    for (wi, dst, off, key, bcol) in ((0, B2, 0, "B2", 5), (1, B0, 3, "B0", 6)):
                        for tq in range(8):
                            pp = pf[pi % 7]
                            pk = ("pf", pi % 7)
                            pi += 1
                            MM(pp[:, :], gwb[:, wi, g, :], xcb[:, tq * 512:(tq + 1) * 512], True, True, ["gwb", "xcb"], [pk])
                            ACT(dst[:, off + tq * 512:off + (tq + 1) * 512], pp[:, :], AF.Sigmoid, [pk, "chp", "B1"], [key],
                                bias=chp[:, g, bcol:bcol + 1])
                    ACT(B3, B2, AF.Exp, ["B2", "cc"], ["B3"], scale=cc[:, g, 1:2])
                    ACT(B3, B3, AF.Sqrt, ["B3"], ["B3"], scale=-1.0, bias=1.0)
                    MEMSET("pool", B3[:, 0:1], 1.0, ["B3"])
                    ACT(B2, B2, AF.Exp, ["B2", "cc"], ["B2"], scale=cc[:, g, 0:1])
                    TT("dve", B0[:, 3:3 + S], B0[:, 3:3 + S], B1, ALU.mult, ["B0", "B1"], ["B0"])
                    TT("dve", B0[:, 3:3 + S], B0[:, 3:3 + S], B3, ALU.mult, ["B0", "B3"], ["B0"])
                    SC.add("dve", lambda e: e.tensor_tensor_scan(out=B3, data0=B2, data1=B0[:, 3:3 + S], initial=0.0,
                                                                 op0=ALU.mult, op1=ALU.add), ["B2", "B0", "B3"], ["B3"])
                    for tq in range(8):
                        pp = pf[pi % 7]
                        pk = ("pf", pi % 7)
                        pi += 1
                        for dc in range(8):
                            MM(pp[:, :], wbf2[slot][:, 1, dc * 128:(dc + 1) * 128], xT[:, dc, tq * 512:(tq + 1) * 512], dc == 0, dc == 7,
                               [("wbf2", slot), "xT"], [pk])
                        ACT(B1[:, tq * 512:(tq + 1) * 512], pp[:, :], AF.Gelu_apprx_tanh, [pk, "B0"], ["B1"])
                    TT("dve", Yb, B3, B1, ALU.mult, ["B3", "B1", "yrTd"], ["Yb"])
                    DMA(yrT_d[g * 128:(g + 1) * 128, :], Yb, ["Yb"], ["yrTd"], "yrTd")

            if "A3" in phases:
                SC.barrier()
                AR.reset()
                wg = AR.alloc([16, 8, 128], BF16)
                wba = AR.alloc([8, 1024], BF16)
                wbl = AR.alloc([8, 1024], BF16)
                wo = AR.alloc([8, 1024], BF16)
                stg = [AR.alloc([1024], F32) for _ in range(2)]
                lng = AR.alloc([D], F32)
                lnb = AR.alloc([D], F32)
                DMA(lng, ln1g_d[l].partition_broadcast(128), [], ["lnp"], "p4")
                DMA(lnb, ln1b_d[l].partition_broadcast(128), [], ["lnp"], "p4")
                si_ = 0
                for cb in range(16):
                    b = si_ % 2
                    si_ += 1
                    DMA(stg[b], win_d[l, 40 + cb], [], [("stg", b)], "stg%d" % b)
                    CP("dve" if cb % 2 else "pool", wg[:, cb, :, :], stg[b].rearrange("p (a b) -> p a b", a=8), [("stg", b)], ["wg"])
                for (src, dst, key) in ((wba_d, wba, "wba"), (wbl_d, wbl, "wbl"), (wo_d, wo, "wo")):
                    for kc in range(8):
                        b = si_ % 2
                        si_ += 1
                        DMA(stg[b], src[l, :, kc, :], [], [("stg", b)], "stg%d" % b)
                        CP("dve" if kc % 2 else "pool", dst[:, kc, :], stg[b], [("stg", b)], [key])
                xTb = [AR.alloc([8, 512], BF16) for _ in range(2)]
                yaTb = [AR.alloc([8, 512], BF16) for _ in range(2)]
                yrTb = [AR.alloc([8, 512], BF16) for _ in range(2)]
                mT = AR.alloc([8, 512], BF16)
                sga = [AR.alloc([512], F32) for _ in range(2)]
                sgr = [AR.alloc([512], F32) for _ in range(2)]
                xres = [AR.alloc([D], F32) for _ in range(2)]
                yln = [AR.alloc([D], F32) for _ in range(2)]
                wk = {"st": AR.alloc([12], F32), "mv": AR.alloc([2], F32), "lnv": AR.alloc([1], F32),
                      "rstd": AR.alloc([1], F32), "eps": AR.alloc([1], F32)}
                MEMSET("pool", wk["eps"], LN_EPS, ["eps"])
                ti = 0
                for tq in range(8):
                    b = tq % 2
                    DMA(xTb[b], xT_d[:, tq * 512:(tq + 1) * 512].rearrange("(a p) t -> p a t", p=128), ["xTd"], [("xTb", b)], "xTb%d" % b)
                    DMA(yaTb[b], yaT_d[:, tq * 512:(tq + 1) * 512].rearrange("(a p) t -> p a t", p=128), ["yaTd"], [("yaTb", b)], "yaTb%d" % b)
                    DMA(yrTb[b], yrT_d[:, tq * 512:(tq + 1) * 512].rearrange("(a p) t -> p a t", p=128), ["yrTd"], [("yrTb", b)], "yrTb%d" % b)
                    for jb in range(8):
                        for dc in range(8):
                            MM(pf[0][:, :], wg[:, jb, dc, :], xTb[b][:, dc, :], dc == 0, dc == 7, ["wg", ("xTb", b)], ["pf0"])
                        for dc in range(8):
                            MM(pf[1][:, :], wg[:, 8 + jb, dc, :], xTb[b][:, dc, :], dc == 0, dc == 7, ["wg", ("xTb", b)], ["pf1"])
                        for kc in range(8):
                            MM(pf[2][:, :], wba[:, kc, jb * 128:(jb + 1) * 128], yaTb[b][:, kc, :], kc == 0, kc == 7, ["wba", ("yaTb", b)], ["pf2"])
                        for kc in range(8):
                            MM(pf[3][:, :], wbl[:, kc, jb * 128:(jb + 1) * 128], yrTb[b][:, kc, :], kc == 0, kc == 7, ["wbl", ("yrTb", b)], ["pf3"])
                        sb2 = jb % 2
                        ACT(sga[sb2], pf[0][:, :], AF.Sigmoid, ["pf0"], [("sga", sb2)])
                        ACT(sgr[sb2], pf[1][:, :], AF.Sigmoid, ["pf1"], [("sgr", sb2)])
                        TT("dve", sga[sb2], sga[sb2], pf[2][:, :], ALU.mult, [("sga", sb2), "pf2"], [("sga", sb2)])
                        TT("dve", sgr[sb2], sgr[sb2], pf[3][:, :], ALU.mult, [("sgr", sb2), "pf3"], [("sgr", sb2)])
                        TT("pool", mT[:, jb, :], sga[sb2], sgr[sb2], ALU.add, [("sga", sb2), ("sgr", sb2)], ["mT"])
                    for ts_ in range(4):
                        tb = tq * 4 + ts_
                        xb_ = ti % 2
                        ti += 1
                        DMA(xres[xb_], x_src[tb * 128:(tb + 1) * 128, :], [xsrc_key], [("xres", xb_)], "xres%d" % xb_)
                        for nh in range(2):
                            for jb in range(8):
                                MM(pf[4 + nh][:, :], mT[:, jb, ts_ * 128:(ts_ + 1) * 128], wo[:, jb, nh * 512:(nh + 1) * 512], jb == 0, jb == 7,
                                   ["mT", "wo"], [("pf", 4 + nh)])
                            STT(yln[xb_][:, nh * 512:(nh + 1) * 512], xres[xb_][:, nh * 512:(nh + 1) * 512], ALPHA, pf[4 + nh][:, :],
                                ALU.mult, ALU.add, [("xres", xb_), ("pf", 4 + nh)], [("yln", xb_)])
                        layernorm(yln[xb_], lng, lnb, yln[xb_], wk, [("yln", xb_)], [("yln", xb_)], "ln")
                        DMA(x1_d[tb * 128:(tb + 1) * 128, :], yln[xb_], [("yln", xb_)], ["x1d"], "x1st%d" % xb_)

            if "B" in phases:
                SC.barrier()
                AR.reset()
                wq = AR.alloc([16, 8, 128], BF16)
                skT = AR.alloc([16, 128], BF16)
                lng = AR.alloc([D], F32)
                lnb = AR.alloc([D], F32)
                bmark = AR.off
                stg = [AR.alloc([2048], F32) for _ in range(2)]
                DMA(lng, ln2g_d[l].partition_broadcast(128), [], ["lnp"], "p4")
                DMA(lnb, ln2b_d[l].partition_broadcast(128), [], ["lnp"], "p4")
                for cb in range(16):
                    b = cb % 2
                    DMA(stg[b][:, 0:1024], wq_d[l, cb], [], [("stg", b)], "stg%d" % b)
                    CP("dve" if cb % 2 else "pool", wq[:, cb, :, :], stg[b][:, 0:1024].rearrange("p (a b) -> p a b", a=8), [("stg", b)], ["wq"])
                DMA(stg[0], sk_d[l], [], [("stg", 0)], "stg0")
                CP("dve", skT, stg[0].rearrange("p (a b) -> p a b", a=16), [("stg", 0)], ["skT"])
                SC.barrier()
                AR.off = bmark
                x1 = [AR.alloc([D], F32) for _ in range(2)]
                x1b = [AR.alloc([D], BF16) for _ in range(2)]
                x1T = AR.alloc([8, 128], BF16)
                qTs = AR.alloc([16, 128], BF16)
                scs = AR.alloc([16, 128], F32)
                top = AR.alloc([16, 16], F32)
                tix = AR.alloc([16, 16], U32)
                tixf = AR.alloc([16, 16], F32)
                work = AR.alloc([256], F32)
                work2 = AR.alloc([256], F32)
                wka = AR.alloc([128], F32)
                wkb = AR.alloc([128], F32)
                cand = AR.alloc([8, 256], F32)
                eid = scs.rearrange("p a b -> p (a b)").rearrange("p (h x) -> p h x", h=8)
                tsv = AR.alloc([8, 16], F32)
                pos = AR.alloc([8, 16], U32)
                posf = AR.alloc([8, 16], F32)
                ef = AR.alloc([128], F32)
                af_ = AR.alloc([128], F32)
                bf_ = AR.alloc([128], F32)
                e1_ = af_
                e2_ = bf_
                eidx = [AR.alloc([128], I32) for _ in range(2)]
                gt = [AR.alloc([8, 16], F32) for _ in range(2)]
                dsm = AR.alloc([8, 16], F32)
                zs = AR.alloc([8], F32)
                actv = AR.alloc([128], F32)
                wgt = AR.alloc([128], F32)
                acc = AR.alloc([D], F32)
                junkb = AR.alloc([D], BF16)
                ND = 8
                diag = [AR.alloc([128], BF16) for _ in range(ND)]
                wk = {"st": AR.alloc([12], F32), "mv": AR.alloc([2], F32), "lnv": AR.alloc([1], F32),
                      "rstd": AR.alloc([1], F32), "eps": AR.alloc([1], F32)}
                NG = 4
                LOOK = 5
                NRB = (LOOK + 1) * NG
                print("phaseB arena before Rb:", AR.off, "NRB", NRB, "ARN", ARN)
                Rb = [AR.alloc([2 * D], BF16) for _ in range(NRB)]
                MEMSET("pool", wk["eps"], LN_EPS, ["eps"])
                tbl = tb16[l]
                tkey = ("tb16", l)
                pacc = [pf[5], pf[6]]
                pbx = [pf[3][:, :].bitcast(BF16), pf[4][:, :].bitcast(BF16)]

                def routing(tb):
                    b = tb % 2
                    DMA(x1[b], x1_d[tb * 128:(tb + 1) * 128, :], ["x1d"], [("x1", b)], "x1ld%d" % b)
                    CP("act", x1b[b], x1[b], [("x1", b)], [("x1b", b)])
                    for dc in range(8):
                        TR(pb[:, dc * 128:(dc + 1) * 128], x1b[b][:, dc * 128:(dc + 1) * 128], [("x1b", b)], ["pb"])
                    CP("act", x1T, pb[:, :].rearrange("p (a b) -> p a b", a=8), ["pb"], ["x1T"])
                    for dc in range(8):
                        TR(pbx[b][:, dc * 128:(dc + 1) * 128], x1T[:, dc, :], ["x1T"], [("pbx", b)])
                    for c4 in range(4):
                        pp = pf[0]
                        pk = ("pf", 0)
                        for ci in range(4):
                            cb = c4 * 4 + ci
                            for dc in range(8):
                                MM(pp[:, ci * 128:(ci + 1) * 128], wq[:, cb, dc, :], x1T[:, dc, :], dc == 0, dc == 7, ["wq", "x1T"], [pk])
                        CP("act", qTs[:, c4 * 4:(c4 + 1) * 4, :], pp[:, :].rearrange("p (a b) -> p a b", a=4), [pk], ["qTs"])
                    for c4 in range(4):
                        pp = pf[1 + c4 % 2]
                        pk = ("pf", 1 + c4 % 2)
                        for ci in range(4):
                            cb = c4 * 4 + ci
                            MM(pp[:, ci * 128:(ci + 1) * 128], qTs[:, cb, :], skT[:, cb, :], True, True, ["qTs", "skT"], [pk])
                        CP("act", scs[:, c4 * 4:(c4 + 1) * 4, :], pp[:, :].rearrange("p (a b) -> p a b", a=4), [pk], ["scs"])
                    for g0 in range(0, 16, 2):
                        gs = (g0, g0 + 1)
                        wk2 = (wka, wkb)
                        for g, w_ in zip(gs, wk2):
                            SC.add("dve", lambda e, g=g: e.max(out=top[:, g, 0:8], in_=scs[:, g, :]), ["scs"], [("top", g)])
                        for g, w_ in zip(gs, wk2):
                            SC.add("dve", lambda e, g=g: e.max_index(out=tix[:, g, 0:8], in_max=top[:, g, 0:8], in_values=scs[:, g, :]),
                                   ["scs", ("top", g)], [("tix", g)])
                        for g, w_ in zip(gs, wk2):
                            SC.add("dve", lambda e, g=g, w_=w_: e.match_replace(out=w_, in_to_replace=top[:, g, 0:8], in_values=scs[:, g, :],
                                                                                 imm_value=-1e30), ["scs", ("top", g)], [("wk1", g % 2)])
                        for g, w_ in zip(gs, wk2):
                            SC.add("dve", lambda e, g=g, w_=w_: e.max(out=top[:, g, 8:16], in_=w_), [("wk1", g % 2)], [("top", g)])
                        for g, w_ in zip(gs, wk2):
                            SC.add("dve", lambda e, g=g, w_=w_: e.max_index(out=tix[:, g, 8:16], in_max=top[:, g, 8:16], in_values=w_),
                                   [("wk1", g % 2), ("top", g)], [("tix", g)])
                    CP("dve", tixf, tix, [("tix", g) for g in range(16)], ["tixf"])
                    top4 = top.rearrange("p (h c) k -> p h c k", c=2)
                    tix4 = tixf.rearrange("p (h c) k -> p h c k", c=2)
                    cand4 = cand.rearrange("p h (a b) -> p h a b", a=16)
                    eid4 = eid.rearrange("p h (a b) -> p h a b", a=16)
                    TT("dve", cand4, top4[:, :, 0, :].unsqueeze(3).broadcast_to([128, 8, 16, 16]),
                       top4[:, :, 1, :].unsqueeze(2).broadcast_to([128, 8, 16, 16]), ALU.add, [("top", g) for g in range(16)], ["cand"])
                    TS("dve", tix4[:, :, 0, :], tix4[:, :, 0, :], 128.0, None, ALU.mult, None, ["tixf"], ["tixf"])
                    for h0 in range(0, 8, 2):
                        hs = (h0, h0 + 1)
                        wks = (work, work2)
                        for h, w_ in zip(hs, wks):
                            SC.add("dve", lambda e, h=h: e.max(out=tsv[:, h, 0:8], in_=cand[:, h, :]), ["cand"], [("tsv", h)])
                        for h, w_ in zip(hs, wks):
                            SC.add("dve", lambda e, h=h: e.max_index(out=pos[:, h, 0:8], in_max=tsv[:, h, 0:8], in_values=cand[:, h, :]),
                                   ["cand", ("tsv", h)], [("pos", h)])
                        for h, w_ in zip(hs, wks):
                            SC.add("dve", lambda e, h=h, w_=w_: e.match_replace(out=w_, in_to_replace=tsv[:, h, 0:8], in_values=cand[:, h, :],
                                                                                 imm_value=-1e30), ["cand", ("tsv", h)], [("work", h % 2)])
                        for h, w_ in zip(hs, wks):
                            SC.add("dve", lambda e, h=h, w_=w_: e.max(out=tsv[:, h, 8:16], in_=w_), [("work", h % 2)], [("tsv", h)])
                        for h, w_ in zip(hs, wks):
                            SC.add("dve", lambda e, h=h, w_=w_: e.max_index(out=pos[:, h, 8:16], in_max=tsv[:, h, 8:16], in_values=w_),
                                   [("work", h % 2), ("tsv", h)], [("pos", h)])
                    CP("dve", posf, pos, [("pos", h) for h in range(8)], ["posf"])
                    posflat = posf.rearrange("p h k -> p (h k)")
                    ge3 = cand.rearrange("p h x -> p (h x)")[:, 0:128 * 15].rearrange("p (s m) -> p s m", m=15)
                    TT("dve", ge3, posflat.unsqueeze(2).broadcast_to([128, 128, 15]),
                       iota[:, 16:256:16].unsqueeze(1).broadcast_to([128, 128, 15]), ALU.is_ge, ["posf", "iota", "cand"], ["cand"])
                    SC.add("dve", lambda e: e.tensor_reduce(out=af_, in_=ge3, axis=AX.X, op=ALU.add), ["cand"], ["af"])
                    STT(bf_, af_, -16.0, posflat, ALU.mult, ALU.add, ["af", "posf"], ["bf"])
                    io16 = iota[:, 0:16].unsqueeze(1).unsqueeze(1).broadcast_to([128, 8, 16, 16])
                    for (src_, lst, dst_, key) in ((af_, 0, e1_, "e1"), (bf_, 1, e2_, "e2")):
                        TT("dve", eid4, src_.rearrange("p (h k) -> p h k", h=8).unsqueeze(3).broadcast_to([128, 8, 16, 16]), io16,
                           ALU.is_equal, ["af", "bf", "iota", "scs"], ["scs"])
                        TT("dve", eid4, eid4, tix4[:, :, lst, :].unsqueeze(2).broadcast_to([128, 8, 16, 16]), ALU.mult, ["scs", "tixf"], ["scs"])
                        SC.add("dve", lambda e, dst_=dst_: e.tensor_reduce(out=dst_, in_=eid.rearrange("p h (k a) -> p (h k) a", a=16),
                                                                           axis=AX.X, op=ALU.add), ["scs"], ["af", "bf"])
                    TT("dve", ef, e1_, e2_, ALU.add, ["af", "bf"], ["ef"])
                    CP("dve", eidx[b], ef, ["ef"], [("eidx", b)])
                    TT("dve", dsm, tsv, tsv[:, :, 0:1].broadcast_to([128, 8, 16]), ALU.subtract, [("tsv", h) for h in range(8)], ["dsm"])
                    ACT(dsm, dsm, AF.Exp, ["dsm"], ["dsm"])
                    SC.add("dve", lambda e: e.tensor_reduce(out=zs, in_=dsm, axis=AX.X, op=ALU.add), ["dsm"], ["zs"])
                    SC.add("dve", lambda e: e.reciprocal(out=zs, in_=zs), ["zs"], ["zs"])
                    TT("dve", gt[b], dsm, zs.unsqueeze(2).broadcast_to([128, 8, 16]), ALU.mult, ["dsm", "zs"], [("gt", b)])

                NGR = 128 // NG
                gi = [0]
                di = [0]
                slotbuf = {}
                issued = set()

                def gathers(tb, g):
                    if (tb, g) in issued:
                        return
                    issued.add((tb, g))
                    b = tb % 2
                    for i in range(NG):
                        s_ = g * NG + i
                        rb = gi[0] % NRB
                        gi[0] += 1
                        slotbuf[(tb, s_)] = rb
                        SC.add("pool", lambda e, s_=s_, rb=rb, b=b, tbl=tbl: e.indirect_dma_start(
                            out=Rb[rb], out_offset=None, in_=tbl, in_offset=bass.IndirectOffsetOnAxis(ap=eidx[b][:, s_:s_ + 1], axis=0)),
                            [("eidx", b), tkey], [("R", rb)], dma="g%d" % rb)

                def evaluate(tb):
                    b = tb % 2
                    gtf = gt[b].rearrange("p h k -> p (h k)")
                    for g in range(LOOK):
                        gathers(tb, g)
                    for g in range(NGR):
                        if g + LOOK < NGR:
                            gathers(tb, g + LOOK)
                        elif tb + 1 < NTB:
                            gathers(tb + 1, g + LOOK - NGR)
                        sl = slice(g * NG, (g + 1) * NG)
                        for i in range(NG):
                            s_ = g * NG + i
                            rb = slotbuf[(tb, s_)]
                            STT(junkb, Rb[rb][:, 0:D], 1.0, pbx[b], ALU.mult, ALU.mult, [("R", rb), ("pbx", b)], [("actv", s_)],
                                accum=actv[:, s_:s_ + 1])
                        ACT(wgt[:, sl], actv[:, sl], AF.Gelu_apprx_tanh, [("actv", g * NG + i) for i in range(NG)], [("wgt", g)])
                        TT("dve", wgt[:, sl], wgt[:, sl], gtf[:, sl], ALU.mult, [("wgt", g), ("gt", b)], [("wgt", g)])
                        for i in range(NG):
                            s_ = g * NG + i
                            rb = slotbuf.pop((tb, s_))
                            dk = di[0] % ND
                            di[0] += 1
                            ACT(diag[dk], ident, AF.Copy, [("wgt", g), "ident"], [("diag", dk)], scale=wgt[:, s_:s_ + 1])
                            for half in range(2):
                                MM(pacc[half][:, :], diag[dk], Rb[rb][:, D + half * 512:D + (half + 1) * 512], s_ == 0, s_ == 127,
                                   [("diag", dk), ("R", rb)], [("pacc", half)])
                    for half in range(2):
                        STT(acc[:, half * 512:(half + 1) * 512], x1[b][:, half * 512:(half + 1) * 512], ALPHA, pacc[half][:, :],
                            ALU.mult, ALU.add, [("x1", b), ("pacc", half)], ["acc"])
                    layernorm(acc, lng, lnb, acc, wk, ["acc"], ["acc"], "ln")
                    DMA(x_dst[tb * 128:(tb + 1) * 128, :], acc, ["acc"], ["x2d"], "x2st")

                routing(0)
                for tb in range(NTB):
                    if tb + 1 < NTB:
                        routing(tb + 1)
                    evaluate(tb)

        n = SC.finalize(block)
    return nc, n


def prep_shared(inp):
    f = lambda a: np.ascontiguousarray(np.asarray(a, dtype=np.float32))
    w_in = np.asarray(inp["w_in"], np.float32)
    out = {}
    out["w_in"] = f(w_in.reshape(L, 8, 128, 56, 128).transpose(0, 3, 2, 1, 4).reshape(L, 56, 128, 1024))
    for k in ("w_br_attn", "w_br_lru", "w_out"):
        out[k] = f(np.asarray(inp[k], np.float32).reshape(L, 8, 128, 1024).transpose(0, 2, 1, 3))
    wq = np.asarray(inp["peer_wq"], np.float32)
    out["peer_wq"] = f(wq.reshape(L, 8, 128, 16, 128).transpose(0, 3, 2, 1, 4).reshape(L, 16, 128, 1024))
    sk = np.asarray(inp["peer_subkeys"], np.float32)
    out["peer_skT"] = f(sk.transpose(0, 4, 1, 2, 3).reshape(L, 128, 16 * 128))
    for i in range(L):
        out["peer_u%d" % i] = f(np.asarray(inp["peer_u"][i], np.float32))
        out["peer_v%d" % i] = f(np.asarray(inp["peer_v"][i], np.float32))
    out["iota256"] = np.arange(256, dtype=np.float32)
    out["gate_a_w"] = f(np.asarray(inp["gate_a_w"], np.float32).transpose(0, 2, 1, 3).reshape(L, 128, 1024))
    out["gate_x_w"] = f(np.asarray(inp["gate_x_w"], np.float32).transpose(0, 2, 1, 3).reshape(L, 128, 1024))
    chp = np.zeros((L, 8, 128, 8), np.float32)
    cw = np.asarray(inp["conv_w"], np.float32)
    for k in range(4):
        chp[:, :, :, k] = cw[:, k, :].reshape(L, 8, 128)
    chp[:, :, :, 4] = np.asarray(inp["conv_b"], np.float32).reshape(L, 8, 128)
    chp[:, :, :, 5] = np.asarray(inp["gate_a_b"], np.float32).reshape(L, 8, 128)
    chp[:, :, :, 6] = np.asarray(inp["gate_x_b"], np.float32).reshape(L, 8, 128)
    chp[:, :, :, 7] = np.asarray(inp["lru_lambda"], np.float32).reshape(L, 8, 128)
    out["chp"] = f(chp.transpose(0, 2, 1, 3).reshape(L, 128, 64))
    out["lambda_qk"] = f(np.asarray(inp["lambda_qk"], np.float32).reshape(L, 256))
    for k in ("subln_g", "ln1_g", "ln1_b", "ln2_g", "ln2_b"):
        out[k] = f(inp[k])
    augk = np.zeros((3, 128), np.float32)
    augk[0] = np.arange(128)
    augk[1] = 1.0
    augk[2] = 1.0
    augq = np.zeros((3, 8, 512), np.float32)
    qq = np.arange(512)
    for h in range(8):
        ch = 8.0 * 2.0 ** (-(h + 1))
        augq[0, h] = ch
        augq[1, h] = -ch * 128.0 * (qq // 128)
        augq[2, h] = -ch * (qq % 128)
    out["aug_k"] = augk
    out["aug_q"] = f(augq.reshape(3, 8 * 512))
    return out


_CACHE = {}


def kernel(**inputs):
    shared = prep_shared(inputs)
    x = np.asarray(inputs["x"], np.float32)
    nb = x.shape[0]
    if "nc" not in _CACHE:
        _CACHE["nc"] = build_program()[0]
    nc = _CACHE["nc"]
    in_maps = []
    for b in range(nb):
        m = dict(shared)
        m["x"] = np.ascontiguousarray(x[b])
        in_maps.append(m)
    res = run_bass_kernel_spmd(nc, in_maps, core_ids=list(range(nb)))
    return np.stack([np.asarray(r["y"], np.float32) for r in res.results], axis=0)
```

```python
import math
import numpy as np
import concourse.bass as bass
import concourse.mybir as mybir
from concourse.bass_utils import run_bass_kernel_spmd
from contextlib import ExitStack

F32 = mybir.dt.float32
BF16 = mybir.dt.bfloat16
U32 = mybir.dt.uint32
I32 = mybir.dt.int32
AF = mybir.ActivationFunctionType
ALU = mybir.AluOpType
AX = mybir.AxisListType

D = 1024
S = 4096
L = 2
NTB = S // 128
ALPHA = (2.0 * L) ** 0.25
LN_EPS = 1e-5
RMS_EPS = 1e-6
NEXP = 16384
SAME_SYNC = True
SKIP_ARG = 130.0


class Op:
    __slots__ = ("eng", "fn", "deps", "is_dma", "sem", "count", "signals", "waits", "idx", "barrier")


class Sched:
    def __init__(self, nc, stack, same_engine_sync=True):
        self.nc = nc
        self.stack = stack
        self.ops = []
        self.last_w = {}
        self.readers = {}
        self.same = same_engine_sync
        self.semh = {}

    def sem(self, key):
        if key not in self.semh:
            self.semh[key] = self.stack.enter_context(self.nc.semaphore("s_%d" % len(self.semh)))
        return self.semh[key]

    def add(self, eng, fn, reads=(), writes=(), dma=None):
        op = Op()
        op.eng = eng
        op.fn = fn
        op.barrier = False
        op.is_dma = dma is not None
        op.sem = ("dma", dma) if dma is not None else ("eng", eng)
        op.signals = op.is_dma
        op.count = 0
        op.idx = len(self.ops)
        deps = set()
        for r in reads:
            w = self.last_w.get(r)
            if w is not None:
                deps.add(w)
        for w_ in writes:
            w = self.last_w.get(w_)
            if w is not None:
                deps.add(w)
            for rd in self.readers.get(w_, ()):
                deps.add(rd)
        op.deps = deps
        for r in reads:
            self.readers.setdefault(r, []).append(op.idx)
        for w_ in writes:
            self.last_w[w_] = op.idx
            self.readers[w_] = []
        self.ops.append(op)
        return op

    def barrier(self, exclude=("cv",)):
        last = {}
        for op in self.ops:
            if not op.barrier and not op.is_dma:
                last[op.eng] = op
        for op in last.values():
            op.signals = True
        for eng in ("sp", "act", "dve", "pool", "pe"):
            op = Op()
            op.eng = eng
            op.fn = None
            op.barrier = True
            op.is_dma = False
            op.sem = None
            op.signals = False
            op.count = 0
            op.idx = len(self.ops)
            op.deps = set()
            op.waits = tuple(("dma", k) for k in exclude)
            self.ops.append(op)
        keep = {k: v for k, v in self.last_w.items() if self.ops[v].is_dma and self.ops[v].sem[1] in exclude}
        self.last_w = keep
        self.readers = {}

    def _skip(self, dop, op):
        return (not dop.is_dma) and dop.eng == op.eng and (not op.is_dma) and (dop.eng == "pe" or not self.same)

    def finalize(self, block):
        ops = self.ops
        for op in ops:
            for d in op.deps:
                dop = ops[d]
                if dop.is_dma or self._skip(dop, op):
                    continue
                dop.signals = True
        cnt = {}
        for op in ops:
            if op.barrier:
                op.count = dict(cnt)
                continue
            if op.signals:
                inc = 16 if op.is_dma else 1
                cnt[op.sem] = cnt.get(op.sem, 0) + inc
                op.count = cnt[op.sem]
        waited = {}
        for op in ops:
            w = waited.setdefault(op.eng, {})
            if op.barrier:
                excl = op.waits
                need = {s: v for s, v in op.count.items() if s != ("eng", op.eng) and s not in excl}
                op.waits = []
            else:
                op.waits = []
                need = {}
                for d in op.deps:
                    dop = ops[d]
                    if self._skip(dop, op):
                        continue
                    if need.get(dop.sem, 0) < dop.count:
                        need[dop.sem] = dop.count
            for s, v in need.items():
                if w.get(s, 0) < v:
                    w[s] = v
                    op.waits.append((s, v))
        final_waits = [(s, v) for s, v in cnt.items() if s[0] == "dma"]
        for s in cnt:
            self.sem(s)
        per_eng = {}
        for op in ops:
            per_eng.setdefault(op.eng, []).append(op)

        def emit(engname, eng_obj, final=False):
            for op in per_eng.get(engname, []):
                for s, v in op.waits:
                    eng_obj.wait_ge(self.sem(s), v)
                if op.fn is None:
                    continue
                ins = op.fn(eng_obj)
                if op.signals:
                    ins.then_inc(self.sem(op.sem), 16 if op.is_dma else 1)
            if final:
                for s, v in final_waits:
                    eng_obj.wait_ge(self.sem(s), v)

        @block.sync
        def _(e):
            emit("sp", e, final=True)

        @block.scalar
        def _(e):
            emit("act", e)

        @block.vector
        def _(e):
            emit("dve", e)

        @block.gpsimd
        def _(e):
            emit("pool", e)

        @block.tensor
        def _(e):
            emit("pe", e)
        return len(ops)


class Arena:
    def __init__(self, tensor, ncols):
        self.t = tensor
        self.n = ncols
        self.off = 0

    def reset(self):
        self.off = 0

    def alloc(self, shape, dt):
        n = 1
        for s_ in shape:
            n *= s_
        if dt == BF16:
            ncol = (n + 1) // 2
        else:
            ncol = n
        ncol = (ncol + 15) // 16 * 16
        assert self.off + ncol <= self.n, ("arena overflow", self.off, ncol, self.n)
        v = self.t[:, self.off:self.off + ncol]
        self.off += ncol
        if dt != F32:
            v = v.bitcast(dt)
        v = v[:, 0:n]
        if len(shape) == 2:
            v = v.rearrange("p (a b) -> p a b", a=shape[0])
        elif len(shape) == 3:
            v = v.rearrange("p (a b c) -> p a b c", a=shape[0], b=shape[1])
        return v


def build_program(n_layers=L, debug=False, phases=("A0", "A1", "A2", "A3", "B")):
    nc = bass.Bass("TRN2", target_bir_lowering=False)

    def din(name, shape, dt=F32):
        return nc.dram_tensor(name, list(shape), dt, kind="ExternalInput").ap()

    def dscr(name, shape, dt):
        return nc.dram_tensor(name, list(shape), dt, kind="ExternalOutput" if debug else "Internal").ap()

    x_d = din("x", [S, D])
    win_d = din("w_in", [L, 56, 128, 1024])
    wba_d = din("w_br_attn", [L, 128, 8, 1024])
    wbl_d = din("w_br_lru", [L, 128, 8, 1024])
    wo_d = din("w_out", [L, 128, 8, 1024])
    wq_d = din("peer_wq", [L, 16, 128, 1024])
    sk_d = din("peer_skT", [L, 128, 16 * 128])
    pu_ds = [din("peer_u%d" % i, [NEXP, D]) for i in range(L)]
    pv_ds = [din("peer_v%d" % i, [NEXP, D]) for i in range(L)]
    iota_d = din("iota256", [256])
    gaw_d = din("gate_a_w", [L, 128, 8 * 128])
    gxw_d = din("gate_x_w", [L, 128, 8 * 128])
    chp_d = din("chp", [L, 128, 64])
    lq_d = din("lambda_qk", [L, 256])
    sg_d = din("subln_g", [L, 128])
    ln1g_d = din("ln1_g", [L, D])
    ln1b_d = din("ln1_b", [L, D])
    ln2g_d = din("ln2_g", [L, D])
    ln2b_d = din("ln2_b", [L, D])
    augk_d = din("aug_k", [3, 128])
    augq_d = din("aug_q", [3, 8 * 512])
    y_d = nc.dram_tensor("y", [S, D], F32, kind="ExternalOutput").ap()

    xT_d = dscr("xT_s", [D, S], BF16)
    yaT_d = dscr("yaT_s", [D, S], BF16)
    yrT_d = dscr("yrT_s", [D, S], BF16)
    x1_d = dscr("x1_s", [S, D], F32)
    x2_d = dscr("x2_s", [S, D], F32)
    tb16 = [nc.dram_tensor("tb16_%d" % i, [NEXP, 2 * D], BF16, kind="Internal").ap() for i in range(L)]

    with ExitStack() as st:
        ARN = 50000
        arena_t = st.enter_context(nc.sbuf_tensor("arena", [128, ARN], F32))
        cst_t = st.enter_context(nc.sbuf_tensor("cst", [128, 3200], F32))
        pf = [st.enter_context(nc.psum_tensor("pf%d" % i, [128, 512], F32)) for i in range(7)]
        pb = st.enter_context(nc.psum_tensor("pbb", [128, 1024], BF16))
        pbf = pb[:, :].bitcast(F32)
        block = st.enter_context(nc.Block())
        SC = Sched(nc, st, same_engine_sync=SAME_SYNC)
        AR = Arena(arena_t, ARN)
        CA = Arena(cst_t, 3200)

        def DMA(out, in_, reads, writes, key, q="sp"):
            SC.add(q, lambda e: e.dma_start(out=out, in_=in_), reads, writes, dma=key)

        def MM(out, lhsT, rhs, start, stop, reads, writes):
            SC.add("pe", lambda e: e.matmul(out, lhsT=lhsT, rhs=rhs, start=start, stop=stop), reads, writes)

        def TR(out, in_, reads, writes):
            SC.add("pe", lambda e: e.transpose(out=out, in_=in_, identity=ident), list(reads) + ["ident"], writes)

        def ACT(out, in_, func, reads, writes, bias=None, scale=None, accum=None):
            kw = {}
            if bias is not None:
                kw["bias"] = bias
            if scale is not None:
                kw["scale"] = scale
            if accum is not None:
                kw["accum_out"] = accum
            SC.add("act", lambda e: e.activation(out=out, in_=in_, func=func, **kw), reads, writes)

        def CP(eng, out, in_, reads, writes):
            if eng == "act":
                SC.add("act", lambda e: e.activation(out=out, in_=in_, func=AF.Copy), reads, writes)
            else:
                SC.add(eng, lambda e: e.tensor_copy(out=out, in_=in_), reads, writes)

        def TT(eng, out, in0, in1, op, reads, writes):
            SC.add(eng, lambda e: e.tensor_tensor(out=out, in0=in0, in1=in1, op=op), reads, writes)

        def TS(eng, out, in0, s1, s2, op0, op1, reads, writes, accum=None):
            if accum is None:
                if s2 is None:
                    SC.add(eng, lambda e: e.tensor_scalar(out=out, in0=in0, scalar1=s1, scalar2=None, op0=op0), reads, writes)
                else:
                    SC.add(eng, lambda e: e.tensor_scalar(out=out, in0=in0, scalar1=s1, scalar2=s2, op0=op0, op1=op1), reads, writes)
            else:
                SC.add(eng, lambda e: e.tensor_scalar(out=out, in0=in0, scalar1=s1, scalar2=s2, op0=op0, op1=op1, accum_out=accum), reads, writes)

        def STT(out, in0, scalar, in1, op0, op1, reads, writes, accum=None):
            if accum is None:
                SC.add("dve", lambda e: e.scalar_tensor_tensor(out=out, in0=in0, scalar=scalar, in1=in1, op0=op0, op1=op1), reads, writes)
            else:
                SC.add("dve", lambda e: e.scalar_tensor_tensor(out=out, in0=in0, scalar=scalar, in1=in1, op0=op0, op1=op1, accum_out=accum), reads, writes)

        def MEMSET(eng, out, val, writes):
            SC.add(eng, lambda e: e.memset(out, val), (), writes)

        identf = CA.alloc([128], F32)
        ident = CA.alloc([128], BF16)
        trif = CA.alloc([128], F32)
        tri = CA.alloc([128], BF16)
        augk_f = CA.alloc([128], F32)
        augq_f = AR.alloc([8 * 512], F32)
        augk = CA.alloc([128], BF16)
        augq = CA.alloc([8, 512], BF16)
        MEMSET("pool", identf, 1.0, ["identf"])
        SC.add("pool", lambda e: e.affine_select(out=identf, in_=identf, pattern=[[-1, 128]], compare_op=ALU.is_equal,
                                                 fill=0.0, base=0, channel_multiplier=1), ["identf"], ["identf"])
        CP("dve", ident, identf, ["identf"], ["ident"])
        MEMSET("pool", trif, 1.0, ["trif"])
        SC.add("pool", lambda e: e.affine_select(out=trif, in_=trif, pattern=[[1, 128]], compare_op=ALU.is_ge,
                                                 fill=0.0, base=0, channel_multiplier=-1), ["trif"], ["trif"])
        CP("dve", tri, trif, ["trif"], ["tri"])
        zrow = CA.alloc([512], BF16)
        MEMSET("pool", zrow, 0.0, ["zrow"])
        iota = CA.alloc([256], F32)
        DMA(iota, iota_d.partition_broadcast(128), [], ["iota"], "c2")
        DMA(augk_f[64:67, :], augk_d, [], ["augk_f"], "c0")
        DMA(augq_f[64:67, :], augq_d, [], ["augq_f"], "c1")
        CP("dve", augk[64:67, :], augk_f[64:67, :], ["augk_f"], ["augk"])
        CP("dve", augq[64:67, :, :], augq_f[64:67, :].rearrange("p (a b) -> p a b", a=8), ["augq_f"], ["augq"])

        def conv_dma(dst, src, key):
            SC.add("pool", lambda e: e.dma_start(out=dst, in_=src), ["xT"], [key], dma="cv")

        conv_chunks = []
        for l_ in range(n_layers):
            for (src, c0) in ((pu_ds[l_], 0), (pv_ds[l_], D)):
                for ch in range(4):
                    conv_chunks.append((tb16[l_][ch * 4096:(ch + 1) * 4096, c0:c0 + D], src[ch * 4096:(ch + 1) * 4096, :], ("tb16", l_)))

        def emit_conversion(n):
            for _ in range(n):
                if conv_chunks and "B" in phases:
                    conv_dma(*conv_chunks.pop(0))

        def layernorm(y, g_bc, b_bc, out, wk, rkeys, wkeys, tagk):
            stt, mv, lnv, rstd = wk["st"], wk["mv"], wk["lnv"], wk["rstd"]
            SC.add("dve", lambda e: e.bn_stats(out=stt[:, 0:6], in_=y[:, 0:512]), rkeys, [tagk + "st"])
            SC.add("dve", lambda e: e.bn_stats(out=stt[:, 6:12], in_=y[:, 512:1024]), rkeys, [tagk + "st"])
            SC.add("dve", lambda e: e.bn_aggr(out=mv, in_=stt), [tagk + "st"], [tagk + "mv"])
            ACT(lnv, mv[:, 1:2], AF.Ln, [tagk + "mv", "eps"], [tagk + "lnv"], bias=wk["eps"])
            ACT(rstd, lnv, AF.Exp, [tagk + "lnv"], [tagk + "rstd"], scale=-0.5)
            TS("dve", y, y, mv[:, 0:1], rstd, ALU.subtract, ALU.mult, list(rkeys) + [tagk + "mv", tagk + "rstd"], wkeys_y(rkeys))
            TT("pool", y, y, g_bc, ALU.mult, list(rkeys) + ["lnp"], wkeys_y(rkeys))
            TT("pool", out, y, b_bc, ALU.add, list(rkeys) + ["lnp"], wkeys)

        def wkeys_y(rkeys):
            return list(rkeys)

        for l in range(n_layers):
            lam_init = 0.8 - 0.6 * math.exp(-0.3 * l)
            x_src = x_d if l == 0 else x2_d
            x_dst = y_d if l == n_layers - 1 else x2_d
            xsrc_key = "x2d"
            SC.barrier()
            AR.reset()
            xT = AR.alloc([8, S], BF16)
            a1_mark = AR.off
            if "A0" in phases:
                xs = [AR.alloc([D], F32) for _ in range(2)]
                xb = [AR.alloc([D], BF16) for _ in range(2)]
                for tb in range(NTB):
                    b = tb % 2
                    DMA(xs[b], x_src[tb * 128:(tb + 1) * 128, :], [xsrc_key], [("xs", b)], "xs%d" % b)
                    CP("act", xb[b], xs[b], [("xs", b)], [("xb", b)])
                    for dc in range(8):
                        TR(pb[:, dc * 128:(dc + 1) * 128], xb[b][:, dc * 128:(dc + 1) * 128], [("xb", b)], ["pb"])
                    CP("dve", xT[:, :, tb * 128:(tb + 1) * 128], pb[:, :].rearrange("p (a b) -> p a b", a=8), ["pb"], ["xT"])
                for dc in range(8):
                    DMA(xT_d[dc * 128:(dc + 1) * 128, :], xT[:, dc, :], ["xT"], ["xTd"], "xTd")
                if "A1" not in phases:
                    emit_conversion(len(conv_chunks))

            if "A1" in phases:
                AR.off = a1_mark
                wst = [AR.alloc([3, 1024], F32) for _ in range(2)]
                wbf = [AR.alloc([3, 1024], BF16) for _ in range(2)]
                qT2 = AR.alloc([2, S], BF16)
                kT2 = AR.alloc([2, S], BF16)
                Va = AR.alloc([NTB, 130], BF16)
                NE = 6
                Eb = [AR.alloc([512], BF16) for _ in range(NE)]
                Osb = AR.alloc([4, 512], F32)
                obuf = AR.alloc([4, 128], F32)
                junk = AR.alloc([128], F32)
                yab = AR.alloc([4, 128], BF16)
                yst = [AR.alloc([512], BF16) for _ in range(2)]
                lq = AR.alloc([256], F32)
                sgb = AR.alloc([128], F32)
                gsc = AR.alloc([128], F32)
                sm = AR.alloc([32], F32)
                neglam = sm[:, 0:1]
                s12 = sm[:, 1:3]
                e12 = sm[:, 3:5]
                rz = sm[:, 8:16]
                rz2l = sm[:, 16:20]
                ss = sm[:, 20:24]
                lnv4 = sm[:, 24:28]
                rstd4 = sm[:, 28:32]
                epsr = AR.alloc([1], F32)
                MEMSET("pool", epsr, RMS_EPS, ["epsr"])
                MEMSET("pool", Va[:, :, 128:130], 1.0, ["Va1"])
                for c in range(2):
                    CP("pool", kT2[64:67, c, :].rearrange("p (a b) -> p a b", a=NTB),
                       augk[64:67, :].unsqueeze(1).broadcast_to([3, NTB, 128]), ["augk"], ["kT"])
                DMA(lq, lq_d[l].partition_broadcast(128), [], ["lq"], "p0")
                DMA(sgb, sg_d[l].partition_broadcast(128), [], ["sgb"], "p1")
                STT(junk[:, 0:64], lq[:, 0:64], 1.0, lq[:, 64:128], ALU.mult, ALU.mult, ["lq"], ["junk", "s1"], accum=s12[:, 0:1])
                STT(junk[:, 0:64], lq[:, 128:192], 1.0, lq[:, 192:256], ALU.mult, ALU.mult, ["lq"], ["junk", "s2"], accum=s12[:, 1:2])
                ACT(e12, s12, AF.Exp, ["s1", "s2"], ["e12"])
                TT("dve", neglam, e12[:, 1:2], e12[:, 0:1], ALU.subtract, ["e12"], ["neglam"])
                TS("dve", neglam, neglam, -lam_init, None, ALU.add, None, ["neglam"], ["neglam"])
                TS("dve", gsc, sgb, 1.0 - lam_init, None, ALU.mult, None, ["sgb"], ["gsc"])

                def load_w(h, slot):
                    for i, cb in enumerate((h, 8 + h, 16 + h)):
                        DMA(wst[slot][:, i, :], win_d[l, cb], [], [("wst", slot)], "wst%d" % slot)
                    CP("dve", wbf[slot], wst[slot], [("wst", slot)], [("wbf", slot)])

                def epilogue2(h, j):
                    SC.add("dve", lambda e: e.reciprocal(out=rz.rearrange("p (a b) -> p a b", a=4),
                                                         in_=Osb[:, :, 128:512:256]), ["Osb"], ["rz"])
                    TS("dve", rz2l, rz[:, 4:8], neglam, None, ALU.mult, None, ["rz", "neglam"], ["rz2l"])
                    for qs in range(4):
                        o1 = Osb[:, qs // 2, (qs % 2) * 256:(qs % 2) * 256 + 128]
                        o2 = Osb[:, 2 + qs // 2, (qs % 2) * 256:(qs % 2) * 256 + 128]
                        TS("dve", obuf[:, qs, :], o1, rz[:, qs:qs + 1], None, ALU.mult, None, ["Osb", "rz"], ["obuf"])
                        STT(obuf[:, qs, :], o2, rz2l[:, qs:qs + 1], obuf[:, qs, :], ALU.mult, ALU.add, ["Osb", "rz2l", "obuf"], ["obuf"])
                        STT(junk, obuf[:, qs, :], 1.0, obuf[:, qs, :], ALU.mult, ALU.mult, ["obuf"], ["junk", "ss"], accum=ss[:, qs:qs + 1])
                    ACT(lnv4, ss, AF.Ln, ["ss"], ["lnv4"], scale=1.0 / 128.0, bias=epsr)
                    ACT(rstd4, lnv4, AF.Exp, ["lnv4"], ["rstd4"], scale=-0.5)
                    for qs in range(4):
                        STT(yab[:, qs, :], obuf[:, qs, :], rstd4[:, qs:qs + 1], gsc, ALU.mult, ALU.mult, ["obuf", "rstd4", "gsc"], ["yab"])

                def epilogue3(h, j):
                    for qs in range(4):
                        TR(pb[:, qs * 128:(qs + 1) * 128], yab[:, qs, :], ["yab"], ["pb"])
                    ys = yst[j % 2]
                    CP("dve", ys, pb[:, 0:512], ["pb"], [("yst", j % 2)])
                    DMA(yaT_d[h * 128:(h + 1) * 128, j * 512:(j + 1) * 512], ys, [("yst", j % 2)], ["yaTd"], "yst%d" % (j % 2))

                load_w(0, 0)
                ei = 0
                si = 0
                for h in range(8):
                    slot = h % 2
                    if h + 1 < 8:
                        load_w(h + 1, 1 - slot)
                    emit_conversion(2)
                    slope = 2.0 ** (-(h + 1))
                    for c in range(2):
                        CP("dve", qT2[64:67, c, :].rearrange("p (a b) -> p a b", a=8),
                           augq[64:67, h, :].unsqueeze(1).broadcast_to([3, 8, 512]), ["augq"], ["qT"])
                    pj = [(pf[6], "pf6"), (pbf, "pb")]
                    pji = 0
                    for tq in range(8):
                        for (wi, dst, key) in ((0, qT2, "qT"), (1, kT2, "kT")):
                            pp_, pk_ = pj[pji % 2]
                            pji += 1
                            for dc in range(8):
                                MM(pp_[:, :], wbf[slot][:, wi, dc * 128:(dc + 1) * 128], xT[:, dc, tq * 512:(tq + 1) * 512],
                                   dc == 0, dc == 7, [("wbf", slot), "xT"], [pk_])
                            CP("dve", dst[0:64, 0, tq * 512:(tq + 1) * 512], pp_[0:64, :], [pk_], [key])
                            CP("act", dst[0:64, 1, tq * 512:(tq + 1) * 512], pp_[64:128, :], [pk_], [key])
                    for tb4 in range(8):
                        pp_, pk_ = pj[pji % 2]
                        pji += 1
                        for t in range(4):
                            tb = tb4 * 4 + t
                            for dc in range(8):
                                MM(pp_[:, t * 128:(t + 1) * 128], xT[:, dc, tb * 128:(tb + 1) * 128],
                                   wbf[slot][:, 2, dc * 128:(dc + 1) * 128], dc == 0, dc == 7, [("wbf", slot), "xT"], [pk_])
                        CP("dve", Va[:, tb4 * 4:(tb4 + 1) * 4, 0:128], pp_[:, :].rearrange("p (a b) -> p a b", a=4), [pk_], ["Va"])
                    def live(j, kb):
                        return slope * (j * 512 - kb * 128 - 127) <= SKIP_ARG
                    tiles = [(j, c, kb) for j in range(8) for c in range(2) for kb in range(4 * j + 4) if live(j, kb)]
                    first_of_j = {}
                    for (j_, c_, kb_) in tiles:
                        first_of_j.setdefault(j_, (c_, kb_))
                    sinfo = {}

                    def emit_S(i):
                        nonlocal si
                        j, c, kb = tiles[i]
                        r = kb - 4 * j
                        nq0 = max(0, r) * 128
                        sb_ = pf[4 + si % 2]
                        skey = ("S", si % 2)
                        si += 1
                        MM(sb_[:, nq0:512], kT2[0:67, c, kb * 128:(kb + 1) * 128],
                           qT2[0:67, c, j * 512 + nq0:(j + 1) * 512], True, True, ["qT", "kT"], [skey])
                        sinfo[i] = (sb_, skey)

                    pending = []

                    def emit_rest(i):
                        nonlocal ei
                        j, c, kb = tiles[i]
                        r = kb - 4 * j
                        nq0 = max(0, r) * 128
                        sb_, skey = sinfo.pop(i)
                        if (c, kb) == first_of_j[j]:
                            for bnk in range(4):
                                MM(pf[bnk][:, :], zrow[0:1, 0:128], zrow[0:1, 0:512], True, False, ["zrow"], [("O", bnk)])
                        E = Eb[ei % NE]
                        ekey = ("E", ei % NE)
                        ei += 1
                        ACT(E[:, nq0:512], sb_[:, nq0:512], AF.Exp, [skey], [ekey], scale=0.125,
                            bias=float(slope * (kb * 128 - j * 512)))
                        if r >= 0:
                            TT("dve", E[:, r * 128:(r + 1) * 128], E[:, r * 128:(r + 1) * 128], tri, ALU.mult, [ekey, "tri"], [ekey])
                        for qs in range(max(0, r), 4):
                            ob = pf[c * 2 + qs // 2]
                            MM(ob[:, (qs % 2) * 256:(qs % 2) * 256 + 129], E[:, qs * 128:(qs + 1) * 128], Va[:, kb, 0:129],
                               False, kb == 4 * j + qs, [ekey, "Va", "Va1"], [("O", c * 2 + qs // 2)])
                        if c == 1 and kb == 4 * j + 3:
                            for bnk in range(4):
                                CP("dve", Osb[:, bnk, :], pf[bnk][:, :], [("O", bnk)], ["Osb"])
                            pending.append((i + 4, 2, h, j))
                            pending.append((i + 12, 3, h, j))
                        while pending and (pending[0][0] <= i or i == len(tiles) - 1):
                            _, kind, hh, jj = pending.pop(0)
                            (epilogue2 if kind == 2 else epilogue3)(hh, jj)

                    emit_S(0)
                    for i in range(len(tiles)):
                        if i + 1 < len(tiles):
                            emit_S(i + 1)
                        emit_rest(i)

            if "A2" in phases:
                SC.barrier()
                AR.off = a1_mark
                B0 = AR.alloc([S + 16], F32)
                B1 = AR.alloc([S], F32)
                B2 = AR.alloc([S], F32)
                B3 = AR.alloc([S], F32)
                xcb = AR.alloc([S], BF16)
                Yb = AR.alloc([S], BF16)
                wst2 = [AR.alloc([2, 1024], F32) for _ in range(2)]
                wbf2 = [AR.alloc([2, 1024], BF16) for _ in range(2)]
                gwf = AR.alloc([2, 1024], F32)
                gwb = AR.alloc([2, 8, 128], BF16)
                chp = AR.alloc([8, 8], F32)
                cc = AR.alloc([8, 4], F32)
                DMA(gwf[:, 0, :], gaw_d[l], [], ["gwf"], "p2")
                DMA(gwf[:, 1, :], gxw_d[l], [], ["gwf"], "p2")
                CP("dve", gwb, gwf.rearrange("p a (g j) -> p a g j", g=8), ["gwf"], ["gwb"])
                DMA(chp, chp_d[l].rearrange("p (g f) -> p g f", g=8), [], ["chp"], "p3")
                ACT(cc[:, :, 2], chp[:, :, 7], AF.Exp, ["chp"], ["cc"], scale=-1.0)
                ACT(cc[:, :, 3], cc[:, :, 2], AF.Ln, ["cc"], ["cc"], bias=1.0)
                TS("dve", cc[:, :, 0], cc[:, :, 3], -8.0, None, ALU.mult, None, ["cc"], ["cc"])
                TS("dve", cc[:, :, 1], cc[:, :, 3], -16.0, None, ALU.mult, None, ["cc"], ["cc"])

                def load_w2(g, slot):
                    DMA(wst2[slot][:, 0, :], win_d[l, 24 + g], [], [("wst2", slot)], "wst2%d" % slot)
                    DMA(wst2[slot][:, 1, :], win_d[l, 32 + g], [], [("wst2", slot)], "wst2%d" % slot)
                    CP("pool", wbf2[slot], wst2[slot], [("wst2", slot)], [("wbf2", slot)])

                load_w2(0, 0)
                pi = 0
                for g in range(8):
                    slot = g % 2
                    if g + 1 < 8:
                        load_w2(g + 1, 1 - slot)
                    MEMSET("pool", B0[:, 0:3], 0.0, ["B0"])
                    for tq in range(8):
                        pp = pf[pi % 7]
                        pk = ("pf", pi % 7)
                        pi += 1
                        for dc in range(8):
                            MM(pp[:, :], wbf2[slot][:, 0, dc * 128:(dc + 1) * 128], xT[:, dc, tq * 512:(tq + 1) * 512], dc == 0, dc == 7,
                               [("wbf2", slot), "xT"], [pk])
                        CP("act", B0[:, 3 + tq * 512:3 + (tq + 1) * 512], pp[:, :], [pk], ["B0"])
                    TS("dve", B1, B0[:, 0:S], chp[:, g, 0:1], chp[:, g, 4:5], ALU.mult, ALU.add, ["B0", "chp"], ["B1"])
                    for k in range(1, 4):
                        STT(B1, B0[:, k:k + S], chp[:, g, k:k + 1], B1, ALU.mult, ALU.add, ["B0", "chp", "B1"], ["B1"])
                    CP("pool", xcb, B1, ["B1"], ["xcb"])
                    for (wi, dst, off, key, bcol) in ((0, B2, 0, "B2", 5), (1, B0, 3, "B0", 6)):
                        for tq in range(8):
                            pp = pf[pi % 7]
                            pk = ("pf", pi % 7)
                            pi += 1
                            MM(pp[:, :], gwb[:, wi, g, :], xcb[:, tq * 512:(tq + 1) * 512], True, True, ["gwb", "xcb"], [pk])
                            ACT(dst[:, off + tq * 512:off + (tq + 1) * 512], pp[:, :], AF.Sigmoid, [pk, "chp", "B1"], [key],
                                bias=chp[:, g, bcol:bcol + 1])
                    ACT(B3, B2, AF.Exp, ["B2", "cc"], ["B3"], scale=cc[:, g, 1:2])
                    ACT(B3, B3, AF.Sqrt, ["B3"], ["B3"], scale=-1.0, bias=1.0)
                    MEMSET("pool", B3[:, 0:1], 1.0, ["B3"])
                    ACT(B2, B2, AF.Exp, ["B2", "cc"], ["B2"], scale=cc[:, g, 0:1])
                    TT("dve", B0[:, 3:3 + S], B0[:, 3:3 + S], B1, ALU.mult, ["B0", "B1"], ["B0"])
                    TT("dve", B0[:, 3:3 + S], B0[:, 3:3 + S], B3, ALU.mult, ["B0", "B3"], ["B0"])
                    SC.add("dve", lambda e: e.tensor_tensor_scan(out=B3, data0=B2, data1=B0[:, 3:3 + S], initial=0.0,
                                                                 op0=ALU.mult, op1=ALU.add), ["B2", "B0", "B3"], ["B3"])
                    for tq in range(8):
                        pp = pf[pi % 7]
                        pk = ("pf", pi % 7)
                        pi += 1
                        for dc in range(8):
                            MM(pp[:, :], wbf2[slot][:, 1, dc * 128:(dc + 1) * 128], xT[:, dc, tq * 512:(tq + 1) * 512], dc == 0, dc == 7,
                               [("wbf2", slot), "xT"], [pk])
                        ACT(B1[:, tq * 512:(tq + 1) * 512], pp[:, :], AF.Gelu_apprx_tanh, [pk, "B0"], ["B1"])
                    TT("dve", Yb, B3, B1, ALU.mult, ["B3", "B1", "yrTd"], ["Yb"])
                    DMA(yrT_d[g * 128:(g + 1) * 128, :], Yb, ["Yb"], ["yrTd"], "yrTd")

            if "A3" in phases:
                SC.barrier()
                AR.reset()
                wg = AR.alloc([16, 8, 128], BF16)
                wba = AR.alloc([8, 1024], BF16)
                wbl = AR.alloc([8, 1024], BF16)
                wo = AR.alloc([8, 1024], BF16)
                stg = [AR.alloc([1024], F32) for _ in range(2)]
                lng = AR.alloc([D], F32)
                lnb = AR.alloc([D], F32)
                DMA(lng, ln1g_d[l].partition_broadcast(128), [], ["lnp"], "p4")
                DMA(lnb, ln1b_d[l].partition_broadcast(128), [], ["lnp"], "p4")
                si_ = 0
                for cb in range(16):
                    b = si_ % 2
                    si_ += 1
                    DMA(stg[b], win_d[l, 40 + cb], [], [("stg", b)], "stg%d" % b)
                    CP("dve" if cb % 2 else "pool", wg[:, cb, :, :], stg[b].rearrange("p (a b) -> p a b", a=8), [("stg", b)], ["wg"])
                for (src, dst, key) in ((wba_d, wba, "wba"), (wbl_d, wbl, "wbl"), (wo_d, wo, "wo")):
                    for kc in range(8):
                        b = si_ % 2
                        si_ += 1
                        DMA(stg[b], src[l, :, kc, :], [], [("stg", b)], "stg%d" % b)
                        CP("dve" if kc % 2 else "pool", dst[:, kc, :], stg[b], [("stg", b)], [key])
                xTb = [AR.alloc([8, 512], BF16) for _ in range(2)]
                yaTb = [AR.alloc([8, 512], BF16) for _ in range(2)]
                yrTb = [AR.alloc([8, 512], BF16) for _ in range(2)]
                mT = AR.alloc([8, 512], BF16)
                sga = [AR.alloc([512], F32) for _ in range(2)]
                sgr = [AR.alloc([512], F32) for _ in range(2)]
                xres = [AR.alloc([D], F32) for _ in range(2)]
                yln = [AR.alloc([D], F32) for _ in range(2)]
                wk = {"st": AR.alloc([12], F32), "mv": AR.alloc([2], F32), "lnv": AR.alloc([1], F32),
                      "rstd": AR.alloc([1], F32), "eps": AR.alloc([1], F32)}
                MEMSET("pool", wk["eps"], LN_EPS, ["eps"])
                ti = 0
                for tq in range(8):
                    b = tq % 2
                    DMA(xTb[b], xT_d[:, tq * 512:(tq + 1) * 512].rearrange("(a p) t -> p a t", p=128), ["xTd"], [("xTb", b)], "xTb%d" % b)
                    DMA(yaTb[b], yaT_d[:, tq * 512:(tq + 1) * 512].rearrange("(a p) t -> p a t", p=128), ["yaTd"], [("yaTb", b)], "yaTb%d" % b)
                    DMA(yrTb[b], yrT_d[:, tq * 512:(tq + 1) * 512].rearrange("(a p) t -> p a t", p=128), ["yrTd"], [("yrTb", b)], "yrTb%d" % b)
                    for jb in range(8):
                        for dc in range(8):
                            MM(pf[0][:, :], wg[:, jb, dc, :], xTb[b][:, dc, :], dc == 0, dc == 7, ["wg", ("xTb", b)], ["pf0"])
                        for dc in range(8):
                            MM(pf[1][:, :], wg[:, 8 + jb, dc, :], xTb[b][:, dc, :], dc == 0, dc == 7, ["wg", ("xTb", b)], ["pf1"])
                        for kc in range(8):
                            MM(pf[2][:, :], wba[:, kc, jb * 128:(jb + 1) * 128], yaTb[b][:, kc, :], kc == 0, kc == 7, ["wba", ("yaTb", b)], ["pf2"])
                        for kc in range(8):
                            MM(pf[3][:, :], wbl[:, kc, jb * 128:(jb + 1) * 128], yrTb[b][:, kc, :], kc == 0, kc == 7, ["wbl", ("yrTb", b)], ["pf3"])
                        sb2 = jb % 2
                        ACT(sga[sb2], pf[0][:, :], AF.Sigmoid, ["pf0"], [("sga", sb2)])
                        ACT(sgr[sb2], pf[1][:, :], AF.Sigmoid, ["pf1"], [("sgr", sb2)])
                        TT("dve", sga[sb2], sga[sb2], pf[2][:, :], ALU.mult, [("sga", sb2), "pf2"], [("sga", sb2)])
                        TT("dve", sgr[sb2], sgr[sb2], pf[3][:, :], ALU.mult, [("sgr", sb2), "pf3"], [("sgr", sb2)])
                        TT("pool", mT[:, jb, :], sga[sb2], sgr[sb2], ALU.add, [("sga", sb2), ("sgr", sb2)], ["mT"])
                    for ts_ in range(4):
                        tb = tq * 4 + ts_
                        xb_ = ti % 2
                        ti += 1
                        DMA(xres[xb_], x_src[tb * 128:(tb + 1) * 128, :], [xsrc_key], [("xres", xb_)], "xres%d" % xb_)
                        for nh in range(2):
                            for jb in range(8):
                                MM(pf[4 + nh][:, :], mT[:, jb, ts_ * 128:(ts_ + 1) * 128], wo[:, jb, nh * 512:(nh + 1) * 512], jb == 0, jb == 7,
                                   ["mT", "wo"], [("pf", 4 + nh)])
                            STT(yln[xb_][:, nh * 512:(nh + 1) * 512], xres[xb_][:, nh * 512:(nh + 1) * 512], ALPHA, pf[4 + nh][:, :],
                                ALU.mult, ALU.add, [("xres", xb_), ("pf", 4 + nh)], [("yln", xb_)])
                        layernorm(yln[xb_], lng, lnb, yln[xb_], wk, [("yln", xb_)], [("yln", xb_)], "ln")
                        DMA(x1_d[tb * 128:(tb + 1) * 128, :], yln[xb_], [("yln", xb_)], ["x1d"], "x1st%d" % xb_)

            if "B" in phases:
                SC.barrier()
                AR.reset()
                wq = AR.alloc([16, 8, 128], BF16)
                skT = AR.alloc([16, 128], BF16)
                lng = AR.alloc([D], F32)
                lnb = AR.alloc([D], F32)
                bmark = AR.off
                stg = [AR.alloc([2048], F32) for _ in range(2)]
                DMA(lng, ln2g_d[l].partition_broadcast(128), [], ["lnp"], "p4")
                DMA(lnb, ln2b_d[l].partition_broadcast(128), [], ["lnp"], "p4")
                for cb in range(16):
                    b = cb % 2
                    DMA(stg[b][:, 0:1024], wq_d[l, cb], [], [("stg", b)], "stg%d" % b)
                    CP("dve" if cb % 2 else "pool", wq[:, cb, :, :], stg[b][:, 0:1024].rearrange("p (a b) -> p a b", a=8), [("stg", b)], ["wq"])
                DMA(stg[0], sk_d[l], [], [("stg", 0)], "stg0")
                CP("dve", skT, stg[0].rearrange("p (a b) -> p a b", a=16), [("stg", 0)], ["skT"])
                SC.barrier()
                AR.off = bmark
                x1 = [AR.alloc([D], F32) for _ in range(2)]
                x1b = [AR.alloc([D], BF16) for _ in range(2)]
                x1T = AR.alloc([8, 128], BF16)
                qTs = AR.alloc([16, 128], BF16)
                scs = AR.alloc([16, 128], F32)
                top = AR.alloc([16, 16], F32)
                tix = AR.alloc([16, 16], U32)
                tixf = AR.alloc([16, 16], F32)
                work = AR.alloc([256], F32)
                work2 = AR.alloc([256], F32)
                wka = AR.alloc([128], F32)
                wkb = AR.alloc([128], F32)
                cand = AR.alloc([8, 256], F32)
                eid = scs.rearrange("p a b -> p (a b)").rearrange("p (h x) -> p h x", h=8)
                tsv = AR.alloc([8, 16], F32)
                pos = AR.alloc([8, 16], U32)
                posf = AR.alloc([8, 16], F32)
                ef = AR.alloc([128], F32)
                af_ = AR.alloc([128], F32)
                bf_ = AR.alloc([128], F32)
                e1_ = af_
                e2_ = bf_
                eidx = [AR.alloc([128], I32) for _ in range(2)]
                gt = [AR.alloc([8, 16], F32) for _ in range(2)]
                dsm = AR.alloc([8, 16], F32)
                zs = AR.alloc([8], F32)
                actv = AR.alloc([128], F32)
                wgt = AR.alloc([128], F32)
                acc = AR.alloc([D], F32)
                junkb = AR.alloc([D], BF16)
                ND = 8
                diag = [AR.alloc([128], BF16) for _ in range(ND)]
                wk = {"st": AR.alloc([12], F32), "mv": AR.alloc([2], F32), "lnv": AR.alloc([1], F32),
                      "rstd": AR.alloc([1], F32), "eps": AR.alloc([1], F32)}
                NG = 4
                LOOK = 5
                NRB = (LOOK + 1) * NG
                print("phaseB arena before Rb:", AR.off, "NRB", NRB, "ARN", ARN)
                Rb = [AR.alloc([2 * D], BF16) for _ in range(NRB)]
                MEMSET("pool", wk["eps"], LN_EPS, ["eps"])
                tbl = tb16[l]
                tkey = ("tb16", l)
                pacc = [pf[5], pf[6]]
                pbx = [pf[3][:, :].bitcast(BF16), pf[4][:, :].bitcast(BF16)]

                def routing(tb):
                    b = tb % 2
                    DMA(x1[b], x1_d[tb * 128:(tb + 1) * 128, :], ["x1d"], [("x1", b)], "x1ld%d" % b)
                    CP("act", x1b[b], x1[b], [("x1", b)], [("x1b", b)])
                    for dc in range(8):
                        TR(pb[:, dc * 128:(dc + 1) * 128], x1b[b][:, dc * 128:(dc + 1) * 128], [("x1b", b)], ["pb"])
                    yield
                    CP("act", x1T, pb[:, :].rearrange("p (a b) -> p a b", a=8), ["pb"], ["x1T"])
                    for dc in range(8):
                        TR(pbx[b][:, dc * 128:(dc + 1) * 128], x1T[:, dc, :], ["x1T"], [("pbx", b)])
                    for c4 in range(4):
                        pp = pf[0]
                        pk = ("pf", 0)
                        for ci in range(4):
                            cb = c4 * 4 + ci
                            for dc in range(8):
                                MM(pp[:, ci * 128:(ci + 1) * 128], wq[:, cb, dc, :], x1T[:, dc, :], dc == 0, dc == 7, ["wq", "x1T"], [pk])
                        CP("act", qTs[:, c4 * 4:(c4 + 1) * 4, :], pp[:, :].rearrange("p (a b) -> p a b", a=4), [pk], ["qTs"])
                        yield
                    for c4 in range(4):
                        pp = pf[1 + c4 % 2]
                        pk = ("pf", 1 + c4 % 2)
                        for ci in range(4):
                            cb = c4 * 4 + ci
                            MM(pp[:, ci * 128:(ci + 1) * 128], qTs[:, cb, :], skT[:, cb, :], True, True, ["qTs", "skT"], [pk])
                        CP("act", scs[:, c4 * 4:(c4 + 1) * 4, :], pp[:, :].rearrange("p (a b) -> p a b", a=4), [pk], ["scs"])
                        yield
                    for g0 in range(0, 16, 2):
                        gs = (g0, g0 + 1)
                        wk2 = (wka, wkb)
                        for g, w_ in zip(gs, wk2):
                            SC.add("dve", lambda e, g=g: e.max(out=top[:, g, 0:8], in_=scs[:, g, :]), ["scs"], [("top", g)])
                        for g, w_ in zip(gs, wk2):
                            SC.add("dve", lambda e, g=g: e.max_index(out=tix[:, g, 0:8], in_max=top[:, g, 0:8], in_values=scs[:, g, :]),
                                   ["scs", ("top", g)], [("tix", g)])
                        for g, w_ in zip(gs, wk2):
                            SC.add("dve", lambda e, g=g, w_=w_: e.match_replace(out=w_, in_to_replace=top[:, g, 0:8], in_values=scs[:, g, :],
                                                                                 imm_value=-1e30), ["scs", ("top", g)], [("wk1", g % 2)])
                        for g, w_ in zip(gs, wk2):
                            SC.add("dve", lambda e, g=g, w_=w_: e.max(out=top[:, g, 8:16], in_=w_), [("wk1", g % 2)], [("top", g)])
                        for g, w_ in zip(gs, wk2):
                            SC.add("dve", lambda e, g=g, w_=w_: e.max_index(out=tix[:, g, 8:16], in_max=top[:, g, 8:16], in_values=w_),
                                   [("wk1", g % 2), ("top", g)], [("tix", g)])
                        yield
                    CP("dve", tixf, tix, [("tix", g) for g in range(16)], ["tixf"])
                    top4 = top.rearrange("p (h c) k -> p h c k", c=2)
                    tix4 = tixf.rearrange("p (h c) k -> p h c k", c=2)
                    cand4 = cand.rearrange("p h (a b) -> p h a b", a=16)
                    eid4 = eid.rearrange("p h (a b) -> p h a b", a=16)
                    TT("dve", cand4, top4[:, :, 0, :].unsqueeze(3).broadcast_to([128, 8, 16, 16]),
                       top4[:, :, 1, :].unsqueeze(2).broadcast_to([128, 8, 16, 16]), ALU.add, [("top", g) for g in range(16)], ["cand"])
                    TS("dve", tix4[:, :, 0, :], tix4[:, :, 0, :], 128.0, None, ALU.mult, None, ["tixf"], ["tixf"])
                    for h0 in range(0, 8, 2):
                        hs = (h0, h0 + 1)
                        wks = (work, work2)
                        for h, w_ in zip(hs, wks):
                            SC.add("dve", lambda e, h=h: e.max(out=tsv[:, h, 0:8], in_=cand[:, h, :]), ["cand"], [("tsv", h)])
                        for h, w_ in zip(hs, wks):
                            SC.add("dve", lambda e, h=h: e.max_index(out=pos[:, h, 0:8], in_max=tsv[:, h, 0:8], in_values=cand[:, h, :]),
                                   ["cand", ("tsv", h)], [("pos", h)])
                        for h, w_ in zip(hs, wks):
                            SC.add("dve", lambda e, h=h, w_=w_: e.match_replace(out=w_, in_to_replace=tsv[:, h, 0:8], in_values=cand[:, h, :],
                                                                                 imm_value=-1e30), ["cand", ("tsv", h)], [("work", h % 2)])
                        for h, w_ in zip(hs, wks):
                            SC.add("dve", lambda e, h=h, w_=w_: e.max(out=tsv[:, h, 8:16], in_=w_), [("work", h % 2)], [("tsv", h)])
                        for h, w_ in zip(hs, wks):
                            SC.add("dve", lambda e, h=h, w_=w_: e.max_index(out=pos[:, h, 8:16], in_max=tsv[:, h, 8:16], in_values=w_),
                                   [("work", h % 2), ("tsv", h)], [("pos", h)])
                        yield
                    CP("dve", posf, pos, [("pos", h) for h in range(8)], ["posf"])
                    yield
                    posflat = posf.rearrange("p h k -> p (h k)")
                    ge3 = cand.rearrange("p h x -> p (h x)")[:, 0:128 * 15].rearrange("p (s m) -> p s m", m=15)
                    TT("dve", ge3, posflat.unsqueeze(2).broadcast_to([128, 128, 15]),
                       iota[:, 16:256:16].unsqueeze(1).broadcast_to([128, 128, 15]), ALU.is_ge, ["posf", "iota", "cand"], ["cand"])
                    SC.add("dve", lambda e: e.tensor_reduce(out=af_, in_=ge3, axis=AX.X, op=ALU.add), ["cand"], ["af"])
                    STT(bf_, af_, -16.0, posflat, ALU.mult, ALU.add, ["af", "posf"], ["bf"])
                    yield
                    io16 = iota[:, 0:16].unsqueeze(1).unsqueeze(1).broadcast_to([128, 8, 16, 16])
                    for (src_, lst, dst_, key) in ((af_, 0, e1_, "e1"), (bf_, 1, e2_, "e2")):
                        TT("dve", eid4, src_.rearrange("p (h k) -> p h k", h=8).unsqueeze(3).broadcast_to([128, 8, 16, 16]), io16,
                           ALU.is_equal, ["af", "bf", "iota", "scs"], ["scs"])
                        TT("dve", eid4, eid4, tix4[:, :, lst, :].unsqueeze(2).broadcast_to([128, 8, 16, 16]), ALU.mult, ["scs", "tixf"], ["scs"])
                        SC.add("dve", lambda e, dst_=dst_: e.tensor_reduce(out=dst_, in_=eid.rearrange("p h (k a) -> p (h k) a", a=16),
                                                                           axis=AX.X, op=ALU.add), ["scs"], ["af", "bf"])
                        yield
                    TT("dve", ef, e1_, e2_, ALU.add, ["af", "bf"], ["ef"])
                    CP("dve", eidx[b], ef, ["ef"], [("eidx", b)])
                    yield
                    TT("dve", dsm, tsv, tsv[:, :, 0:1].broadcast_to([128, 8, 16]), ALU.subtract, [("tsv", h) for h in range(8)], ["dsm"])
                    ACT(dsm, dsm, AF.Exp, ["dsm"], ["dsm"])
                    SC.add("dve", lambda e: e.tensor_reduce(out=zs, in_=dsm, axis=AX.X, op=ALU.add), ["dsm"], ["zs"])
                    SC.add("dve", lambda e: e.reciprocal(out=zs, in_=zs), ["zs"], ["zs"])
                    TT("dve", gt[b], dsm, zs.unsqueeze(2).broadcast_to([128, 8, 16]), ALU.mult, ["dsm", "zs"], [("gt", b)])

                NGR = 128 // NG
                gi = [0]
                di = [0]
                slotbuf = {}
                issued = set()

                def gathers(tb, g):
                    if (tb, g) in issued:
                        return
                    issued.add((tb, g))
                    b = tb % 2
                    for i in range(NG):
                        s_ = g * NG + i
                        rb = gi[0] % NRB
                        gi[0] += 1
                        slotbuf[(tb, s_)] = rb
                        SC.add("pool", lambda e, s_=s_, rb=rb, b=b, tbl=tbl: e.indirect_dma_start(
                            out=Rb[rb], out_offset=None, in_=tbl, in_offset=bass.IndirectOffsetOnAxis(ap=eidx[b][:, s_:s_ + 1], axis=0)),
                            [("eidx", b), tkey], [("R", rb)], dma="g%d" % rb)

                def evaluate(tb, rgen):
                    b = tb % 2
                    gtf = gt[b].rearrange("p h k -> p (h k)")
                    for g in range(LOOK):
                        gathers(tb, g)
                    for g in range(NGR):
                        if rgen is not None:
                            if g + LOOK >= NGR - 1:
                                for _ in rgen:
                                    pass
                            else:
                                for _ in range(RSTEP):
                                    next(rgen, None)
                        if g + LOOK < NGR:
                            gathers(tb, g + LOOK)
                        elif tb + 1 < NTB:
                            gathers(tb + 1, g + LOOK - NGR)
                        sl = slice(g * NG, (g + 1) * NG)
                        for i in range(NG):
                            s_ = g * NG + i
                            rb = slotbuf[(tb, s_)]
                            STT(junkb, Rb[rb][:, 0:D], 1.0, pbx[b], ALU.mult, ALU.mult, [("R", rb), ("pbx", b)], [("actv", s_)],
                                accum=actv[:, s_:s_ + 1])
                        ACT(wgt[:, sl], actv[:, sl], AF.Gelu_apprx_tanh, [("actv", g * NG + i) for i in range(NG)], [("wgt", g)])
                        TT("dve", wgt[:, sl], wgt[:, sl], gtf[:, sl], ALU.mult, [("wgt", g), ("gt", b)], [("wgt", g)])
                        for i in range(NG):
                            s_ = g * NG + i
                            rb = slotbuf.pop((tb, s_))
                            dk = di[0] % ND
                            di[0] += 1
                            ACT(diag[dk], ident, AF.Copy, [("wgt", g), "ident"], [("diag", dk)], scale=wgt[:, s_:s_ + 1])
                            for half in range(2):
                                MM(pacc[half][:, :], diag[dk], Rb[rb][:, D + half * 512:D + (half + 1) * 512], s_ == 0, s_ == 127,
                                   [("diag", dk), ("R", rb)], [("pacc", half)])
                    for half in range(2):
                        STT(acc[:, half * 512:(half + 1) * 512], x1[b][:, half * 512:(half + 1) * 512], ALPHA, pacc[half][:, :],
                            ALU.mult, ALU.add, [("x1", b), ("pacc", half)], ["acc"])
                    layernorm(acc, lng, lnb, acc, wk, ["acc"], ["acc"], "ln")
                    DMA(x_dst[tb * 128:(tb + 1) * 128, :], acc, ["acc"], ["x2d"], "x2st")

                RSTEP = 2
                for _ in routing(0):
                    pass
                for tb in range(NTB):
                    evaluate(tb, routing(tb + 1) if tb + 1 < NTB else None)

        n = SC.finalize(block)
    return nc, n


def prep_shared(inp):
    f = lambda a: np.ascontiguousarray(np.asarray(a, dtype=np.float32))
    w_in = np.asarray(inp["w_in"], np.float32)
    out = {}
    out["w_in"] = f(w_in.reshape(L, 8, 128, 56, 128).transpose(0, 3, 2, 1, 4).reshape(L, 56, 128, 1024))
    for k in ("w_br_attn", "w_br_lru", "w_out"):
        out[k] = f(np.asarray(inp[k], np.float32).reshape(L, 8, 128, 1024).transpose(0, 2, 1, 3))
    wq = np.asarray(inp["peer_wq"], np.float32)
    out["peer_wq"] = f(wq.reshape(L, 8, 128, 16, 128).transpose(0, 3, 2, 1, 4).reshape(L, 16, 128, 1024))
    sk = np.asarray(inp["peer_subkeys"], np.float32)
    out["peer_skT"] = f(sk.transpose(0, 4, 1, 2, 3).reshape(L, 128, 16 * 128))
    for i in range(L):
        out["peer_u%d" % i] = f(np.asarray(inp["peer_u"][i], np.float32))
        out["peer_v%d" % i] = f(np.asarray(inp["peer_v"][i], np.float32))
    out["iota256"] = np.arange(256, dtype=np.float32)
    out["gate_a_w"] = f(np.asarray(inp["gate_a_w"], np.float32).transpose(0, 2, 1, 3).reshape(L, 128, 1024))
    out["gate_x_w"] = f(np.asarray(inp["gate_x_w"], np.float32).transpose(0, 2, 1, 3).reshape(L, 128, 1024))
    chp = np.zeros((L, 8, 128, 8), np.float32)
    cw = np.asarray(inp["conv_w"], np.float32)
    for k in range(4):
        chp[:, :, :, k] = cw[:, k, :].reshape(L, 8, 128)
    chp[:, :, :, 4] = np.asarray(inp["conv_b"], np.float32).reshape(L, 8, 128)
    chp[:, :, :, 5] = np.asarray(inp["gate_a_b"], np.float32).reshape(L, 8, 128)
    chp[:, :, :, 6] = np.asarray(inp["gate_x_b"], np.float32).reshape(L, 8, 128)
    chp[:, :, :, 7] = np.asarray(inp["lru_lambda"], np.float32).reshape(L, 8, 128)
    out["chp"] = f(chp.transpose(0, 2, 1, 3).reshape(L, 128, 64))
    out["lambda_qk"] = f(np.asarray(inp["lambda_qk"], np.float32).reshape(L, 256))
    for k in ("subln_g", "ln1_g", "ln1_b", "ln2_g", "ln2_b"):
        out[k] = f(inp[k])
    augk = np.zeros((3, 128), np.float32)
    augk[0] = np.arange(128)
    augk[1] = 1.0
    augk[2] = 1.0
    augq = np.zeros((3, 8, 512), np.float32)
    qq = np.arange(512)
    for h in range(8):
        ch = 8.0 * 2.0 ** (-(h + 1))
        augq[0, h] = ch
        augq[1, h] = -ch * 128.0 * (qq // 128)
        augq[2, h] = -ch * (qq % 128)
    out["aug_k"] = augk
    out["aug_q"] = f(augq.reshape(3, 8 * 512))
    return out


_CACHE = {}


def kernel(**inputs):
    shared = prep_shared(inputs)
    x = np.asarray(inputs["x"], np.float32)
    nb = x.shape[0]
    if "nc" not in _CACHE:
        _CACHE["nc"] = build_program()[0]
    nc = _CACHE["nc"]
    in_maps = []
    for b in range(nb):
        m = dict(shared)
        m["x"] = np.ascontiguousarray(x[b])
        in_maps.append(m)
    res = run_bass_kernel_spmd(nc, in_maps, core_ids=list(range(nb)))
    return np.stack([np.asarray(r["y"], np.float32) for r in res.results], axis=0)
```

```python
import math
import numpy as np
import concourse.bass as bass
import concourse.mybir as mybir
from concourse.bass_utils import run_bass_kernel_spmd
from contextlib import ExitStack

F32 = mybir.dt.float32
BF16 = mybir.dt.bfloat16
U32 = mybir.dt.uint32
I32 = mybir.dt.int32
AF = mybir.ActivationFunctionType
ALU = mybir.AluOpType
AX = mybir.AxisListType

D = 1024
S = 4096
L = 2
NTB = S // 128
ALPHA = (2.0 * L) ** 0.25
LN_EPS = 1e-5
RMS_EPS = 1e-6
NEXP = 16384
SAME_SYNC = True
SKIP_ARG = 130.0


class Op:
    __slots__ = ("eng", "fn", "deps", "is_dma", "sem", "count", "signals", "waits", "idx", "barrier")


class Sched:
    def __init__(self, nc, stack, same_engine_sync=True):
        self.nc = nc
        self.stack = stack
        self.ops = []
        self.last_w = {}
        self.readers = {}
        self.same = same_engine_sync
        self.semh = {}

    def sem(self, key):
        if key not in self.semh:
            self.semh[key] = self.stack.enter_context(self.nc.semaphore("s_%d" % len(self.semh)))
        return self.semh[key]

    def add(self, eng, fn, reads=(), writes=(), dma=None):
        op = Op()
        op.eng = eng
        op.fn = fn
        op.barrier = False
        op.is_dma = dma is not None
        op.sem = ("dma", dma) if dma is not None else ("eng", eng)
        op.signals = op.is_dma
        op.count = 0
        op.idx = len(self.ops)
        deps = set()
        for r in reads:
            w = self.last_w.get(r)
            if w is not None:
                deps.add(w)
        for w_ in writes:
            w = self.last_w.get(w_)
            if w is not None:
                deps.add(w)
            for rd in self.readers.get(w_, ()):
                deps.add(rd)
        op.deps = deps
        for r in reads:
            self.readers.setdefault(r, []).append(op.idx)
        for w_ in writes:
            self.last_w[w_] = op.idx
            self.readers[w_] = []
        self.ops.append(op)
        return op

    def barrier(self, exclude=("cv",)):
        last = {}
        for op in self.ops:
            if not op.barrier and not op.is_dma:
                last[op.eng] = op
        for op in last.values():
            op.signals = True
        for eng in ("sp", "act", "dve", "pool", "pe"):
            op = Op()
            op.eng = eng
            op.fn = None
            op.barrier = True
            op.is_dma = False
            op.sem = None
            op.signals = False
            op.count = 0
            op.idx = len(self.ops)
            op.deps = set()
            op.waits = tuple(("dma", k) for k in exclude)
            self.ops.append(op)
        keep = {k: v for k, v in self.last_w.items() if self.ops[v].is_dma and self.ops[v].sem[1] in exclude}
        self.last_w = keep
        self.readers = {}

    def _skip(self, dop, op):
        return (not dop.is_dma) and dop.eng == op.eng and (not op.is_dma) and (dop.eng == "pe" or not self.same)

    def finalize(self, block):
        ops = self.ops
        for op in ops:
            for d in op.deps:
                dop = ops[d]
                if dop.is_dma or self._skip(dop, op):
                    continue
                dop.signals = True
        cnt = {}
        for op in ops:
            if op.barrier:
                op.count = dict(cnt)
                continue
            if op.signals:
                inc = 16 if op.is_dma else 1
                cnt[op.sem] = cnt.get(op.sem, 0) + inc
                op.count = cnt[op.sem]
        waited = {}
        for op in ops:
            w = waited.setdefault(op.eng, {})
            if op.barrier:
                excl = op.waits
                need = {s: v for s, v in op.count.items() if s != ("eng", op.eng) and s not in excl}
                op.waits = []
            else:
                op.waits = []
                need = {}
                for d in op.deps:
                    dop = ops[d]
                    if self._skip(dop, op):
                        continue
                    if need.get(dop.sem, 0) < dop.count:
                        need[dop.sem] = dop.count
            for s, v in need.items():
                if w.get(s, 0) < v:
                    w[s] = v
                    op.waits.append((s, v))
        final_waits = [(s, v) for s, v in cnt.items() if s[0] == "dma"]
        for s in cnt:
            self.sem(s)
        per_eng = {}
        for op in ops:
            per_eng.setdefault(op.eng, []).append(op)

        def emit(engname, eng_obj, final=False):
            for op in per_eng.get(engname, []):
                for s, v in op.waits:
                    eng_obj.wait_ge(self.sem(s), v)
                if op.fn is None:
                    continue
                ins = op.fn(eng_obj)
                if op.signals:
                    ins.then_inc(self.sem(op.sem), 16 if op.is_dma else 1)
            if final:
                for s, v in final_waits:
                    eng_obj.wait_ge(self.sem(s), v)

        @block.sync
        def _(e):
            emit("sp", e, final=True)

        @block.scalar
        def _(e):
            emit("act", e)

        @block.vector
        def _(e):
            emit("dve", e)

        @block.gpsimd
        def _(e):
            emit("pool", e)

        @block.tensor
        def _(e):
            emit("pe", e)
        return len(ops)


class Arena:
    def __init__(self, tensor, ncols):
        self.t = tensor
        self.n = ncols
        self.off = 0

    def reset(self):
        self.off = 0

    def alloc(self, shape, dt):
        n = 1
        for s_ in shape:
            n *= s_
        if dt == BF16:
            ncol = (n + 1) // 2
        else:
            ncol = n
        ncol = (ncol + 15) // 16 * 16
        assert self.off + ncol <= self.n, ("arena overflow", self.off, ncol, self.n)
        v = self.t[:, self.off:self.off + ncol]
        self.off += ncol
        if dt != F32:
            v = v.bitcast(dt)
        v = v[:, 0:n]
        if len(shape) == 2:
            v = v.rearrange("p (a b) -> p a b", a=shape[0])
        elif len(shape) == 3:
            v = v.rearrange("p (a b c) -> p a b c", a=shape[0], b=shape[1])
        return v


def build_program(n_layers=L, debug=False, phases=("A0", "A1", "A2", "A3", "B")):
    nc = bass.Bass("TRN2", target_bir_lowering=False)

    def din(name, shape, dt=F32):
        return nc.dram_tensor(name, list(shape), dt, kind="ExternalInput").ap()

    def dscr(name, shape, dt):
        return nc.dram_tensor(name, list(shape), dt, kind="ExternalOutput" if debug else "Internal").ap()

    x_d = din("x", [S, D])
    win_d = din("w_in", [L, 56, 128, 1024])
    wba_d = din("w_br_attn", [L, 128, 8, 1024])
    wbl_d = din("w_br_lru", [L, 128, 8, 1024])
    wo_d = din("w_out", [L, 128, 8, 1024])
    wq_d = din("peer_wq", [L, 16, 128, 1024])
    sk_d = din("peer_skT", [L, 128, 16 * 128])
    pu_ds = [din("peer_u%d" % i, [NEXP, D]) for i in range(L)]
    pv_ds = [din("peer_v%d" % i, [NEXP, D]) for i in range(L)]
    iota_d = din("iota256", [256])
    gaw_d = din("gate_a_w", [L, 128, 8 * 128])
    gxw_d = din("gate_x_w", [L, 128, 8 * 128])
    chp_d = din("chp", [L, 128, 64])
    lq_d = din("lambda_qk", [L, 256])
    sg_d = din("subln_g", [L, 128])
    ln1g_d = din("ln1_g", [L, D])
    ln1b_d = din("ln1_b", [L, D])
    ln2g_d = din("ln2_g", [L, D])
    ln2b_d = din("ln2_b", [L, D])
    augk_d = din("aug_k", [3, 128])
    augq_d = din("aug_q", [3, 8 * 512])
    y_d = nc.dram_tensor("y", [S, D], F32, kind="ExternalOutput").ap()

    xT_d = dscr("xT_s", [D, S], BF16)
    yaT_d = dscr("yaT_s", [D, S], BF16)
    yrT_d = dscr("yrT_s", [D, S], BF16)
    x1_d = dscr("x1_s", [S, D], F32)
    x2_d = dscr("x2_s", [S, D], F32)
    tb16 = [nc.dram_tensor("tb16_%d" % i, [NEXP, 2 * D], BF16, kind="Internal").ap() for i in range(L)]

    with ExitStack() as st:
        ARN = 50000
        arena_t = st.enter_context(nc.sbuf_tensor("arena", [128, ARN], F32))
        cst_t = st.enter_context(nc.sbuf_tensor("cst", [128, 3200], F32))
        pf = [st.enter_context(nc.psum_tensor("pf%d" % i, [128, 512], F32)) for i in range(7)]
        pb = st.enter_context(nc.psum_tensor("pbb", [128, 1024], BF16))
        pbf = pb[:, :].bitcast(F32)
        block = st.enter_context(nc.Block())
        SC = Sched(nc, st, same_engine_sync=SAME_SYNC)
        AR = Arena(arena_t, ARN)
        CA = Arena(cst_t, 3200)

        def DMA(out, in_, reads, writes, key, q="sp"):
            SC.add(q, lambda e: e.dma_start(out=out, in_=in_), reads, writes, dma=key)

        def MM(out, lhsT, rhs, start, stop, reads, writes):
            SC.add("pe", lambda e: e.matmul(out, lhsT=lhsT, rhs=rhs, start=start, stop=stop), reads, writes)

        def TR(out, in_, reads, writes):
            SC.add("pe", lambda e: e.transpose(out=out, in_=in_, identity=ident), list(reads) + ["ident"], writes)

        def ACT(out, in_, func, reads, writes, bias=None, scale=None, accum=None):
            kw = {}
            if bias is not None:
                kw["bias"] = bias
            if scale is not None:
                kw["scale"] = scale
            if accum is not None:
                kw["accum_out"] = accum
            SC.add("act", lambda e: e.activation(out=out, in_=in_, func=func, **kw), reads, writes)

        def CP(eng, out, in_, reads, writes):
            if eng == "act":
                SC.add("act", lambda e: e.activation(out=out, in_=in_, func=AF.Copy), reads, writes)
            else:
                SC.add(eng, lambda e: e.tensor_copy(out=out, in_=in_), reads, writes)

        def TT(eng, out, in0, in1, op, reads, writes):
            SC.add(eng, lambda e: e.tensor_tensor(out=out, in0=in0, in1=in1, op=op), reads, writes)

        def TS(eng, out, in0, s1, s2, op0, op1, reads, writes, accum=None):
            if accum is None:
                if s2 is None:
                    SC.add(eng, lambda e: e.tensor_scalar(out=out, in0=in0, scalar1=s1, scalar2=None, op0=op0), reads, writes)
                else:
                    SC.add(eng, lambda e: e.tensor_scalar(out=out, in0=in0, scalar1=s1, scalar2=s2, op0=op0, op1=op1), reads, writes)
            else:
                SC.add(eng, lambda e: e.tensor_scalar(out=out, in0=in0, scalar1=s1, scalar2=s2, op0=op0, op1=op1, accum_out=accum), reads, writes)

        def STT(out, in0, scalar, in1, op0, op1, reads, writes, accum=None):
            if accum is None:
                SC.add("dve", lambda e: e.scalar_tensor_tensor(out=out, in0=in0, scalar=scalar, in1=in1, op0=op0, op1=op1), reads, writes)
            else:
                SC.add("dve", lambda e: e.scalar_tensor_tensor(out=out, in0=in0, scalar=scalar, in1=in1, op0=op0, op1=op1, accum_out=accum), reads, writes)

        def MEMSET(eng, out, val, writes):
            SC.add(eng, lambda e: e.memset(out, val), (), writes)

        identf = CA.alloc([128], F32)
        ident = CA.alloc([128], BF16)
        trif = CA.alloc([128], F32)
        tri = CA.alloc([128], BF16)
        augk_f = CA.alloc([128], F32)
        augq_f = AR.alloc([8 * 512], F32)
        augk = CA.alloc([128], BF16)
        augq = CA.alloc([8, 512], BF16)
        MEMSET("pool", identf, 1.0, ["identf"])
        SC.add("pool", lambda e: e.affine_select(out=identf, in_=identf, pattern=[[-1, 128]], compare_op=ALU.is_equal,
                                                 fill=0.0, base=0, channel_multiplier=1), ["identf"], ["identf"])
        CP("dve", ident, identf, ["identf"], ["ident"])
        MEMSET("pool", trif, 1.0, ["trif"])
        SC.add("pool", lambda e: e.affine_select(out=trif, in_=trif, pattern=[[1, 128]], compare_op=ALU.is_ge,
                                                 fill=0.0, base=0, channel_multiplier=-1), ["trif"], ["trif"])
        CP("dve", tri, trif, ["trif"], ["tri"])
        zrow = CA.alloc([512], BF16)
        MEMSET("pool", zrow, 0.0, ["zrow"])
        iota = CA.alloc([256], F32)
        DMA(iota, iota_d.partition_broadcast(128), [], ["iota"], "c2")
        DMA(augk_f[64:67, :], augk_d, [], ["augk_f"], "c0")
        DMA(augq_f[64:67, :], augq_d, [], ["augq_f"], "c1")
        CP("dve", augk[64:67, :], augk_f[64:67, :], ["augk_f"], ["augk"])
        CP("dve", augq[64:67, :, :], augq_f[64:67, :].rearrange("p (a b) -> p a b", a=8), ["augq_f"], ["augq"])

        def conv_dma(dst, src, key):
            SC.add("pool", lambda e: e.dma_start(out=dst, in_=src), ["xT"], [key], dma="cv")

        conv_chunks = []
        for l_ in range(n_layers):
            for (src, c0) in ((pu_ds[l_], 0), (pv_ds[l_], D)):
                for ch in range(4):
                    conv_chunks.append((tb16[l_][ch * 4096:(ch + 1) * 4096, c0:c0 + D], src[ch * 4096:(ch + 1) * 4096, :], ("tb16", l_)))

        def emit_conversion(n):
            for _ in range(n):
                if conv_chunks and "B" in phases:
                    conv_dma(*conv_chunks.pop(0))

        def layernorm(y, g_bc, b_bc, out, wk, rkeys, wkeys, tagk):
            stt, mv, lnv, rstd = wk["st"], wk["mv"], wk["lnv"], wk["rstd"]
            SC.add("dve", lambda e: e.bn_stats(out=stt[:, 0:6], in_=y[:, 0:512]), rkeys, [tagk + "st"])
            SC.add("dve", lambda e: e.bn_stats(out=stt[:, 6:12], in_=y[:, 512:1024]), rkeys, [tagk + "st"])
            SC.add("dve", lambda e: e.bn_aggr(out=mv, in_=stt), [tagk + "st"], [tagk + "mv"])
            ACT(lnv, mv[:, 1:2], AF.Ln, [tagk + "mv", "eps"], [tagk + "lnv"], bias=wk["eps"])
            ACT(rstd, lnv, AF.Exp, [tagk + "lnv"], [tagk + "rstd"], scale=-0.5)
            TS("dve", y, y, mv[:, 0:1], rstd, ALU.subtract, ALU.mult, list(rkeys) + [tagk + "mv", tagk + "rstd"], wkeys_y(rkeys))
            TT("pool", y, y, g_bc, ALU.mult, list(rkeys) + ["lnp"], wkeys_y(rkeys))
            TT("pool", out, y, b_bc, ALU.add, list(rkeys) + ["lnp"], wkeys)

        def wkeys_y(rkeys):
            return list(rkeys)

        for l in range(n_layers):
            lam_init = 0.8 - 0.6 * math.exp(-0.3 * l)
            x_src = x_d if l == 0 else x2_d
            x_dst = y_d if l == n_layers - 1 else x2_d
            xsrc_key = "x2d"
            SC.barrier()
            AR.reset()
            xT = AR.alloc([8, S], BF16)
            a1_mark = AR.off
            if "A0" in phases:
                xs = [AR.alloc([D], F32) for _ in range(2)]
                xb = [AR.alloc([D], BF16) for _ in range(2)]
                for tb in range(NTB):
                    b = tb % 2
                    DMA(xs[b], x_src[tb * 128:(tb + 1) * 128, :], [xsrc_key], [("xs", b)], "xs%d" % b)
                    CP("act", xb[b], xs[b], [("xs", b)], [("xb", b)])
                    for dc in range(8):
                        TR(pb[:, dc * 128:(dc + 1) * 128], xb[b][:, dc * 128:(dc + 1) * 128], [("xb", b)], ["pb"])
                    CP("dve", xT[:, :, tb * 128:(tb + 1) * 128], pb[:, :].rearrange("p (a b) -> p a b", a=8), ["pb"], ["xT"])
                for dc in range(8):
                    DMA(xT_d[dc * 128:(dc + 1) * 128, :], xT[:, dc, :], ["xT"], ["xTd"], "xTd")
                if "A1" not in phases:
                    emit_conversion(len(conv_chunks))

            if "A1" in phases:
                AR.off = a1_mark
                wst = [AR.alloc([3, 1024], F32) for _ in range(2)]
                wbf = [AR.alloc([3, 1024], BF16) for _ in range(2)]
                qT2 = AR.alloc([2, S], BF16)
                kT2 = AR.alloc([2, S], BF16)
                Va = AR.alloc([NTB, 130], BF16)
                NE = 6
                Eb = [AR.alloc([512], BF16) for _ in range(NE)]
                Osb = AR.alloc([4, 512], F32)
                obuf = AR.alloc([4, 128], F32)
                junk = AR.alloc([128], F32)
                yab = AR.alloc([4, 128], BF16)
                yst = [AR.alloc([512], BF16) for _ in range(2)]
                lq = AR.alloc([256], F32)
                sgb = AR.alloc([128], F32)
                gsc = AR.alloc([128], F32)
                sm = AR.alloc([32], F32)
                neglam = sm[:, 0:1]
                s12 = sm[:, 1:3]
                e12 = sm[:, 3:5]
                rz = sm[:, 8:16]
                rz2l = sm[:, 16:20]
                ss = sm[:, 20:24]
                lnv4 = sm[:, 24:28]
                rstd4 = sm[:, 28:32]
                epsr = AR.alloc([1], F32)
                MEMSET("pool", epsr, RMS_EPS, ["epsr"])
                MEMSET("pool", Va[:, :, 128:130], 1.0, ["Va1"])
                for c in range(2):
                    CP("pool", kT2[64:67, c, :].rearrange("p (a b) -> p a b", a=NTB),
                       augk[64:67, :].unsqueeze(1).broadcast_to([3, NTB, 128]), ["augk"], ["kT"])
                DMA(lq, lq_d[l].partition_broadcast(128), [], ["lq"], "p0")
                DMA(sgb, sg_d[l].partition_broadcast(128), [], ["sgb"], "p1")
                STT(junk[:, 0:64], lq[:, 0:64], 1.0, lq[:, 64:128], ALU.mult, ALU.mult, ["lq"], ["junk", "s1"], accum=s12[:, 0:1])
                STT(junk[:, 0:64], lq[:, 128:192], 1.0, lq[:, 192:256], ALU.mult, ALU.mult, ["lq"], ["junk", "s2"], accum=s12[:, 1:2])
                ACT(e12, s12, AF.Exp, ["s1", "s2"], ["e12"])
                TT("dve", neglam, e12[:, 1:2], e12[:, 0:1], ALU.subtract, ["e12"], ["neglam"])
                TS("dve", neglam, neglam, -lam_init, None, ALU.add, None, ["neglam"], ["neglam"])
                TS("dve", gsc, sgb, 1.0 - lam_init, None, ALU.mult, None, ["sgb"], ["gsc"])

                def load_w(h, slot):
                    for i, cb in enumerate((h, 8 + h, 16 + h)):
                        DMA(wst[slot][:, i, :], win_d[l, cb], [], [("wst", slot)], "wst%d" % slot)
                    CP("dve", wbf[slot], wst[slot], [("wst", slot)], [("wbf", slot)])

                def epilogue2(h, j):
                    SC.add("dve", lambda e: e.reciprocal(out=rz.rearrange("p (a b) -> p a b", a=4),
                                                         in_=Osb[:, :, 128:512:256]), ["Osb"], ["rz"])
                    TS("dve", rz2l, rz[:, 4:8], neglam, None, ALU.mult, None, ["rz", "neglam"], ["rz2l"])
                    for qs in range(4):
                        o1 = Osb[:, qs // 2, (qs % 2) * 256:(qs % 2) * 256 + 128]
                        o2 = Osb[:, 2 + qs // 2, (qs % 2) * 256:(qs % 2) * 256 + 128]
                        TS("dve", obuf[:, qs, :], o1, rz[:, qs:qs + 1], None, ALU.mult, None, ["Osb", "rz"], ["obuf"])
                        STT(obuf[:, qs, :], o2, rz2l[:, qs:qs + 1], obuf[:, qs, :], ALU.mult, ALU.add, ["Osb", "rz2l", "obuf"], ["obuf"])
                        STT(junk, obuf[:, qs, :], 1.0, obuf[:, qs, :], ALU.mult, ALU.mult, ["obuf"], ["junk", "ss"], accum=ss[:, qs:qs + 1])
                    ACT(lnv4, ss, AF.Ln, ["ss"], ["lnv4"], scale=1.0 / 128.0, bias=epsr)
                    ACT(rstd4, lnv4, AF.Exp, ["lnv4"], ["rstd4"], scale=-0.5)
                    for qs in range(4):
                        STT(yab[:, qs, :], obuf[:, qs, :], rstd4[:, qs:qs + 1], gsc, ALU.mult, ALU.mult, ["obuf", "rstd4", "gsc"], ["yab"])

                def epilogue3(h, j):
                    for qs in range(4):
                        TR(pb[:, qs * 128:(qs + 1) * 128], yab[:, qs, :], ["yab"], ["pb"])
                    ys = yst[j % 2]
                    CP("dve", ys, pb[:, 0:512], ["pb"], [("yst", j % 2)])
                    DMA(yaT_d[h * 128:(h + 1) * 128, j * 512:(j + 1) * 512], ys, [("yst", j % 2)], ["yaTd"], "yst%d" % (j % 2))

                load_w(0, 0)
                ei = 0
                si = 0
                for h in range(8):
                    slot = h % 2
                    if h + 1 < 8:
                        load_w(h + 1, 1 - slot)
                    emit_conversion(2)
                    slope = 2.0 ** (-(h + 1))
                    for c in range(2):
                        CP("dve", qT2[64:67, c, :].rearrange("p (a b) -> p a b", a=8),
                           augq[64:67, h, :].unsqueeze(1).broadcast_to([3, 8, 512]), ["augq"], ["qT"])
                    pj = [(pf[6], "pf6"), (pbf, "pb")]
                    pji = 0
                    for tq in range(8):
                        for (wi, dst, key) in ((0, qT2, "qT"), (1, kT2, "kT")):
                            pp_, pk_ = pj[pji % 2]
                            pji += 1
                            for dc in range(8):
                                MM(pp_[:, :], wbf[slot][:, wi, dc * 128:(dc + 1) * 128], xT[:, dc, tq * 512:(tq + 1) * 512],
                                   dc == 0, dc == 7, [("wbf", slot), "xT"], [pk_])
                            CP("dve", dst[0:64, 0, tq * 512:(tq + 1) * 512], pp_[0:64, :], [pk_], [key])
                            CP("act", dst[0:64, 1, tq * 512:(tq + 1) * 512], pp_[64:128, :], [pk_], [key])
                    for tb4 in range(8):
                        pp_, pk_ = pj[pji % 2]
                        pji += 1
                        for t in range(4):
                            tb = tb4 * 4 + t
                            for dc in range(8):
                                MM(pp_[:, t * 128:(t + 1) * 128], xT[:, dc, tb * 128:(tb + 1) * 128],
                                   wbf[slot][:, 2, dc * 128:(dc + 1) * 128], dc == 0, dc == 7, [("wbf", slot), "xT"], [pk_])
                        CP("dve", Va[:, tb4 * 4:(tb4 + 1) * 4, 0:128], pp_[:, :].rearrange("p (a b) -> p a b", a=4), [pk_], ["Va"])
                    def live(j, kb):
                        return slope * (j * 512 - kb * 128 - 127) <= SKIP_ARG
                    tiles = [(j, c, kb) for j in range(8) for c in range(2) for kb in range(4 * j + 4) if live(j, kb)]
                    first_of_j = {}
                    for (j_, c_, kb_) in tiles:
                        first_of_j.setdefault(j_, (c_, kb_))
                    sinfo = {}

                    def emit_S(i):
                        nonlocal si
                        j, c, kb = tiles[i]
                        r = kb - 4 * j
                        nq0 = max(0, r) * 128
                        sb_ = pf[4 + si % 2]
                        skey = ("S", si % 2)
                        si += 1
                        MM(sb_[:, nq0:512], kT2[0:67, c, kb * 128:(kb + 1) * 128],
                           qT2[0:67, c, j * 512 + nq0:(j + 1) * 512], True, True, ["qT", "kT"], [skey])
                        sinfo[i] = (sb_, skey)

                    pending = []

                    def emit_rest(i):
                        nonlocal ei
                        j, c, kb = tiles[i]
                        r = kb - 4 * j
                        nq0 = max(0, r) * 128
                        sb_, skey = sinfo.pop(i)
                        if (c, kb) == first_of_j[j]:
                            for bnk in range(4):
                                MM(pf[bnk][:, :], zrow[0:1, 0:128], zrow[0:1, 0:512], True, False, ["zrow"], [("O", bnk)])
                        E = Eb[ei % NE]
                        ekey = ("E", ei % NE)
                        ei += 1
                        ACT(E[:, nq0:512], sb_[:, nq0:512], AF.Exp, [skey], [ekey], scale=0.125,
                            bias=float(slope * (kb * 128 - j * 512)))
                        if r >= 0:
                            TT("dve", E[:, r * 128:(r + 1) * 128], E[:, r * 128:(r + 1) * 128], tri, ALU.mult, [ekey, "tri"], [ekey])
                        for qs in range(max(0, r), 4):
                            ob = pf[c * 2 + qs // 2]
                            MM(ob[:, (qs % 2) * 256:(qs % 2) * 256 + 129], E[:, qs * 128:(qs + 1) * 128], Va[:, kb, 0:129],
                               False, kb == 4 * j + qs, [ekey, "Va", "Va1"], [("O", c * 2 + qs // 2)])
                        if c == 1 and kb == 4 * j + 3:
                            for bnk in range(4):
                                CP("dve", Osb[:, bnk, :], pf[bnk][:, :], [("O", bnk)], ["Osb"])
                            pending.append((i + 4, 2, h, j))
                            pending.append((i + 12, 3, h, j))
                        while pending and (pending[0][0] <= i or i == len(tiles) - 1):
                            _, kind, hh, jj = pending.pop(0)
                            (epilogue2 if kind == 2 else epilogue3)(hh, jj)

                    emit_S(0)
                    for i in range(len(tiles)):
                        if i + 1 < len(tiles):
                            emit_S(i + 1)
                        emit_rest(i)

            if "A2" in phases:
                SC.barrier()
                AR.off = a1_mark
                B0 = AR.alloc([S + 16], F32)
                B1 = AR.alloc([S], F32)
                B2 = AR.alloc([S], F32)
                B3 = AR.alloc([S], F32)
                xcb = AR.alloc([S], BF16)
                Yb = AR.alloc([S], BF16)
                wst2 = [AR.alloc([2, 1024], F32) for _ in range(2)]
                wbf2 = [AR.alloc([2, 1024], BF16) for _ in range(2)]
                gwf = AR.alloc([2, 1024], F32)
                gwb = AR.alloc([2, 8, 128], BF16)
                chp = AR.alloc([8, 8], F32)
                cc = AR.alloc([8, 4], F32)
                DMA(gwf[:, 0, :], gaw_d[l], [], ["gwf"], "p2")
                DMA(gwf[:, 1, :], gxw_d[l], [], ["gwf"], "p2")
                CP("dve", gwb, gwf.rearrange("p a (g j) -> p a g j", g=8), ["gwf"], ["gwb"])
                DMA(chp, chp_d[l].rearrange("p (g f) -> p g f", g=8), [], ["chp"], "p3")
                ACT(cc[:, :, 2], chp[:, :, 7], AF.Exp, ["chp"], ["cc"], scale=-1.0)
                ACT(cc[:, :, 3], cc[:, :, 2], AF.Ln, ["cc"], ["cc"], bias=1.0)
                TS("dve", cc[:, :, 0], cc[:, :, 3], -8.0, None, ALU.mult, None, ["cc"], ["cc"])
                TS("dve", cc[:, :, 1], cc[:, :, 3], -16.0, None, ALU.mult, None, ["cc"], ["cc"])

                def load_w2(g, slot):
                    DMA(wst2[slot][:, 0, :], win_d[l, 24 + g], [], [("wst2", slot)], "wst2%d" % slot)
                    DMA(wst2[slot][:, 1, :], win_d[l, 32 + g], [], [("wst2", slot)], "wst2%d" % slot)
                    CP("pool", wbf2[slot], wst2[slot], [("wst2", slot)], [("wbf2", slot)])

                load_w2(0, 0)
                pi = 0
                for g in range(8):
                    slot = g % 2
                    if g + 1 < 8:
                        load_w2(g + 1, 1 - slot)
                    MEMSET("pool", B0[:, 0:3], 0.0, ["B0"])
                    for tq in range(8):
                        pp = pf[pi % 7]
                        pk = ("pf", pi % 7)
                        pi += 1
                        for dc in range(8):
                            MM(pp[:, :], wbf2[slot][:, 0, dc * 128:(dc + 1) * 128], xT[:, dc, tq * 512:(tq + 1) * 512], dc == 0, dc == 7,
                               [("wbf2", slot), "xT"], [pk])
                        CP("act", B0[:, 3 + tq * 512:3 + (tq + 1) * 512], pp[:, :], [pk], ["B0"])
                    TS("dve", B1, B0[:, 0:S], chp[:, g, 0:1], chp[:, g, 4:5], ALU.mult, ALU.add, ["B0", "chp"], ["B1"])
                    for k in range(1, 4):
                        STT(B1, B0[:, k:k + S], chp[:, g, k:k + 1], B1, ALU.mult, ALU.add, ["B0", "chp", "B1"], ["B1"])
                    CP("pool", xcb, B1, ["B1"], ["xcb"])
                    for (wi, dst, off, key, bcol) in ((0, B2, 0, "B2", 5), (1, B0, 3, "B0", 6)):
                        for tq in range(8):
                            pp = pf[pi % 7]
                            pk = ("pf", pi % 7)
                            pi += 1
                            MM(pp[:, :], gwb[:, wi, g, :], xcb[:, tq * 512:(tq + 1) * 512], True, True, ["gwb", "xcb"], [pk])
                            ACT(dst[:, off + tq * 512:off + (tq + 1) * 512], pp[:, :], AF.Sigmoid, [pk, "chp", "B1"], [key],
                                bias=chp[:, g, bcol:bcol + 1])
                    ACT(B3, B2, AF.Exp, ["B2", "cc"], ["B3"], scale=cc[:, g, 1:2])
                    ACT(B3, B3, AF.Sqrt, ["B3"], ["B3"], scale=-1.0, bias=1.0)
                    MEMSET("pool", B3[:, 0:1], 1.0, ["B3"])
                    ACT(B2, B2, AF.Exp, ["B2", "cc"], ["B2"], scale=cc[:, g, 0:1])
                    TT("dve", B0[:, 3:3 + S], B0[:, 3:3 + S], B1, ALU.mult, ["B0", "B1"], ["B0"])
                    TT("dve", B0[:, 3:3 + S], B0[:, 3:3 + S], B3, ALU.mult, ["B0", "B3"], ["B0"])
                    SC.add("dve", lambda e: e.tensor_tensor_scan(out=B3, data0=B2, data1=B0[:, 3:3 + S], initial=0.0,
                                                                 op0=ALU.mult, op1=ALU.add), ["B2", "B0", "B3"], ["B3"])
                    for tq in range(8):
                        pp = pf[pi % 7]
                        pk = ("pf", pi % 7)
                        pi += 1
                        for dc in range(8):
                            MM(pp[:, :], wbf2[slot][:, 1, dc * 128:(dc + 1) * 128], xT[:, dc, tq * 512:(tq + 1) * 512], dc == 0, dc == 7,
                               [("wbf2", slot), "xT"], [pk])
                        ACT(B1[:, tq * 512:(tq + 1) * 512], pp[:, :], AF.Gelu_apprx_tanh, [pk, "B0"], ["B1"])
                    TT("dve", Yb, B3, B1, ALU.mult, ["B3", "B1", "yrTd"], ["Yb"])
                    DMA(yrT_d[g * 128:(g + 1) * 128, :], Yb, ["Yb"], ["yrTd"], "yrTd")

            if "A3" in phases:
                SC.barrier()
                AR.reset()
                wg = AR.alloc([16, 8, 128], BF16)
                wba = AR.alloc([8, 1024], BF16)
                wbl = AR.alloc([8, 1024], BF16)
                wo = AR.alloc([8, 1024], BF16)
                stg = [AR.alloc([1024], F32) for _ in range(2)]
                lng = AR.alloc([D], F32)
                lnb = AR.alloc([D], F32)
                DMA(lng, ln1g_d[l].partition_broadcast(128), [], ["lnp"], "p4")
                DMA(lnb, ln1b_d[l].partition_broadcast(128), [], ["lnp"], "p4")
                si_ = 0
                for cb in range(16):
                    b = si_ % 2
                    si_ += 1
                    DMA(stg[b], win_d[l, 40 + cb], [], [("stg", b)], "stg%d" % b)
                    CP("dve" if cb % 2 else "pool", wg[:, cb, :, :], stg[b].rearrange("p (a b) -> p a b", a=8), [("stg", b)], ["wg"])
                for (src, dst, key) in ((wba_d, wba, "wba"), (wbl_d, wbl, "wbl"), (wo_d, wo, "wo")):
                    for kc in range(8):
                        b = si_ % 2
                        si_ += 1
                        DMA(stg[b], src[l, :, kc, :], [], [("stg", b)], "stg%d" % b)
                        CP("dve" if kc % 2 else "pool", dst[:, kc, :], stg[b], [("stg", b)], [key])
                xTb = [AR.alloc([8, 512], BF16) for _ in range(2)]
                yaTb = [AR.alloc([8, 512], BF16) for _ in range(2)]
                yrTb = [AR.alloc([8, 512], BF16) for _ in range(2)]
                mT = AR.alloc([8, 512], BF16)
                sga = [AR.alloc([512], F32) for _ in range(2)]
                sgr = [AR.alloc([512], F32) for _ in range(2)]
                xres = [AR.alloc([D], F32) for _ in range(2)]
                yln = [AR.alloc([D], F32) for _ in range(2)]
                wk = {"st": AR.alloc([12], F32), "mv": AR.alloc([2], F32), "lnv": AR.alloc([1], F32),
                      "rstd": AR.alloc([1], F32), "eps": AR.alloc([1], F32)}
                MEMSET("pool", wk["eps"], LN_EPS, ["eps"])
                ti = 0
                for tq in range(8):
                    b = tq % 2
                    DMA(xTb[b], xT_d[:, tq * 512:(tq + 1) * 512].rearrange("(a p) t -> p a t", p=128), ["xTd"], [("xTb", b)], "xTb%d" % b)
                    DMA(yaTb[b], yaT_d[:, tq * 512:(tq + 1) * 512].rearrange("(a p) t -> p a t", p=128), ["yaTd"], [("yaTb", b)], "yaTb%d" % b)
                    DMA(yrTb[b], yrT_d[:, tq * 512:(tq + 1) * 512].rearrange("(a p) t -> p a t", p=128), ["yrTd"], [("yrTb", b)], "yrTb%d" % b)
                    for jb in range(8):
                        for dc in range(8):
                            MM(pf[0][:, :], wg[:, jb, dc, :], xTb[b][:, dc, :], dc == 0, dc == 7, ["wg", ("xTb", b)], ["pf0"])
                        for dc in range(8):
                            MM(pf[1][:, :], wg[:, 8 + jb, dc, :], xTb[b][:, dc, :], dc == 0, dc == 7, ["wg", ("xTb", b)], ["pf1"])
                        for kc in range(8):
                            MM(pf[2][:, :], wba[:, kc, jb * 128:(jb + 1) * 128], yaTb[b][:, kc, :], kc == 0, kc == 7, ["wba", ("yaTb", b)], ["pf2"])
                        for kc in range(8):
                            MM(pf[3][:, :], wbl[:, kc, jb * 128:(jb + 1) * 128], yrTb[b][:, kc, :], kc == 0, kc == 7, ["wbl", ("yrTb", b)], ["pf3"])
                        sb2 = jb % 2
                        ACT(sga[sb2], pf[0][:, :], AF.Sigmoid, ["pf0"], [("sga", sb2)])
                        ACT(sgr[sb2], pf[1][:, :], AF.Sigmoid, ["pf1"], [("sgr", sb2)])
                        TT("dve", sga[sb2], sga[sb2], pf[2][:, :], ALU.mult, [("sga", sb2), "pf2"], [("sga", sb2)])
                        TT("dve", sgr[sb2], sgr[sb2], pf[3][:, :], ALU.mult, [("sgr", sb2), "pf3"], [("sgr", sb2)])
                        TT("pool", mT[:, jb, :], sga[sb2], sgr[sb2], ALU.add, [("sga", sb2), ("sgr", sb2)], ["mT"])
                    for ts_ in range(4):
                        tb = tq * 4 + ts_
                        xb_ = ti % 2
                        ti += 1
                        DMA(xres[xb_], x_src[tb * 128:(tb + 1) * 128, :], [xsrc_key], [("xres", xb_)], "xres%d" % xb_)
                        for nh in range(2):
                            for jb in range(8):
                                MM(pf[4 + nh][:, :], mT[:, jb, ts_ * 128:(ts_ + 1) * 128], wo[:, jb, nh * 512:(nh + 1) * 512], jb == 0, jb == 7,
                                   ["mT", "wo"], [("pf", 4 + nh)])
                            STT(yln[xb_][:, nh * 512:(nh + 1) * 512], xres[xb_][:, nh * 512:(nh + 1) * 512], ALPHA, pf[4 + nh][:, :],
                                ALU.mult, ALU.add, [("xres", xb_), ("pf", 4 + nh)], [("yln", xb_)])
                        layernorm(yln[xb_], lng, lnb, yln[xb_], wk, [("yln", xb_)], [("yln", xb_)], "ln")
                        DMA(x1_d[tb * 128:(tb + 1) * 128, :], yln[xb_], [("yln", xb_)], ["x1d"], "x1st%d" % xb_)

            if "B" in phases:
                SC.barrier()
                AR.reset()
                wq = AR.alloc([16, 8, 128], BF16)
                skT = AR.alloc([16, 128], BF16)
                lng = AR.alloc([D], F32)
                lnb = AR.alloc([D], F32)
                bmark = AR.off
                stg = [AR.alloc([2048], F32) for _ in range(2)]
                DMA(lng, ln2g_d[l].partition_broadcast(128), [], ["lnp"], "p4")
                DMA(lnb, ln2b_d[l].partition_broadcast(128), [], ["lnp"], "p4")
                for cb in range(16):
                    b = cb % 2
                    DMA(stg[b][:, 0:1024], wq_d[l, cb], [], [("stg", b)], "stg%d" % b)
                    CP("dve" if cb % 2 else "pool", wq[:, cb, :, :], stg[b][:, 0:1024].rearrange("p (a b) -> p a b", a=8), [("stg", b)], ["wq"])
                DMA(stg[0], sk_d[l], [], [("stg", 0)], "stg0")
                CP("dve", skT, stg[0].rearrange("p (a b) -> p a b", a=16), [("stg", 0)], ["skT"])
                SC.barrier()
                AR.off = bmark
                x1 = [AR.alloc([D], F32) for _ in range(2)]
                x1b = [AR.alloc([D], BF16) for _ in range(2)]
                x1T = AR.alloc([8, 128], BF16)
                qTs = AR.alloc([16, 128], BF16)
                scs = AR.alloc([16, 128], F32)
                top = AR.alloc([16, 16], F32)
                tix = AR.alloc([16, 16], U32)
                tixf = AR.alloc([16, 16], F32)
                work = AR.alloc([256], F32)
                work2 = AR.alloc([256], F32)
                wka = AR.alloc([128], F32)
                wkb = AR.alloc([128], F32)
                cand = AR.alloc([8, 256], F32)
                eid = scs.rearrange("p a b -> p (a b)").rearrange("p (h x) -> p h x", h=8)
                tsv = AR.alloc([8, 16], F32)
                pos = AR.alloc([8, 16], U32)
                posf = AR.alloc([8, 16], F32)
                ef = AR.alloc([128], F32)
                af_ = AR.alloc([128], F32)
                bf_ = AR.alloc([128], F32)
                e1_ = af_
                e2_ = bf_
                eidx = [AR.alloc([128], I32) for _ in range(2)]
                gt = [AR.alloc([8, 16], F32) for _ in range(2)]
                dsm = AR.alloc([8, 16], F32)
                zs = AR.alloc([8], F32)
                actv = AR.alloc([128], F32)
                wgt = AR.alloc([128], F32)
                acc = AR.alloc([D], F32)
                junkb = AR.alloc([D], BF16)
                ND = 8
                diag = [AR.alloc([128], BF16) for _ in range(ND)]
                wk = {"st": AR.alloc([12], F32), "mv": AR.alloc([2], F32), "lnv": AR.alloc([1], F32),
                      "rstd": AR.alloc([1], F32), "eps": AR.alloc([1], F32)}
                NG = 4
                LOOK = 5
                NRB = (LOOK + 1) * NG
                print("phaseB arena before Rb:", AR.off, "NRB", NRB, "ARN", ARN)
                Rb = [AR.alloc([2 * D], BF16) for _ in range(NRB)]
                MEMSET("pool", wk["eps"], LN_EPS, ["eps"])
                tbl = tb16[l]
                tkey = ("tb16", l)
                pacc = [pf[5], pf[6]]
                pbx = [pf[3][:, :].bitcast(BF16), pf[4][:, :].bitcast(BF16)]

                def routing(tb):
                    b = tb % 2
                    DMA(x1[b], x1_d[tb * 128:(tb + 1) * 128, :], ["x1d"], [("x1", b)], "x1ld%d" % b)
                    CP("act", x1b[b], x1[b], [("x1", b)], [("x1b", b)])
                    for dc in range(8):
                        TR(pb[:, dc * 128:(dc + 1) * 128], x1b[b][:, dc * 128:(dc + 1) * 128], [("x1b", b)], ["pb"])
                    yield
                    CP("act", x1T, pb[:, :].rearrange("p (a b) -> p a b", a=8), ["pb"], ["x1T"])
                    for dc in range(8):
                        TR(pbx[b][:, dc * 128:(dc + 1) * 128], x1T[:, dc, :], ["x1T"], [("pbx", b)])
                    for c4 in range(4):
                        pp = pf[0]
                        pk = ("pf", 0)
                        for ci in range(4):
                            cb = c4 * 4 + ci
                            for dc in range(8):
                                MM(pp[:, ci * 128:(ci + 1) * 128], wq[:, cb, dc, :], x1T[:, dc, :], dc == 0, dc == 7, ["wq", "x1T"], [pk])
                        CP("act", qTs[:, c4 * 4:(c4 + 1) * 4, :], pp[:, :].rearrange("p (a b) -> p a b", a=4), [pk], ["qTs"])
                        yield
                    for c4 in range(4):
                        pp = pf[1 + c4 % 2]
                        pk = ("pf", 1 + c4 % 2)
                        for ci in range(4):
                            cb = c4 * 4 + ci
                            MM(pp[:, ci * 128:(ci + 1) * 128], qTs[:, cb, :], skT[:, cb, :], True, True, ["qTs", "skT"], [pk])
                        CP("act", scs[:, c4 * 4:(c4 + 1) * 4, :], pp[:, :].rearrange("p (a b) -> p a b", a=4), [pk], ["scs"])
                        yield
                    for g0 in range(0, 16, 2):
                        gs = (g0, g0 + 1)
                        wk2 = (wka, wkb)
                        for g, w_ in zip(gs, wk2):
                            SC.add("dve", lambda e, g=g: e.max(out=top[:, g, 0:8], in_=scs[:, g, :]), ["scs"], [("top", g)])
                        for g, w_ in zip(gs, wk2):
                            SC.add("dve", lambda e, g=g: e.max_index(out=tix[:, g, 0:8], in_max=top[:, g, 0:8], in_values=scs[:, g, :]),
                                   ["scs", ("top", g)], [("tix", g)])
                        for g, w_ in zip(gs, wk2):
                            SC.add("dve", lambda e, g=g, w_=w_: e.match_replace(out=w_, in_to_replace=top[:, g, 0:8], in_values=scs[:, g, :],
                                                                                 imm_value=-1e30), ["scs", ("top", g)], [("wk1", g % 2)])
                        for g, w_ in zip(gs, wk2):
                            SC.add("dve", lambda e, g=g, w_=w_: e.max(out=top[:, g, 8:16], in_=w_), [("wk1", g % 2)], [("top", g)])
                        for g, w_ in zip(gs, wk2):
                            SC.add("dve", lambda e, g=g, w_=w_: e.max_index(out=tix[:, g, 8:16], in_max=top[:, g, 8:16], in_values=w_),
                                   [("wk1", g % 2), ("top", g)], [("tix", g)])
                        yield
                    CP("dve", tixf, tix, [("tix", g) for g in range(16)], ["tixf"])
                    top4 = top.rearrange("p (h c) k -> p h c k", c=2)
                    tix4 = tixf.rearrange("p (h c) k -> p h c k", c=2)
                    cand4 = cand.rearrange("p h (a b) -> p h a b", a=16)
                    eid4 = eid.rearrange("p h (a b) -> p h a b", a=16)
                    TT("dve", cand4, top4[:, :, 0, :].unsqueeze(3).broadcast_to([128, 8, 16, 16]),
                       top4[:, :, 1, :].unsqueeze(2).broadcast_to([128, 8, 16, 16]), ALU.add, [("top", g) for g in range(16)], ["cand"])
                    TS("dve", tix4[:, :, 0, :], tix4[:, :, 0, :], 128.0, None, ALU.mult, None, ["tixf"], ["tixf"])
                    for h0 in range(0, 8, 2):
                        hs = (h0, h0 + 1)
                        wks = (work, work2)
                        for h, w_ in zip(hs, wks):
                            SC.add("dve", lambda e, h=h: e.max(out=tsv[:, h, 0:8], in_=cand[:, h, :]), ["cand"], [("tsv", h)])
                        for h, w_ in zip(hs, wks):
                            SC.add("dve", lambda e, h=h: e.max_index(out=pos[:, h, 0:8], in_max=tsv[:, h, 0:8], in_values=cand[:, h, :]),
                                   ["cand", ("tsv", h)], [("pos", h)])
                        for h, w_ in zip(hs, wks):
                            SC.add("dve", lambda e, h=h, w_=w_: e.match_replace(out=w_, in_to_replace=tsv[:, h, 0:8], in_values=cand[:, h, :],
                                                                                 imm_value=-1e30), ["cand", ("tsv", h)], [("work", h % 2)])
                        for h, w_ in zip(hs, wks):
                            SC.add("dve", lambda e, h=h, w_=w_: e.max(out=tsv[:, h, 8:16], in_=w_), [("work", h % 2)], [("tsv", h)])
                        for h, w_ in zip(hs, wks):
                            SC.add("dve", lambda e, h=h, w_=w_: e.max_index(out=pos[:, h, 8:16], in_max=tsv[:, h, 8:16], in_values=w_),
                                   [("work", h % 2), ("tsv", h)], [("pos", h)])
                        yield
                    CP("dve", posf, pos, [("pos", h) for h in range(8)], ["posf"])
                    yield
                    posflat = posf.rearrange("p h k -> p (h k)")
                    ge3 = cand.rearrange("p h x -> p (h x)")[:, 0:128 * 15].rearrange("p (s m) -> p s m", m=15)
                    TT("dve", ge3, posflat.unsqueeze(2).broadcast_to([128, 128, 15]),
                       iota[:, 16:256:16].unsqueeze(1).broadcast_to([128, 128, 15]), ALU.is_ge, ["posf", "iota", "cand"], ["cand"])
                    SC.add("dve", lambda e: e.tensor_reduce(out=af_, in_=ge3, axis=AX.X, op=ALU.add), ["cand"], ["af"])
                    STT(bf_, af_, -16.0, posflat, ALU.mult, ALU.add, ["af", "posf"], ["bf"])
                    yield
                    io16 = iota[:, 0:16].unsqueeze(1).unsqueeze(1).broadcast_to([128, 8, 16, 16])
                    for (src_, lst, dst_, key) in ((af_, 0, e1_, "e1"), (bf_, 1, e2_, "e2")):
                        TT("dve", eid4, src_.rearrange("p (h k) -> p h k", h=8).unsqueeze(3).broadcast_to([128, 8, 16, 16]), io16,
                           ALU.is_equal, ["af", "bf", "iota", "scs"], ["scs"])
                        TT("dve", eid4, eid4, tix4[:, :, lst, :].unsqueeze(2).broadcast_to([128, 8, 16, 16]), ALU.mult, ["scs", "tixf"], ["scs"])
                        SC.add("dve", lambda e, dst_=dst_: e.tensor_reduce(out=dst_, in_=eid.rearrange("p h (k a) -> p (h k) a", a=16),
                                                                           axis=AX.X, op=ALU.add), ["scs"], ["af", "bf"])
                        yield
                    TT("dve", ef, e1_, e2_, ALU.add, ["af", "bf"], ["ef"])
                    CP("dve", eidx[b], ef, ["ef"], [("eidx", b)])
                    yield
                    TT("dve", dsm, tsv, tsv[:, :, 0:1].broadcast_to([128, 8, 16]), ALU.subtract, [("tsv", h) for h in range(8)], ["dsm"])
                    ACT(dsm, dsm, AF.Exp, ["dsm"], ["dsm"])
                    SC.add("dve", lambda e: e.tensor_reduce(out=zs, in_=dsm, axis=AX.X, op=ALU.add), ["dsm"], ["zs"])
                    SC.add("dve", lambda e: e.reciprocal(out=zs, in_=zs), ["zs"], ["zs"])
                    TT("dve", gt[b], dsm, zs.unsqueeze(2).broadcast_to([128, 8, 16]), ALU.mult, ["dsm", "zs"], [("gt", b)])

                NGR = 128 // NG
                gi = [0]
                di = [0]
                slotbuf = {}
                issued = set()

                def gathers(tb, g):
                    if (tb, g) in issued:
                        return
                    issued.add((tb, g))
                    b = tb % 2
                    for i in range(NG):
                        s_ = g * NG + i
                        rb = gi[0] % NRB
                        gi[0] += 1
                        slotbuf[(tb, s_)] = rb
                        SC.add("pool", lambda e, s_=s_, rb=rb, b=b, tbl=tbl: e.indirect_dma_start(
                            out=Rb[rb], out_offset=None, in_=tbl, in_offset=bass.IndirectOffsetOnAxis(ap=eidx[b][:, s_:s_ + 1], axis=0)),
                            [("eidx", b), tkey], [("R", rb)], dma="g%d" % rb)

                def evaluate(tb, rgen):
                    b = tb % 2
                    gtf = gt[b].rearrange("p h k -> p (h k)")
                    for g in range(LOOK):
                        gathers(tb, g)
                    for g in range(NGR):
                        if rgen is not None:
                            if g + LOOK >= NGR - 1:
                                for _ in rgen:
                                    pass
                            else:
                                for _ in range(RSTEP):
                                    next(rgen, None)
                        if g + LOOK < NGR:
                            gathers(tb, g + LOOK)
                        elif tb + 1 < NTB:
                            gathers(tb + 1, g + LOOK - NGR)
                        sl = slice(g * NG, (g + 1) * NG)
                        for i in range(NG):
                            s_ = g * NG + i
                            rb = slotbuf[(tb, s_)]
                            STT(junkb, Rb[rb][:, 0:D], 1.0, pbx[b], ALU.mult, ALU.mult, [("R", rb), ("pbx", b)], [("actv", s_)],
                                accum=actv[:, s_:s_ + 1])
                        ACT(wgt[:, sl], actv[:, sl], AF.Gelu_apprx_tanh, [("actv", g * NG + i) for i in range(NG)], [("wgt", g)])
                        TT("dve", wgt[:, sl], wgt[:, sl], gtf[:, sl], ALU.mult, [("wgt", g), ("gt", b)], [("wgt", g)])
                        for i in range(NG):
                            s_ = g * NG + i
                            rb = slotbuf.pop((tb, s_))
                            dk = di[0] % ND
                            di[0] += 1
                            ACT(diag[dk], ident, AF.Copy, [("wgt", g), "ident"], [("diag", dk)], scale=wgt[:, s_:s_ + 1])
                            for half in range(2):
                                MM(pacc[half][:, :], diag[dk], Rb[rb][:, D + half * 512:D + (half + 1) * 512], s_ == 0, s_ == 127,
                                   [("diag", dk), ("R", rb)], [("pacc", half)])
                    for half in range(2):
                        STT(acc[:, half * 512:(half + 1) * 512], x1[b][:, half * 512:(half + 1) * 512], ALPHA, pacc[half][:, :],
                            ALU.mult, ALU.add, [("x1", b), ("pacc", half)], ["acc"])
                    layernorm(acc, lng, lnb, acc, wk, ["acc"], ["acc"], "ln")
                    DMA(x_dst[tb * 128:(tb + 1) * 128, :], acc, ["acc"], ["x2d"], "x2st")

                RSTEP = 1
                for _ in routing(0):
                    pass
                for tb in range(NTB):
                    evaluate(tb, routing(tb + 1) if tb + 1 < NTB else None)

        n = SC.finalize(block)
    return nc, n


def prep_shared(inp):
    f = lambda a: np.ascontiguousarray(np.asarray(a, dtype=np.float32))
    w_in = np.asarray(inp["w_in"], np.float32)
    out = {}
    out["w_in"] = f(w_in.reshape(L, 8, 128, 56, 128).transpose(0, 3, 2, 1, 4).reshape(L, 56, 128, 1024))
    for k in ("w_br_attn", "w_br_lru", "w_out"):
        out[k] = f(np.asarray(inp[k], np.float32).reshape(L, 8, 128, 1024).transpose(0, 2, 1, 3))
    wq = np.asarray(inp["peer_wq"], np.float32)
    out["peer_wq"] = f(wq.reshape(L, 8, 128, 16, 128).transpose(0, 3, 2, 1, 4).reshape(L, 16, 128, 1024))
    sk = np.asarray(inp["peer_subkeys"], np.float32)
    out["peer_skT"] = f(sk.transpose(0, 4, 1, 2, 3).reshape(L, 128, 16 * 128))
    for i in range(L):
        out["peer_u%d" % i] = f(np.asarray(inp["peer_u"][i], np.float32))
        out["peer_v%d" % i] = f(np.asarray(inp["peer_v"][i], np.float32))
    out["iota256"] = np.arange(256, dtype=np.float32)
    out["gate_a_w"] = f(np.asarray(inp["gate_a_w"], np.float32).transpose(0, 2, 1, 3).reshape(L, 128, 1024))
    out["gate_x_w"] = f(np.asarray(inp["gate_x_w"], np.float32).transpose(0, 2, 1, 3).reshape(L, 128, 1024))
    chp = np.zeros((L, 8, 128, 8), np.float32)
    cw = np.asarray(inp["conv_w"], np.float32)
    for k in range(4):
        chp[:, :, :, k] = cw[:, k, :].reshape(L, 8, 128)
    chp[:, :, :, 4] = np.asarray(inp["conv_b"], np.float32).reshape(L, 8, 128)
    chp[:, :, :, 5] = np.asarray(inp["gate_a_b"], np.float32).reshape(L, 8, 128)
    chp[:, :, :, 6] = np.asarray(inp["gate_x_b"], np.float32).reshape(L, 8, 128)
    chp[:, :, :, 7] = np.asarray(inp["lru_lambda"], np.float32).reshape(L, 8, 128)
    out["chp"] = f(chp.transpose(0, 2, 1, 3).reshape(L, 128, 64))
    out["lambda_qk"] = f(np.asarray(inp["lambda_qk"], np.float32).reshape(L, 256))
    for k in ("subln_g", "ln1_g", "ln1_b", "ln2_g", "ln2_b"):
        out[k] = f(inp[k])
    augk = np.zeros((3, 128), np.float32)
    augk[0] = np.arange(128)
    augk[1] = 1.0
    augk[2] = 1.0
    augq = np.zeros((3, 8, 512), np.float32)
    qq = np.arange(512)
    for h in range(8):
        ch = 8.0 * 2.0 ** (-(h + 1))
        augq[0, h] = ch
        augq[1, h] = -ch * 128.0 * (qq // 128)
        augq[2, h] = -ch * (qq % 128)
    out["aug_k"] = augk
    out["aug_q"] = f(augq.reshape(3, 8 * 512))
    return out


_CACHE = {}


def kernel(**inputs):
    shared = prep_shared(inputs)
    x = np.asarray(inputs["x"], np.float32)
    nb = x.shape[0]
    if "nc" not in _CACHE:
        _CACHE["nc"] = build_program()[0]
    nc = _CACHE["nc"]
    in_maps = []
    for b in range(nb):
        m = dict(shared)
        m["x"] = np.ascontiguousarray(x[b])
        in_maps.append(m)
    res = run_bass_kernel_spmd(nc, in_maps, core_ids=list(range(nb)))
    return np.stack([np.asarray(r["y"], np.float32) for r in res.results], axis=0)
```
